# Optimizing a Trainium2 kernel written in Bass

```python
import math
import jax, jax.numpy as jnp
from jax import lax
import numpy as np

D_MODEL = 1024
BATCH = 2
SEQ = 8192
DEPTH = 2

GRID_W = 64
CTX_LEN = 256
N_MIXERS = 2
EXPAND = 2
E_HY = EXPAND * D_MODEL
FILTER_EMB = 33
FILTER_WIDTH = 64
DECAY_TARGET = 1e-2
FAST_DECAY_PCT = 0.3
SLOW_DECAY_PCT = 1.5
MAX_DECAY = math.log(DECAY_TARGET) / FAST_DECAY_PCT
MIN_DECAY = math.log(DECAY_TARGET) / SLOW_DECAY_PCT
HEAD_DIM = 128
N_HEADS = (EXPAND * D_MODEL) // HEAD_DIM
N_KV = N_HEADS // 4
GROUP = N_HEADS // N_KV
QD = N_HEADS * HEAD_DIM
KVD = N_KV * HEAD_DIM
ROPE_AXIS_DIM = HEAD_DIM // 2
ROPE_THETA = 10000.0
Q_BLOCK = 128
EPS = 1e-6

kernel_name = "hybrid_hyena_gqa_prefix_dit"


def rms_norm(x, g):
    xf = x.astype(jnp.float32)
    y = xf * lax.rsqrt(jnp.mean(xf * xf, axis=-1, keepdims=True) + EPS)
    return (y * g.astype(jnp.float32)).astype(x.dtype)


def short_conv(x, w, b):
    xp = jnp.pad(x, ((0, 0), (1, 1), (0, 0)))
    return xp[:, :-2] * w[0] + xp[:, 1:-1] * w[1] + xp[:, 2:] * w[2] + b


def hyena_filter(L, w1, b1, w2, b2, w3, b3, w4, freq):
    f32 = jnp.float32
    t = jnp.linspace(0.0, 1.0, L, dtype=f32)[:, None]
    bands = (FILTER_EMB - 1) // 2
    w = 2.0 * math.pi * jnp.arange(L, dtype=f32)[:, None] / L
    fr = jnp.linspace(1e-4, bands - 1, bands, dtype=f32)[None, :]
    z = jnp.concatenate([t, jnp.cos(fr * w), -jnp.sin(fr * w)], axis=-1)
    fq = freq.astype(f32)
    h = jnp.sin(fq * (z @ w1.astype(f32) + b1.astype(f32)))
    h = jnp.sin(fq * (h @ w2.astype(f32) + b2.astype(f32)))
    h = jnp.sin(fq * (h @ w3.astype(f32) + b3.astype(f32)))
    h = h @ w4.astype(f32)
    deltas = jnp.abs(jnp.linspace(MIN_DECAY, MAX_DECAY, E_HY, dtype=f32))
    decay = jnp.exp(-t * deltas[None, :])
    hf = h[:, :E_HY] * decay
    hb = h[:, E_HY:] * decay
    k = jnp.concatenate([hf, jnp.zeros((1, E_HY), f32), hb[:0:-1]], axis=0)
    return k / jnp.sum(jnp.abs(k), axis=0, keepdims=True)


def hyena_mixer(h, w_in, conv_w, conv_b, fw1, fb1, fw2, fb2, fw3, fb3, fw4, freq, d_skip, w_out):
    L = h.shape[1]
    proj = h @ w_in
    xv = short_conv(proj[..., :3 * E_HY], conv_w, conv_b)
    z = proj[..., 3 * E_HY:]
    x0, x1, v = jnp.split(xv, 3, axis=-1)
    u = x1 * v
    k = hyena_filter(L, fw1, fb1, fw2, fb2, fw3, fb3, fw4, freq)
    U = jnp.fft.rfft(u.astype(jnp.float32), n=2 * L, axis=1)
    K = jnp.fft.rfft(k, n=2 * L, axis=0)
    y = jnp.fft.irfft(U * K[None], n=2 * L, axis=1)[:, :L].astype(h.dtype)
    y = (y + u * d_skip) * x0
    return (y * jax.nn.silu(z)) @ w_out


def rope_half(x, ang):
    F = ang.shape[-1]
    shape = (ang.shape[0],) + (1,) * (x.ndim - 3) + (F,)
    cos = jnp.cos(ang).reshape(shape).astype(x.dtype)
    sin = jnp.sin(ang).reshape(shape).astype(x.dtype)
    x1, x2 = x[..., :F], x[..., F:]
    return jnp.concatenate([x1 * cos - x2 * sin, x1 * sin + x2 * cos], axis=-1)


def axial_rope(x, row_ang, col_ang):
    return jnp.concatenate([rope_half(x[..., :ROPE_AXIS_DIM], row_ang),
                            rope_half(x[..., ROPE_AXIS_DIM:], col_ang)], axis=-1)


def attend(q, k, v):
    s = jnp.einsum('bqkgd,bskd->bkgqs', q, k).astype(jnp.float32) * (HEAD_DIM ** -0.5)
    p = jax.nn.softmax(s, axis=-1).astype(v.dtype)
    return jnp.einsum('bkgqs,bskd->bqkgd', p, v)


def gqa_mixer(h_lat, h_ctx, w_in, q_g, k_g, w_out, want_ctx):
    B, L, _ = h_lat.shape
    C = h_ctx.shape[1]
    p = h_lat @ w_in
    q = rms_norm(p[..., :QD].reshape(B, L, N_KV, GROUP, HEAD_DIM), q_g)
    k = rms_norm(p[..., QD:QD + KVD].reshape(B, L, N_KV, HEAD_DIM), k_g)
    v = p[..., QD + KVD:QD + 2 * KVD].reshape(B, L, N_KV, HEAD_DIM)
    z = p[..., QD + 2 * KVD:]
    rows = L // GRID_W
    row = jnp.repeat(jnp.arange(rows, dtype=jnp.float32), GRID_W)
    col = jnp.tile(jnp.arange(GRID_W, dtype=jnp.float32), rows)
    inv = 1.0 / (ROPE_THETA ** (jnp.arange(0, ROPE_AXIS_DIM, 2, dtype=jnp.float32) / ROPE_AXIS_DIM))
    row_ang = row[:, None] * inv[None, :]
    col_ang = col[:, None] * inv[None, :]
    q = axial_rope(q, row_ang, col_ang)
    k = axial_rope(k, row_ang, col_ang)
    if want_ctx:
        pc = h_ctx @ w_in
    else:
        pc = h_ctx @ w_in[:, QD:QD + 2 * KVD]
        pc = jnp.pad(pc, ((0, 0), (0, 0), (QD, 0)))
    kc = rms_norm(pc[..., QD:QD + KVD].reshape(B, C, N_KV, HEAD_DIM), k_g)
    vc = pc[..., QD + KVD:QD + 2 * KVD].reshape(B, C, N_KV, HEAD_DIM)
    k_all = jnp.concatenate([k, kc], axis=1)
    v_all = jnp.concatenate([v, vc], axis=1)
    nb = L // Q_BLOCK
    qb = q.reshape(B, nb, Q_BLOCK, N_KV, GROUP, HEAD_DIM).transpose(1, 0, 2, 3, 4, 5)
    o = lax.map(lambda qq: attend(qq, k_all, v_all), qb)
    o = o.transpose(1, 0, 2, 3, 4, 5).reshape(B, L, QD)
    y_lat = (o * jax.nn.silu(z)) @ w_out
    if want_ctx:
        qc = rms_norm(pc[..., :QD].reshape(B, C, N_KV, GROUP, HEAD_DIM), q_g)
        oc = attend(qc, kc, vc).reshape(B, C, QD)
        y_ctx = (oc * jax.nn.silu(pc[..., QD + 2 * KVD:])) @ w_out
    else:
        y_ctx = None
    return y_lat, y_ctx


def setup_inputs(seed: int = 0) -> dict:
    key = jax.random.key(seed)
    ks = jax.random.split(key, 32)
    f32 = jnp.float32
    na = (DEPTH + 1) // 2
    nb = DEPTH // 2

    def nrm(k, shape, scale):
        return jax.random.normal(k, shape, f32) * scale

    return {
        "x": nrm(ks[0], (BATCH, SEQ, D_MODEL), 1.0),
        "c": nrm(ks[1], (BATCH, D_MODEL), 1.0),
        "ctx": nrm(ks[2], (BATCH, CTX_LEN, D_MODEL), 1.0),
        "c_ctx": nrm(ks[3], (D_MODEL,), 1.0),
        "norm_g": 1.0 + nrm(ks[4], (DEPTH, D_MODEL), 0.05),
        "ada_w": nrm(ks[5], (DEPTH, D_MODEL, 3 * D_MODEL), D_MODEL ** -0.5),
        "ada_b": nrm(ks[6], (DEPTH, 3 * D_MODEL), 0.02),
        "hy_w_in": nrm(ks[7], (na, D_MODEL, 4 * E_HY), D_MODEL ** -0.5),
        "hy_conv_w": nrm(ks[8], (na, 3, 3 * E_HY), 3 ** -0.5),
        "hy_conv_b": nrm(ks[9], (na, 3 * E_HY), 0.02),
        "hy_fw1": nrm(ks[10], (na, FILTER_EMB, FILTER_WIDTH), FILTER_EMB ** -0.5),
        "hy_fb1": nrm(ks[11], (na, FILTER_WIDTH), 0.02),
        "hy_fw2": nrm(ks[12], (na, FILTER_WIDTH, FILTER_WIDTH), FILTER_WIDTH ** -0.5),
        "hy_fb2": nrm(ks[13], (na, FILTER_WIDTH), 0.02),
        "hy_fw3": nrm(ks[14], (na, FILTER_WIDTH, FILTER_WIDTH), FILTER_WIDTH ** -0.5),
        "hy_fb3": nrm(ks[15], (na, FILTER_WIDTH), 0.02),
        "hy_fw4": nrm(ks[16], (na, FILTER_WIDTH, 2 * E_HY), FILTER_WIDTH ** -0.5),
        "hy_freq": 1.0 + nrm(ks[17], (na, FILTER_WIDTH), 0.05),
        "hy_d": nrm(ks[18], (na, E_HY), 0.5),
        "hy_w_out": nrm(ks[19], (na, E_HY, D_MODEL), E_HY ** -0.5),
        "at_w_in": nrm(ks[20], (nb, D_MODEL, 2 * QD + 2 * KVD), D_MODEL ** -0.5),
        "at_q_g": 1.0 + nrm(ks[21], (nb, HEAD_DIM), 0.05),
        "at_k_g": 1.0 + nrm(ks[22], (nb, HEAD_DIM), 0.05),
        "at_w_out": nrm(ks[23], (nb, QD, D_MODEL), QD ** -0.5),
    }


def reference(x, c, ctx, c_ctx, norm_g, ada_w, ada_b,
              hy_w_in, hy_conv_w, hy_conv_b, hy_fw1, hy_fb1, hy_fw2, hy_fb2, hy_fw3, hy_fb3,
              hy_fw4, hy_freq, hy_d, hy_w_out,
              at_w_in, at_q_g, at_k_g, at_w_out):
    sc_lat = jax.nn.silu(c)
    sc_ctx = jax.nn.silu(c_ctx)
    x_lat, x_ctx = x, ctx
    for i in range(DEPTH):
        last = i == DEPTH - 1
        mixer = i % N_MIXERS
        j = i // N_MIXERS
        need_ctx_in = (not last) or mixer == 1
        sh, sc, gt = jnp.split(sc_lat @ ada_w[i] + ada_b[i], 3, axis=-1)
        h_lat = rms_norm(x_lat, norm_g[i]) * (1.0 + sc[:, None]) + sh[:, None]
        if need_ctx_in:
            sh_c, sc_c, gt_c = jnp.split(sc_ctx @ ada_w[i] + ada_b[i], 3, axis=-1)
            h_ctx = rms_norm(x_ctx, norm_g[i]) * (1.0 + sc_c) + sh_c
        if mixer == 0:
            hp = (hy_w_in[j], hy_conv_w[j], hy_conv_b[j], hy_fw1[j], hy_fb1[j], hy_fw2[j], hy_fb2[j],
                  hy_fw3[j], hy_fb3[j], hy_fw4[j], hy_freq[j], hy_d[j], hy_w_out[j])
            y_lat = hyena_mixer(h_lat, *hp)
            y_ctx = None if last else hyena_mixer(h_ctx, *hp)
        else:
            y_lat, y_ctx = gqa_mixer(h_lat, h_ctx, at_w_in[j], at_q_g[j], at_k_g[j], at_w_out[j],
                                     not last)
        x_lat = x_lat + gt[:, None] * y_lat
        if not last:
            x_ctx = x_ctx + gt_c * y_ctx
    return x_lat
```

```python
import math
from contextlib import ExitStack

import numpy as np
import ml_dtypes
import concourse.bass as bass
import concourse.mybir as mybir
from concourse.bass_utils import run_bass_kernel_spmd

F32 = mybir.dt.float32
BF16 = mybir.dt.bfloat16
ALU = mybir.AluOpType
AF = mybir.ActivationFunctionType
AX = mybir.AxisListType

D = 1024
B = 2
L = 8192
CTX = 256
E = 2048
EPS = 1e-6
NCORES = 8


class Sem:
    def __init__(self, h):
        self.h = h
        self.cnt = 0


class Buf:
    __slots__ = ("name", "w", "rs")

    def __init__(self, name):
        self.name = name
        self.w = None
        self.rs = []


class KB:
    ENG = ("pe", "act", "dve", "pool", "sp")

    def __init__(self, nc, es, pre=""):
        self.nc = nc
        self.es = es
        self.pre = pre
        self.all_sems = []
        self.esem = {e: Sem(es.enter_context(nc.semaphore(pre + "e_" + e))) for e in self.ENG}
        self.all_sems.extend(self.esem.values())
        self.prog = {e: [] for e in self.ENG}
        self.seen = {e: {} for e in self.ENG}
        self.pending_noinc = {e: False for e in self.ENG}
        self.nsem = 0

    def sbuf(self, name, shape, dt):
        t = self.es.enter_context(self.nc.sbuf_tensor(self.pre + name, list(shape), dt))
        return t, Buf(name)

    def psum(self, name, shape, dt=F32):
        t = self.es.enter_context(self.nc.psum_tensor(self.pre + name, list(shape), dt))
        return t, Buf(name)

    def dsem(self, name=None):
        self.nsem += 1
        sm = Sem(self.es.enter_context(self.nc.semaphore(self.pre + (name or ("d%d" % self.nsem)))))
        self.all_sems.append(sm)
        return sm

    def dram(self, name, shape, dt, kind="Internal"):
        t = self.nc.dram_tensor(self.pre + name, list(shape), dt, kind=kind)
        return t.ap(), Buf(name)

    def barrier_all(self):
        for eng in self.ENG:
            waits = []
            seen = self.seen[eng]
            for sm in self.all_sems:
                if sm.cnt > 0 and seen.get(sm, 0) < sm.cnt:
                    seen[sm] = sm.cnt
                    waits.append((sm, sm.cnt))
            self.prog[eng].append((waits, None, None, 0))

    def _waits(self, eng, reads, writes):
        deps = []
        for b in reads:
            if b.w is not None:
                deps.append(b.w)
        for b in writes:
            if b.w is not None:
                deps.append(b.w)
            deps.extend(b.rs)
        need = {}
        for s, v in deps:
            if s is self.esem[eng] and eng == "pe":
                continue
            if need.get(s, 0) < v:
                need[s] = v
        out = []
        seen = self.seen[eng]
        for s, v in need.items():
            if seen.get(s, 0) >= v:
                continue
            seen[s] = v
            out.append((s, v))
        return out

    def op(self, eng, fn, reads=(), writes=(), inc=True):
        waits = self._waits(eng, reads, writes)
        s = self.esem[eng]
        if inc:
            s.cnt += 1
            val = s.cnt
        else:
            val = s.cnt + 1
        for b in writes:
            b.w = (s, val)
            b.rs = []
        for b in reads:
            if b not in writes:
                b.rs.append((s, val))
                if len(b.rs) > 64:
                    b.rs = _compact(b.rs)
        self.prog[eng].append((waits, fn, s if inc else None, 1))

    def dma(self, q, out, in_, dst, src, sem):
        waits = self._waits(q, [src], [dst])
        sem.cnt += 16
        val = sem.cnt
        dst.w = (sem, val)
        dst.rs = []
        src.rs.append((sem, val))
        if len(src.rs) > 64:
            src.rs = _compact(src.rs)
        self.prog[q].append((waits, lambda e: e.dma_start(out=out, in_=in_), sem, 16))

    def wait_all(self, eng, bufs):
        waits = self._waits(eng, list(bufs), [])
        self.prog[eng].append((waits, None, None, 0))

    def emit(self):
        nc = self.nc
        with nc.Block() as block:
            def run(eng_name):
                def body(e):
                    for waits, fn, sem, n in self.prog[eng_name]:
                        for s, v in waits:
                            e.wait_ge(s.h, v)
                        if fn is None:
                            continue
                        ins = fn(e)
                        if sem is not None:
                            ins.then_inc(sem.h, n)
                return body
            block.tensor(run("pe"))
            block.scalar(run("act"))
            block.vector(run("dve"))
            block.gpsimd(run("pool"))
            block.sync(run("sp"))


class Chunked:
    def __init__(self, aps, cw):
        self.aps = aps
        self.cw = cw

    def cols(self, c0, n):
        j, o = divmod(c0, self.cw)
        assert o + n <= self.cw
        return self.aps[j][:, o:o + n]

    def rows_cols(self, r0, r1, c0, n):
        j, o = divmod(c0, self.cw)
        assert o + n <= self.cw
        return self.aps[j][r0:r1, o:o + n]


def _compact(rs):
    best = {}
    for s, v in rs:
        if best.get(s, 0) < v:
            best[s] = v
    return list(best.items())


def bf16_np(a):
    return np.asarray(a).astype(ml_dtypes.bfloat16)


def emit_wout_bf(k, w_out, b_in, stage=None):
    wb, wb_b = k.sbuf("wb", [128, 16, D], BF16)
    wst = [stage if stage is not None else k.sbuf("wost0", [128, D], F32)] * 2
    wst_s = [k.dsem()] * 2
    w_v = w_out.rearrange("(k p) n -> p k n", p=128)
    for kc in range(16):
        t, tb = wst[kc % 2]
        k.dma("sp", t[:], w_v[:, kc, :], tb, b_in, wst_s[kc % 2])
        k.op("dve", lambda e, t=t, kc=kc: e.tensor_copy(out=wb[:, kc, :], in_=t[:]), [tb], [wb_b])
    return wb, wb_b


def emit_gt_rows(k, cvec, ada_w, ada_b, b_in, pg, scratch=None):
    cT, cT_b = k.sbuf("g_cT", [128, 2, 8], F32)
    cs = k.dsem()
    k.dma("sp", cT[:], cvec.rearrange("r (p k) -> p r k", k=8), cT_b, b_in, cs)
    sT, sT_b = k.sbuf("g_sT", [128, 2, 8], F32)
    k.op("act", lambda e: e.activation(out=sT[:], in_=cT[:], func=AF.Silu), [cT_b], [sT_b])
    if scratch is None:
        srep_t, srep_b = k.sbuf("g_srep", [128, 2, 8, 128], F32)
        srep = srep_t[:]
        aw = [k.sbuf("g_aw%d" % i, [128, 8, 128], F32) for i in range(2)]
        aw = [(t_[:], b_) for t_, b_ in aw]
        brow_t, brow_b = k.sbuf("g_brow", [128, D], F32)
        brow = brow_t[:]
    else:
        (srep, srep_b), aw0, aw1, (brow, brow_b) = scratch
        aw = [aw0, aw1]
    k.op("dve", lambda e: e.tensor_copy(out=srep, in_=sT[:].unsqueeze(3).to_broadcast([128, 2, 8, 128])),
         [sT_b], [srep_b])
    aws = [k.dsem() for _ in range(2)]
    a_v = ada_w.rearrange("(p k) n -> p k n", k=8)
    bs = k.dsem()
    k.dma("sp", brow, ada_b.rearrange("(o n) -> o n", o=1).partition_broadcast(128), brow_b, b_in, bs)
    gt, gt_b = k.sbuf("g_gt", [128, 2, D], F32)
    for cc in range(8):
        at, at_b = aw[cc % 2]
        k.dma("sp", at, a_v[:, :, cc * 128:(cc + 1) * 128], at_b, b_in, aws[cc % 2])
        for r in range(2):
            p, pb = pg[r * 2 + cc // 4]
            o = (cc % 4) * 128
            for kc in range(8):
                k.op("pe", lambda e, p=p, r=r, kc=kc, at=at, o=o: e.matmul(
                    p[:, o:o + 128], lhsT=srep[:, r, kc, :], rhs=at[:, kc, :],
                    start=(kc == 0), stop=(kc == 7)), [srep_b, at_b], [pb], inc=(kc == 7))
    for r in range(2):
        for nb in range(2):
            p, pb = pg[r * 2 + nb]
            k.op("dve", lambda e, p=p, r=r, nb=nb: e.tensor_tensor(
                out=gt[:, r, nb * 512:(nb + 1) * 512], in0=p[:], in1=brow[:, nb * 512:(nb + 1) * 512],
                op=ALU.add), [pb, brow_b], [gt_b])
    return gt, gt_b


def build_op(n_lat_tiles, n_ctx_tok, nc=None, T=None, pre=""):
    fused = nc is not None
    if not fused:
        nc = bass.Bass("TRN2", target_bir_lowering=False)
    ntok = n_lat_tiles * 128 + n_ctx_tok
    dt = nc.dram_tensor

    def inp_(name, shape, dty):
        if fused:
            return T[name]
        return dt(name, shape, dty, kind="ExternalInput").ap()
    ygT = inp_("ygT", [E, ntok], BF16)
    if not fused:
        ygT = Chunked([ygT], ntok)
    xres = inp_("xres", [ntok, D], F32)
    w_out = inp_("w_out", [E, D], F32)
    cvec = inp_("cvec", [2, D], F32)
    ada_w = inp_("ada_w_gt", [D, D], F32)
    ada_b = inp_("ada_b_gt", [D], F32)
    xout = T["xout"] if fused else dt("xout", [ntok, D], F32, kind="ExternalOutput").ap()
    b_in = Buf("in")
    b_out = Buf("xout")
    with ExitStack() as es:
        k = KB(nc, es, pre)
        pg = [k.psum("pg%d" % i, [128, 512]) for i in range(4)]
        wb, wb_b = emit_wout_bf(k, w_out, b_in)
        gt, gt_b = emit_gt_rows(k, cvec, ada_w, ada_b, b_in, pg)
        tiles = [(128, 0)] * n_lat_tiles
        if n_ctx_tok:
            tiles.append((n_ctx_tok, 1))
        NB = 2
        yb = [k.sbuf("yb%d" % i, [128, 16, 128], BF16) for i in range(NB)]
        yb_s = [k.dsem() for _ in range(NB)]
        xb = [k.sbuf("xb%d" % i, [128, D], F32) for i in range(NB)]
        xb_s = [k.dsem() for _ in range(NB)]
        ob = [k.sbuf("ob%d" % i, [128, D], F32) for i in range(NB)]
        ob_s = [k.dsem() for _ in range(NB)]
        po = [k.psum("po%d" % i, [128, 512]) for i in range(4)]
        t0 = 0
        outs = []
        for ti, (m, which) in enumerate(tiles):
            s = ti % NB
            yt, yt_b = yb[s]
            xt, xt_b = xb[s]
            ot, ot_b = ob[s]
            k.dma("sp", yt[:, :, :m], ygT.cols(t0, m).rearrange("(k p) t -> p k t", p=128), yt_b, b_in, yb_s[s])
            k.dma("sp", xt[:m, :], xres[t0:t0 + m, :], xt_b, b_in, xb_s[s])
            for nb in range(2):
                p, pb = po[(ti % 2) * 2 + nb]
                for kc in range(16):
                    k.op("pe", lambda e, p=p, yt=yt, kc=kc, nb=nb, m=m: e.matmul(
                        p[:m, :], lhsT=yt[:, kc, :m], rhs=wb[:, kc, nb * 512:(nb + 1) * 512],
                        start=(kc == 0), stop=(kc == 15)), [yt_b, wb_b], [pb], inc=(kc == 15))
                sl = slice(nb * 512, (nb + 1) * 512)
                k.op("dve", lambda e, p=p, ot=ot, sl=sl, m=m, which=which: e.tensor_tensor(
                    out=ot[:m, sl], in0=p[:m, :], in1=gt[:m, which, sl], op=ALU.mult), [pb, gt_b], [ot_b])
                k.op("dve", lambda e, ot=ot, xt=xt, sl=sl, m=m: e.tensor_tensor(
                    out=ot[:m, sl], in0=ot[:m, sl], in1=xt[:m, sl], op=ALU.add), [ot_b, xt_b], [ot_b])
            ob_ = Buf("xout%d" % ti)
            outs.append(ob_)
            k.dma("act", xout[t0:t0 + m, :], ot[:m, :], ob_, ot_b, ob_s[s])
            t0 += m
        k.wait_all("act", outs)
        k.barrier_all()
        k.emit()
    return nc


def run_op(ygT_full, xres_list, w_out, c, c_ctx, ada_w_gt, ada_b_gt, with_ctx, ygT_ctx=None, xctx=None):
    n_ctx = 64 if with_ctx else 0
    nc = build_op(16, n_ctx)
    in_maps = []
    for core in range(NCORES):
        b, q = divmod(core, 4)
        yg = ygT_full[b][:, q * 2048:(q + 1) * 2048]
        xr = xres_list[b, q * 2048:(q + 1) * 2048]
        if with_ctx:
            yg = np.concatenate([yg, ygT_ctx[b][:, q * 64:(q + 1) * 64]], axis=1)
            xr = np.concatenate([xr, xctx[b, q * 64:(q + 1) * 64]], axis=0)
        in_maps.append({
            "ygT": np.ascontiguousarray(yg), "xres": np.ascontiguousarray(xr),
            "w_out": w_out, "cvec": np.ascontiguousarray(np.stack([c[b], c_ctx])),
            "ada_w_gt": ada_w_gt, "ada_b_gt": ada_b_gt})
    res = run_bass_kernel_spmd(nc, in_maps, core_ids=list(range(NCORES)))
    xo = np.empty((B, L, D), np.float32)
    xc = np.empty((B, CTX, D), np.float32) if with_ctx else None
    for core in range(NCORES):
        b, q = divmod(core, 4)
        r = res.results[core]["xout"]
        xo[b, q * 2048:(q + 1) * 2048] = r[:2048]
        if with_ctx:
            xc[b, q * 64:(q + 1) * 64] = r[2048:]
    return xo, xc


def row_to_cols(k, pst, pb, col0, row_t, row_b, n, one_t, one_b, off=0):
    for j in range(n):
        k.op("pe", lambda e, j=j: e.matmul(pst[:, col0 + j:col0 + j + 1],
                                            lhsT=row_t[0:1, off + j * 128:off + (j + 1) * 128],
                                            rhs=one_t[0:1, 0:1], start=True, stop=True),
             [row_b, one_b], [pb])


def vec_to_cols(k, vec_ap, n, m, identf, identf_b, pst, pb, col0, b_in, name):
    vt, vt_b = k.sbuf(name, [n, m], F32)
    s = k.dsem()
    k.dma("sp", vt[:], vec_ap.rearrange("(j p) -> j p", p=m), vt_b, b_in, s)
    k.op("pe", lambda e: e.matmul(pst[0:m, col0:col0 + n], lhsT=vt[0:n, 0:m], rhs=identf[0:n, 0:n], start=True, stop=True),
         [vt_b, identf_b], [pb])


def emit_mod(k, cvec, ada_w, ada_b, norm_g, identf, identf_b, b_in, pst, pb, pst2, pb2):
    cT, cT_b = k.sbuf("m_cT", [128, 2, 8], F32)
    s0 = k.dsem()
    k.dma("sp", cT[:], cvec.rearrange("r (p k) -> p r k", k=8), cT_b, b_in, s0)
    sT, sT_b = k.sbuf("m_sT", [128, 8, 2], F32)
    k.op("act", lambda e: e.activation(out=sT[:].rearrange("p k r -> p r k"), in_=cT[:], func=AF.Silu), [cT_b], [sT_b])
    awc = [k.sbuf("m_aw0", [128, 8, 128], F32)] * 2
    aws = [k.dsem()] * 2
    a_v = ada_w.rearrange("(p k) n -> p k n", k=8)
    for j in range(16):
        t, tb = awc[j % 2]
        k.dma("sp", t[:], a_v[:, :, j * 128:(j + 1) * 128], tb, b_in, aws[j % 2])
        for kc in range(8):
            k.op("pe", lambda e, t=t, kc=kc, j=j: e.matmul(pst[:, 2 * j:2 * j + 2], lhsT=t[:, kc, :], rhs=sT[:, kc, :],
                                                             start=(kc == 0), stop=(kc == 7)), [tb, sT_b], [pb], inc=(kc == 7))
    vec_to_cols(k, ada_b, 16, 128, identf, identf_b, pst2, pb2, 0, b_in, "m_bv")
    bg, bg_b = k.sbuf("m_bg", [128, 24], F32)
    k.op("dve", lambda e: e.tensor_copy(out=bg[:, 0:16], in_=pst2[:, 0:16]), [pb2], [bg_b])
    vec_to_cols(k, norm_g, 8, 128, identf, identf_b, pst2, pb2, 16, b_in, "m_gv")
    k.op("dve", lambda e: e.tensor_copy(out=bg[:, 16:24], in_=pst2[:, 16:24]), [pb2], [bg_b])
    mod, mod_b = k.sbuf("m_mod", [128, 16, 2], F32)
    k.op("dve", lambda e: e.tensor_tensor(out=mod[:], in0=pst[:, 0:32].rearrange("p (j r) -> p j r", r=2),
                                           in1=bg[:, 0:16].unsqueeze(2).to_broadcast([128, 16, 2]), op=ALU.add), [pb, bg_b], [mod_b])
    gm, gm_b = k.sbuf("m_gm", [128, 8, 2], F32)
    sh, sh_b = k.sbuf("m_sh", [128, 8, 2], F32)
    k.op("dve", lambda e: e.tensor_copy(out=sh[:], in_=mod[:, 0:8, :]), [mod_b], [sh_b])
    k.op("dve", lambda e: e.scalar_tensor_tensor(
        out=gm[:], in0=mod[:, 8:16, :], scalar=1.0, in1=bg[:, 16:24].unsqueeze(2).to_broadcast([128, 8, 2]),
        op0=ALU.add, op1=ALU.mult), [mod_b, bg_b], [gm_b])
    return (gm, gm_b), (sh, sh_b)


class HT:
    def __init__(self, k, ident_t, ident_b, gm, sh, alloc_x=True):
        self.k = k
        self.ident = (ident_t, ident_b)
        self.gm, self.sh = gm, sh
        self.xt = [k.sbuf("h_xt0", [128, D], F32)] * 2 if alloc_x else None
        self.xs = [k.dsem()] * 2 if alloc_x else None
        self.ss = [k.sbuf("h_ss%d" % i, [128, 2], F32) for i in range(2)]
        self.xh = [k.sbuf("h_xh%d" % i, [128, D], BF16) for i in range(2)]
        self.tmp = [k.sbuf("h_tmp0", [128, 8, 128], F32)] * 2
        self.n = 0

    def tile(self, x_ap, b_in, r, pst_bf, pb, dst_ap, dst_b, src=None):
        k = self.k
        i = self.n % 2
        self.n += 1
        xt, xt_b = self.xt[i] if src is None else src
        ss, ss_b = self.ss[i]
        xh, xh_b = self.xh[i]
        tmp, tmp_b = self.tmp[i]
        junk, junk_b = xh, xh_b
        (gm, gm_b), (sh, sh_b) = self.gm, self.sh
        if src is None:
            k.dma("sp", xt[:], x_ap, xt_b, b_in, self.xs[i])
        k.op("act", lambda e: e.activation(out=junk[:], in_=xt[:], func=AF.Square, accum_out=ss[:, 0:1]),
             [xt_b], [junk_b, ss_b])
        k.op("act", lambda e: e.activation(out=ss[:, 1:2], in_=ss[:, 0:1], func=AF.Sqrt, scale=1.0 / D, bias=EPS),
             [ss_b], [ss_b])
        k.op("dve", lambda e: e.reciprocal(out=ss[:, 1:2], in_=ss[:, 1:2]), [ss_b], [ss_b])
        k.op("act", lambda e: e.activation(out=xh[:], in_=xt[:], func=AF.Identity, scale=ss[:, 1:2]), [xt_b, ss_b], [xh_b])
        for kc in range(8):
            k.op("pe", lambda e, kc=kc: e.transpose(pst_bf[:, kc, :], xh[:, kc * 128:(kc + 1) * 128], self.ident[0][:]),
                 [xh_b, self.ident[1]], [pb], inc=(kc == 7))
        k.op("dve", lambda e: e.tensor_tensor(out=tmp[:], in0=pst_bf, in1=gm[:, :, r:r + 1].to_broadcast([128, 8, 128]),
                                               op=ALU.mult), [pb, gm_b], [tmp_b])
        k.op("dve", lambda e: e.tensor_tensor(out=dst_ap, in0=tmp[:], in1=sh[:, :, r:r + 1].to_broadcast([128, 8, 128]),
                                               op=ALU.add), [tmp_b, sh_b], [dst_b])


NKT = (L + CTX) // 128


class _Stop(Exception):
    pass


def build_att(stop=0, nc=None, T=None, pre=""):
    fused = nc is not None
    if not fused:
        nc = bass.Bass("TRN2", target_bir_lowering=False)
    dt = nc.dram_tensor

    def inp_(name, shape, dty):
        if fused:
            return T[name]
        return dt(name, shape, dty, kind="ExternalInput").ap()
    x1 = inp_("x1" if not fused else "x", [L, D], F32)
    xc = inp_("xc", [CTX, D], F32)
    cvec = inp_("cvec", [2, D], F32)
    ada_w = inp_("ada_w", [D, 2048], F32)
    ada_b = inp_("ada_b", [2048], F32)
    norm_g = inp_("norm_g", [D], F32)
    w = inp_("w", [D, 1280], F32)
    qkg = inp_("qkg", [256], F32)
    cosT = inp_("cosT", [128, L], F32)
    sinT = inp_("sinT", [128, L], F32)
    consts = inp_("consts", [128, 4, 128], BF16)
    identf_d = inp_("identf", [64, 64], F32)
    if fused:
        ogT = T["og_loc"]
    else:
        ogT = Chunked([dt("ogT", [512, L], BF16, kind="ExternalOutput").ap()], L)
    dbgf = dt("dbgf", [128, 2048], F32, kind="ExternalOutput").ap() if stop else None
    dbgh = dt("dbgh", [128, 8192], BF16, kind="ExternalOutput").ap() if stop else None
    b_in = Buf("in")
    with ExitStack() as es:
        k = KB(nc, es, pre)
        dbg_outs = []
        dsm = k.dsem()

        def dump(ap_out, ap_in, src_b):
            ob = Buf("dbg")
            dbg_outs.append(ob)
            k.dma("sp", ap_out, ap_in, ob, src_b, dsm)

        def finish():
            k.wait_all("sp", dbg_outs)
            k.barrier_all()
            k.emit()
        q_scr, _ = k.dram("q_scr", [512, L], BF16)
        g_scr, _ = k.dram("g_scr", [512, L], BF16)
        q_rb = {(cc, bl): Buf("qs") for cc in range(4) for bl in range(16)}
        g_rb = {(cc, bl): Buf("gs") for cc in range(4) for bl in range(16)}
        P = [k.psum("P%d" % i, [128, 512]) for i in range(8)]
        cst, cst_b = k.sbuf("cst", [128, 4, 128], BF16)
        s_c = k.dsem()
        k.dma("sp", cst[:], consts, cst_b, b_in, s_c)
        ident, permT, onesm, ones1 = (cst[:, i, :] for i in range(4))
        identf, identf_b = k.sbuf("identf_sb", [64, 64], F32)
        k.dma("sp", identf[:], identf_d, identf_b, b_in, s_c)
        gm, sh = emit_mod(k, cvec, ada_w, ada_b, norm_g, identf, identf_b, b_in, P[0][0], P[0][1], P[1][0], P[1][1])
        vec_to_cols(k, qkg, 2, 128, identf, identf_b, P[2][0], P[2][1], 0, b_in, "qkv")
        gcol, gcol_b = k.sbuf("gcol", [128, 2], F32)
        k.op("dve", lambda e: e.tensor_copy(out=gcol[:], in_=P[2][0][:, 0:2]), [P[2][1]], [gcol_b])
        if stop == 1:
            dump(dbgf[:, 0:16], gm[0][:].rearrange("p k r -> p (k r)"), gm[1])
            dump(dbgf[:, 16:32], sh[0][:].rearrange("p k r -> p (k r)"), sh[1])
            dump(dbgf[:, 32:34], gcol[:], gcol_b)
            finish()
            return nc
        hTb = [k.sbuf("hTb%d" % i, [128, 8, 512], BF16) for i in range(2)]
        wst = [k.sbuf("wst0", [128, 1280], F32)] * 2
        if fused:
            w0b, w0b_b = emit_wout_bf(k, T["w_out0"], b_in, stage=(wst[0][0][:, 0:D], wst[0][1]))
            h0f = hTb[0][0][:].rearrange("p a b -> p (a b)").bitcast(F32)
            h1f = hTb[1][0][:].rearrange("p a b -> p (a b)").bitcast(F32)
            scratch = ((h0f.rearrange("p (r k m) -> p r k m", r=2, k=8), hTb[0][1]),
                       (h1f[:, 0:1024].rearrange("p (k m) -> p k m", k=8), hTb[1][1]),
                       (h1f[:, 1024:2048].rearrange("p (k m) -> p k m", k=8), hTb[1][1]),
                       (wst[0][0][:, 0:D], wst[0][1]))
            gt0, gt0_b = emit_gt_rows(k, T["cvec0"], T["ada_w_gt0"], T["ada_b_gt0"], b_in, [P[2], P[3], P[4], P[5]], scratch=scratch)
            ygb, ygb_b = k.sbuf("ygb", [128, 16, 512], BF16)
            ygs = k.dsem()
            xin = [k.sbuf("xin0", [128, D], F32)] * 2
            xins = [k.dsem()] * 2
            x1t = [k.sbuf("x1t%d" % i, [128, D], F32) for i in range(2)]
            x1s = [k.dsem() for _ in range(2)]
            yga = T["yg_all"]
            x1_tiles = {}
        wbf, wbf_b = k.sbuf("wbf", [128, 8, 1280], BF16)
        wss = [k.dsem()] * 2
        w_v = w.rearrange("(k p) n -> p k n", p=128)
        for kc in range(8):
            t, tb = wst[kc % 2]
            k.dma("sp", t[:], w_v[:, kc, :], tb, b_in, wss[kc % 2])
            k.op("dve", lambda e, t=t, kc=kc: e.tensor_copy(out=wbf[:, kc, :], in_=t[:]), [tb], [wbf_b])
        KT, KT_b = k.sbuf("KT", [128, L + CTX], BF16)
        V, V_b = k.sbuf("V", [128, NKT, 128], BF16)
        ht = HT(k, ident, cst_b, gm, sh, alloc_x=not fused)
        ntile = 0
        sqb = [k.sbuf("sqb%d" % i, [128, 512], BF16) for i in range(2)]
        qgb = [k.sbuf("qgb%d" % i, [128, 512], BF16) for i in range(2)]
        rst = [k.sbuf("rst%d" % i, [128, 512], F32) for i in range(2)]
        t1 = [k.sbuf("t1_%d" % i, [128, 512], F32) for i in range(2)]
        t2 = [k.sbuf("t2_%d" % i, [128, 512], F32) for i in range(2)]
        qo = [k.sbuf("qo%d" % i, [128, 512], BF16) for i in range(2)]
        qos = [k.dsem() for _ in range(2)]
        cs = [k.sbuf("cs0", [128, 2, 512], F32)] * 2
        css = [k.dsem()] * 2
        gto = [k.sbuf("gto%d" % i, [128, 512], BF16) for i in range(2)]
        gtos = [k.dsem() for _ in range(2)]
        nq = 0
        ngt = 0
        for blk in range(17):
            lat = blk < 16
            ntok = 512 if lat else 256
            tok0 = blk * 512
            hb, hb_b = hTb[blk % 2]
            if fused:
                yg_src = yga.cols(tok0, ntok) if lat else yga.cols(L, CTX)
                k.dma("sp", ygb[:, :, 0:ntok], yg_src.rearrange("(k p) t -> p k t", p=128), ygb_b, b_in, ygs)
            for t in range(ntok // 128):
                xa = x1[tok0 + t * 128: tok0 + (t + 1) * 128, :] if lat else xc[t * 128:(t + 1) * 128, :]
                pt, pb = P[t % 2]
                src = None
                if fused:
                    j = ntile % 2
                    ntile += 1
                    xi, xi_b = xin[j]
                    xo_, xo_b = x1t[j]
                    k.dma("sp", xi[:], xa, xi_b, b_in, xins[j])
                    for nb in range(2):
                        pp, ppb = P[6 + nb]
                        for kc in range(16):
                            k.op("pe", lambda e, pp=pp, kc=kc, nb=nb, t=t: e.matmul(
                                pp[:], lhsT=ygb[:, kc, t * 128:(t + 1) * 128], rhs=w0b[:, kc, nb * 512:(nb + 1) * 512],
                                start=(kc == 0), stop=(kc == 15)), [ygb_b, w0b_b], [ppb], inc=(kc == 15))
                        sl = slice(nb * 512, (nb + 1) * 512)
                        r_ = 0 if lat else 1
                        k.op("dve", lambda e, pp=pp, xo_=xo_, sl=sl, r_=r_: e.tensor_tensor(
                            out=xo_[:, sl], in0=pp[:], in1=gt0[:, r_, sl], op=ALU.mult), [ppb, gt0_b], [xo_b])
                        k.op("dve", lambda e, xo_=xo_, xi=xi, sl=sl: e.tensor_tensor(
                            out=xo_[:, sl], in0=xo_[:, sl], in1=xi[:, sl], op=ALU.add), [xo_b, xi_b], [xo_b])
                    if lat:
                        xb_ = Buf("x1f")
                        x1_tiles[tok0 // 128 + t] = xb_
                        k.dma("act", T["x1_full"][tok0 + t * 128: tok0 + (t + 1) * 128, :], xo_[:], xb_, xo_b, x1s[j])
                    src = (xo_, xo_b)
                ht.tile(xa, b_in, 0 if lat else 1, pt[:].bitcast(BF16)[:, 0:1024].rearrange("p (k t) -> p k t", k=8), pb,
                        hb[:, :, t * 128:(t + 1) * 128], hb_b, src=src)
            if stop == 2:
                dump(dbgh[:, 0:4096], hb[:].rearrange("p k t -> p (k t)"), hb_b)
                finish()
                return nc
            if lat:
                ct, ct_b = cs[blk % 2]
                k.dma("sp", ct[:, 0, :], cosT[:, tok0:tok0 + 512], ct_b, b_in, css[blk % 2])
                k.dma("sp", ct[:, 1, :], sinT[:, tok0:tok0 + 512], ct_b, b_in, css[blk % 2])
            for cc in ([0, 1, 2, 3, 4] if lat else [4]):
                i = nq % 2
                nq += 1
                ps, psb = P[2 + i]
                for kc in range(8):
                    k.op("pe", lambda e, ps=ps, kc=kc, cc=cc, hb=hb, ntok=ntok: e.matmul(
                        ps[:, :ntok], lhsT=wbf[:, kc, cc * 128:(cc + 1) * 128], rhs=hb[:, kc, :ntok],
                        start=(kc == 0), stop=(kc == 7)), [wbf_b, hb_b], [psb], inc=(kc == 7))
                sq, sq_b = sqb[i]
                qg_, qg_b = qgb[i]
                r_, r_b = rst[i]
                gi = 0 if cc < 4 else 1
                k.op("act", lambda e, sq=sq, ps=ps, ntok=ntok: e.activation(out=sq[:, :ntok], in_=ps[:, :ntok], func=AF.Square),
                     [psb], [sq_b])
                k.op("act", lambda e, qg_=qg_, ps=ps, ntok=ntok, gi=gi: e.activation(
                    out=qg_[:, :ntok], in_=ps[:, :ntok], func=AF.Identity, scale=gcol[:, gi:gi + 1]), [psb, gcol_b], [qg_b])
                pm, pmb = P[4 + i]
                k.op("pe", lambda e, pm=pm, sq=sq, ntok=ntok: e.matmul(pm[:, :ntok], lhsT=onesm, rhs=sq[:, :ntok], start=True, stop=True),
                     [cst_b, sq_b], [pmb])
                k.op("act", lambda e, r_=r_, pm=pm, ntok=ntok: e.activation(out=r_[:, :ntok], in_=pm[:, :ntok], func=AF.Sqrt, bias=EPS),
                     [pmb], [r_b])
                k.op("dve", lambda e, r_=r_, ntok=ntok: e.reciprocal(out=r_[:, :ntok], in_=r_[:, :ntok]), [r_b], [r_b])
                if lat:
                    pr, prb = P[6 + i]
                    k.op("pe", lambda e, pr=pr, qg_=qg_: e.matmul(pr[:], lhsT=permT, rhs=qg_[:], start=True, stop=True),
                         [cst_b, qg_b], [prb])
                    a1, a1_b = t1[i]
                    a2, a2_b = t2[i]
                    k.op("dve", lambda e, a1=a1, qg_=qg_, ct=ct: e.tensor_tensor(out=a1[:], in0=qg_[:], in1=ct[:, 0, :], op=ALU.mult),
                         [qg_b, ct_b], [a1_b])
                    k.op("dve", lambda e, a2=a2, pr=pr, ct=ct: e.tensor_tensor(out=a2[:], in0=pr[:], in1=ct[:, 1, :], op=ALU.mult),
                         [prb, ct_b], [a2_b])
                    k.op("dve", lambda e, a1=a1, a2=a2: e.tensor_tensor(out=a1[:], in0=a1[:], in1=a2[:], op=ALU.add),
                         [a1_b, a2_b], [a1_b])
                    if cc < 4:
                        o_, o_b = qo[i]
                        k.op("dve", lambda e, o_=o_, a1=a1, r_=r_: e.tensor_tensor(out=o_[:], in0=a1[:], in1=r_[:], op=ALU.mult),
                             [a1_b, r_b], [o_b])
                        k.dma("act", q_scr[cc * 128:(cc + 1) * 128, tok0:tok0 + 512], o_[:], q_rb[(cc, blk)], o_b, qos[i])
                    else:
                        k.op("dve", lambda e, a1=a1, r_=r_, tok0=tok0: e.tensor_tensor(
                            out=KT[:, tok0:tok0 + 512], in0=a1[:], in1=r_[:], op=ALU.mult), [a1_b, r_b], [KT_b])
                else:
                    k.op("dve", lambda e, qg_=qg_, r_=r_: e.tensor_tensor(
                        out=KT[:, L:L + 256], in0=qg_[:, :256], in1=r_[:, :256], op=ALU.mult), [qg_b, r_b], [KT_b])
            if lat:
                for cc in range(4):
                    i = ngt % 2
                    ngt += 1
                    ps, psb = P[2 + i]
                    for kc in range(8):
                        k.op("pe", lambda e, ps=ps, kc=kc, cc=cc, hb=hb: e.matmul(
                            ps[:], lhsT=wbf[:, kc, 768 + cc * 128:768 + (cc + 1) * 128], rhs=hb[:, kc, :],
                            start=(kc == 0), stop=(kc == 7)), [wbf_b, hb_b], [psb], inc=(kc == 7))
                    g_, g_b = gto[i]
                    k.op("act", lambda e, g_=g_, ps=ps: e.activation(out=g_[:], in_=ps[:], func=AF.Silu), [psb], [g_b])
                    k.dma("act", g_scr[cc * 128:(cc + 1) * 128, tok0:tok0 + 512], g_[:], g_rb[(cc, blk)], g_b, gtos[i])
            for t in range(ntok // 128):
                ps, psb = P[6 + t % 2]
                for kc in range(8):
                    k.op("pe", lambda e, ps=ps, kc=kc, hb=hb, t=t: e.matmul(
                        ps[:, 0:128], lhsT=hb[:, kc, t * 128:(t + 1) * 128], rhs=wbf[:, kc, 640:768],
                        start=(kc == 0), stop=(kc == 7)), [wbf_b, hb_b], [psb], inc=(kc == 7))
                kt = (tok0 // 128 + t) if lat else (L // 128 + t)
                k.op("dve", lambda e, ps=ps, kt=kt: e.tensor_copy(out=V[:, kt, :], in_=ps[:, 0:128]), [psb], [V_b])
            if stop == 3 or (stop == 4 and blk == 16):
                dump(dbgh[:, 0:512], KT[:, 0:512], KT_b)
                dump(dbgh[:, 512:1024], V[:, 0:4, :].rearrange("p a b -> p (a b)"), V_b)
                dump(dbgh[:, 1024:1280], KT[:, L:L + 256], KT_b)
                dump(dbgh[:, 1280:1536], V[:, 64:66, :].rearrange("p a b -> p (a b)"), V_b)
                dump(dbgh[:, 2048:2560], q_scr[0:128, 0:512], q_rb[(0, 0)])
                dump(dbgh[:, 2560:3072], g_scr[0:128, 0:512], g_rb[(0, 0)])
                finish()
                return nc
        qbk = [k.sbuf("qbk%d" % i, [128, 512], BF16) for i in range(2)]
        qbs = [k.dsem() for _ in range(2)]
        gbk = [k.sbuf("gbk%d" % i, [128, 512], BF16) for i in range(2)]
        gbs = [k.dsem() for _ in range(2)]
        pT = [k.sbuf("pT%d" % i, [128, 512], BF16) for i in range(3)] + [sqb[0], sqb[1], qgb[0]]
        SB = [P[0], P[1], P[2], P[7]]
        rd = [k.sbuf("rd%d" % i, [128, 512], F32) for i in range(2)]
        o1 = [k.sbuf("o1_%d" % i, [128, 512], F32) for i in range(2)]
        o2 = [k.sbuf("o2_%d" % i, [128, 512], BF16) for i in range(2)]
        o2s = [k.dsem() for _ in range(2)]
        outs = []
        scale = 1.0 / math.sqrt(128.0)
        iters = [(h, qb) for h in range(4) for qb in range(16)]
        if stop >= 10:
            iters = iters[:stop - 10]

        def load_qg(j):
            h_, qb_ = iters[j]
            q__, q__b = qbk[j % 2]
            gk_, gk__b = gbk[j % 2]
            k.dma("sp", q__[:], q_scr[h_ * 128:(h_ + 1) * 128, qb_ * 512:(qb_ + 1) * 512], q__b, q_rb[(h_, qb_)], qbs[j % 2])
            k.dma("sp", gk_[:], g_scr[h_ * 128:(h_ + 1) * 128, qb_ * 512:(qb_ + 1) * 512], gk__b, g_rb[(h_, qb_)], gbs[j % 2])
        load_qg(0)
        if True:
            for it, (h, qb) in enumerate(iters):
                i = it % 2
                q_, q_b = qbk[i]
                gk, gk_b = gbk[i]
                if it + 1 < len(iters):
                    load_qg(it + 1)
                O, O_b = P[3 + i]
                Dn, Dn_b = P[5 + i]
                ac, ac_b = t1[i]

                def emit_s(kt):
                    S, S_b = SB[kt % 4]
                    k.op("pe", lambda e, S=S, kt=kt, q_=q_: e.matmul(S[:], lhsT=KT[:, kt * 128:(kt + 1) * 128], rhs=q_[:],
                                                                      start=True, stop=True), [KT_b, q_b], [S_b])
                emit_s(0)
                emit_s(1)
                for kt in range(NKT):
                    if kt + 2 < NKT:
                        emit_s(kt + 2)
                    S, S_b = SB[kt % 4]
                    p_, p_b = pT[kt % 6]
                    k.op("act", lambda e, p_=p_, S=S: e.activation(out=p_[:], in_=S[:], func=AF.Exp, scale=scale), [S_b], [p_b])
                    on_pe = (kt % 6 == 0)
                    k.op("pe", lambda e, p_=p_, kt=kt, O=O: e.matmul(O[:], lhsT=V[:, kt, :], rhs=p_[:], start=(kt == 0),
                                                                      stop=(kt == NKT - 1)), [V_b, p_b], [O_b], inc=not on_pe)
                    if on_pe:
                        k.op("pe", lambda e, p_=p_, kt=kt, Dn=Dn: e.matmul(Dn[:], lhsT=ones1, rhs=p_[:], start=(kt == 0),
                                                                            stop=False), [cst_b, p_b], [Dn_b])
                    elif kt == 1:
                        k.op("dve", lambda e, p_=p_, ac=ac: e.tensor_copy(out=ac[:], in_=p_[:]), [p_b], [ac_b])
                    else:
                        k.op("dve", lambda e, p_=p_, ac=ac: e.tensor_tensor(out=ac[:], in0=ac[:], in1=p_[:], op=ALU.add),
                             [ac_b, p_b], [ac_b])
                acb_, acb_b = qo[i]
                k.op("dve", lambda e, ac=ac, acb_=acb_: e.tensor_copy(out=acb_[:], in_=ac[:]), [ac_b], [acb_b])
                k.op("pe", lambda e, acb_=acb_, Dn=Dn: e.matmul(Dn[:], lhsT=ones1, rhs=acb_[:], start=False, stop=True),
                     [cst_b, acb_b], [Dn_b])
                r_, r_b = rd[i]
                a_, a_b = o1[i]
                b_, b_b = o2[i]
                k.op("dve", lambda e, r_=r_, Dn=Dn: e.reciprocal(out=r_[:], in_=Dn[:]), [Dn_b], [r_b])
                k.op("dve", lambda e, a_=a_, r_=r_, O=O: e.tensor_tensor(out=a_[:], in0=O[:], in1=r_[:], op=ALU.mult), [O_b, r_b], [a_b])
                k.op("dve", lambda e, a_=a_, b_=b_, gk=gk: e.tensor_tensor(out=b_[:], in0=a_[:], in1=gk[:], op=ALU.mult),
                     [a_b, gk_b], [b_b])
                ob = Buf("o%d" % it)
                outs.append(ob)
                o_ap = ogT.rows_cols(h * 128, (h + 1) * 128, qb * 512, 512)
                k.dma("sp", o_ap, b_[:], ob, b_b, o2s[i])
        k.wait_all("sp", outs)
        if fused:
            k.wait_all("sp", list(x1_tiles.values()))
        if stop:
            finish()
        else:
            k.barrier_all()
            k.emit()
    return nc


def rope_tables():
    rows = L // 64
    row = np.repeat(np.arange(rows, dtype=np.float32), 64)
    col = np.tile(np.arange(64, dtype=np.float32), rows)
    inv = (1.0 / (10000.0 ** (np.arange(0, 64, 2, dtype=np.float32) / 64))).astype(np.float32)
    row_ang = row[:, None] * inv[None, :]
    col_ang = col[:, None] * inv[None, :]
    cosT = np.empty((128, L), np.float32)
    sinT = np.empty((128, L), np.float32)
    for m in range(128):
        ang = row_ang if m < 64 else col_ang
        j = m % 32
        first = (m % 64) < 32
        cosT[m] = np.cos(ang[:, j])
        sinT[m] = (-1.0 if first else 1.0) * np.sin(ang[:, j])
    return cosT, sinT


def att_consts():
    c = np.zeros((128, 4, 128), np.float32)
    c[:, 0, :] = np.eye(128)
    for m in range(128):
        partner = m + 32 if (m % 64) < 32 else m - 32
        c[partner, 1, m] = 1.0
    c[:, 2, :] = 1.0 / 128.0
    c[:, 3, :] = 1.0
    return bf16_np(c)


def run_att(x1, x1c, c, c_ctx, ada_w1, ada_b1, norm_g1, w_in, q_g, k_g, stop=0):
    nc = build_att(stop)
    cosT, sinT = rope_tables()
    consts = att_consts()
    in_maps = []
    QD, KVD = 2048, 512
    for core in range(NCORES):
        b, g = divmod(core, 4)
        cols = np.concatenate([np.arange(512 * g, 512 * g + 512), QD + np.arange(128 * g, 128 * g + 128),
                               QD + KVD + np.arange(128 * g, 128 * g + 128), QD + 2 * KVD + np.arange(512 * g, 512 * g + 512)])
        in_maps.append({
            "x1": np.ascontiguousarray(x1[b]), "xc": np.ascontiguousarray(x1c[b]),
            "cvec": np.ascontiguousarray(np.stack([c[b], c_ctx])),
            "ada_w": np.ascontiguousarray(ada_w1[:, :2048]), "ada_b": np.ascontiguousarray(ada_b1[:2048]),
            "norm_g": norm_g1, "w": np.ascontiguousarray(w_in[:, cols]),
            "qkg": np.ascontiguousarray(np.concatenate([q_g, k_g])), "cosT": cosT, "sinT": sinT, "consts": consts,
            "identf": np.eye(64, dtype=np.float32)})
    res = run_bass_kernel_spmd(nc, in_maps, core_ids=list(range(NCORES)))
    if stop:
        return res.results
    ogT = [np.concatenate([res.results[b * 4 + g]["ogT"] for g in range(4)], axis=0) for b in range(B)]
    return ogT


MAX_DECAY = math.log(1e-2) / 0.3
MIN_DECAY = math.log(1e-2) / 1.5
RLIST = [127] + list(range(64))
TWO_PI = 2.0 * math.pi


def hy_consts():
    f32 = np.float32
    n2 = np.arange(128)[:, None].astype(np.float64)
    k2 = np.arange(256)[None, :].astype(np.float64)
    ang = TWO_PI * n2 * k2 / 256.0
    F256 = np.concatenate([np.cos(ang), -np.sin(ang)], axis=1)
    n1 = np.arange(128)[:, None].astype(np.float64)
    k1 = np.arange(128)[None, :].astype(np.float64)
    a1 = TWO_PI * n1 * k1 / 128.0
    CS = np.stack([np.cos(a1), np.sin(a1), -np.sin(a1)], axis=1)
    r = np.array(RLIST)[None, :].astype(np.float64)
    ar = TWO_PI * np.arange(128)[:, None] * r / 128.0
    RI = np.stack([np.concatenate([np.cos(ar), np.sin(ar)], 1), np.concatenate([-np.sin(ar), np.cos(ar)], 1)], axis=1)
    CSs = np.zeros((128, 2, 2, 2, 128))
    for jj in range(2):
        for hh in range(2):
            a = TWO_PI * (jj * 128 + np.arange(128)[:, None]) * (hh * 128 + np.arange(128)[None, :]) / 256.0
            CSs[:, jj, hh, 0, :] = np.cos(a)
            CSs[:, jj, hh, 1, :] = -np.sin(a)
    ident = np.eye(128)
    mats = np.concatenate([F256, CS.reshape(128, -1), RI.reshape(128, -1), CSs.reshape(128, -1), ident], axis=1)
    return bf16_np(mats.astype(f32))


def hy_ztab(Lf):
    f32 = np.float32
    t = np.linspace(0.0, 1.0, Lf, dtype=f32)[:, None]
    w = (TWO_PI * np.arange(Lf, dtype=f32)[:, None] / Lf).astype(f32)
    fr = np.linspace(1e-4, 15, 16, dtype=f32)[None, :]
    z = np.concatenate([t, np.cos(fr * w), -np.sin(fr * w)], axis=-1).astype(f32)
    pos = np.concatenate([np.arange(Lf), [0], Lf - np.arange(Lf + 1, 2 * Lf)])
    zext = np.ascontiguousarray(z[pos].T)
    text = t[pos, 0].copy()
    text[Lf] = 1e4
    return zext, text.astype(f32)


def build_hy(stop=0, nc=None, T=None, pre=""):
    fused = nc is not None
    if not fused:
        nc = bass.Bass("TRN2", target_bir_lowering=False)
    dt = nc.dram_tensor

    def inp_(name, shape, dty):
        if fused:
            return T[name]
        return dt(name, shape, dty, kind="ExternalInput").ap()
    x = inp_("x", [L, D], F32)
    xc = inp_("xc", [CTX, D], F32)
    cvec = inp_("cvec", [2, D], F32)
    ada_w = inp_("ada_w", [D, 2048], F32)
    ada_b = inp_("ada_b", [2048], F32)
    norm_g = inp_("norm_g", [D], F32)
    w = inp_("w", [D, 2048], F32)
    vecs = inp_("vecs", [13 * 512], F32)
    fvec = inp_("fvec", [4 * 64], F32)
    fw1 = inp_("fw1", [33, 64], F32)
    fw23 = inp_("fw23", [64, 2, 64], F32)
    fw4 = inp_("fw4", [64, 2, 512], F32)
    zext = inp_("zext", [33, 2 * L], F32)
    zextc = inp_("zextc", [33, 2 * CTX], F32)
    negt = inp_("negt", [128, 128], F32)
    textc = inp_("textc", [128, 2 * CTX], F32)
    drow = inp_("drow", [128, 512], F32)
    ndcol = inp_("ndcol", [128, 4], F32)
    mats = inp_("mats", [128, 2308], BF16)
    identf_d = inp_("identf", [64, 64], F32)
    if fused:
        ygT, ygTc = T["ygT"], T["ygTc"]
    else:
        ygT = Chunked([dt("ygT", [512, L], BF16, kind="ExternalOutput").ap()], L)
        ygTc = dt("ygTc", [512, CTX], BF16, kind="ExternalOutput").ap()
    dbgf = dt("dbgf", [128, 4096], F32, kind="ExternalOutput").ap() if stop else None
    dbgh = dt("dbgh", [128, 16384], BF16, kind="ExternalOutput").ap() if stop else None
    b_in = Buf("in")
    with ExitStack() as es:
        k = KB(nc, es, pre)
        outs = []
        dsm = k.dsem()

        def dump(ap_out, ap_in, src_b):
            ob = Buf("dbg")
            outs.append(ob)
            k.dma("sp", ap_out, ap_in, ob, src_b, dsm)

        def finish():
            k.wait_all("sp", outs)
            k.barrier_all()
            k.emit()
        NT = (L + CTX) // 128
        hT_scr, hT_scr_b = k.dram("hT_scr", [128, 8, L + CTX], BF16)
        P = [k.psum("P%d" % i, [128, 512]) for i in range(8)]
        M, M_b = k.sbuf("mats_sb", [128, 2308], BF16)
        s_c = k.dsem()
        k.dma("sp", M[:], mats, M_b, b_in, s_c)
        F256 = M[:, 0:512]
        CS = M[:, 512:896].rearrange("p (a b) -> p a b", a=3)
        RI = M[:, 896:1156].rearrange("p (a b) -> p a b", a=2)
        CSs = M[:, 1156:2180].rearrange("p (j h t s) -> p j h t s", j=2, h=2, t=2)
        ident = M[:, 2180:2308]
        identf, identf_b = k.sbuf("identf_sb", [64, 64], F32)
        k.dma("sp", identf[:], identf_d, identf_b, b_in, s_c)
        gm, sh = emit_mod(k, cvec, ada_w, ada_b, norm_g, identf, identf_b, b_in, P[0][0], P[0][1], P[1][0], P[1][1])
        one, one_b = k.sbuf("one1", [1, 4], F32)
        k.op("dve", lambda e: e.memset(one[:], 1.0), [], [one_b])
        vec_to_cols(k, vecs, 52, 128, identf, identf_b, P[2][0], P[2][1], 0, b_in, "vv")
        colv, colv_b = k.sbuf("colv", [128, 52], F32)
        k.op("dve", lambda e: e.tensor_copy(out=colv[:], in_=P[2][0][:, 0:52]), [P[2][1]], [colv_b])

        def cv(v, sg):
            return colv[:, v * 4 + sg:v * 4 + sg + 1]
        vec_to_cols(k, fvec, 4, 64, identf, identf_b, P[3][0], P[3][1], 0, b_in, "fv")
        fcol, fcol_b = k.sbuf("fcol", [64, 4], F32)
        k.op("dve", lambda e: e.tensor_copy(out=fcol[:], in_=P[3][0][0:64, 0:4]), [P[3][1]], [fcol_b])
        k.op("dve", lambda e: e.tensor_tensor(out=fcol[:, 1:4], in0=fcol[:, 1:4], in1=fcol[:, 0:1].to_broadcast([64, 3]),
                                               op=ALU.mult), [fcol_b], [fcol_b])
        w1t, w1t_b = k.sbuf("w1t", [33, 64], F32)
        k.dma("sp", w1t[:], fw1, w1t_b, b_in, s_c)
        w23t, w23t_b = k.sbuf("w23t", [64, 2, 64], F32)
        k.dma("sp", w23t[:], fw23, w23t_b, b_in, s_c)
        w4t, w4t_b = k.sbuf("w4t", [64, 2, 512], F32)
        k.dma("sp", w4t[:], fw4, w4t_b, b_in, s_c)
        negt_t, negt_b = k.sbuf("negt_sb", [128, 128], F32)
        k.dma("sp", negt_t[:], negt, negt_b, b_in, s_c)
        textc_t, textc_b = k.sbuf("textc_sb", [128, 2 * CTX], F32)
        k.dma("sp", textc_t[:], textc, textc_b, b_in, s_c)
        drow_t, drow_b = k.sbuf("drow_sb", [128, 512], F32)
        k.dma("sp", drow_t[:], drow, drow_b, b_in, s_c)
        ndcol_t, ndcol_b = k.sbuf("ndcol_sb", [128, 4], F32)
        k.dma("sp", ndcol_t[:], ndcol, ndcol_b, b_in, s_c)
        onesf, onesf_b = k.sbuf("onesf", [128, 128], F32)
        k.op("pool", lambda e: e.memset(onesf[:], 1.0), [], [onesf_b])
        if stop == 1:
            dump(dbgf[:, 0:52], colv[:], colv_b)
            dump(dbgf[0:64, 64:68], fcol[:], fcol_b)
            finish()
            return nc
        ht = HT(k, ident, M_b, gm, sh)
        hst = [k.sbuf("hst0", [128, 8, 128], BF16)] * 2
        hss = [k.dsem()] * 2
        for t in range(NT):
            lat = t < L // 128
            xa = x[t * 128:(t + 1) * 128, :] if lat else xc[(t - L // 128) * 128:(t - L // 128 + 1) * 128, :]
            pt, pb = P[t % 2]
            hs_, hs_b = hst[t % 2]
            ht.tile(xa, b_in, 0 if lat else 1, pt[:].bitcast(BF16)[:, 0:1024].rearrange("p (k t) -> p k t", k=8), pb, hs_[:], hs_b)
            k.dma("act", hT_scr[:, :, t * 128:(t + 1) * 128], hs_[:], hT_scr_b, hs_b, hss[t % 2])
        hT_all = Buf("hT_all")
        k.wait_all("sp", [hT_scr_b])
        wst = [k.sbuf("wst0", [128, 512], F32)] * 2
        wss = [k.dsem()] * 2
        wbf, wbf_b = k.sbuf("wbf", [128, 8, 512], BF16)
        hblk = [k.sbuf("hblk%d" % i, [128, 8, 512], BF16) for i in range(2)]
        hbs = [k.dsem() for _ in range(2)]
        pbuf, pbuf_b = k.sbuf("pbuf", [128, L + 2], BF16)
        bufA, bufA_b = k.sbuf("bufA", [128, L], BF16)
        bufB, bufB_b = k.sbuf("bufB", [128, L], BF16)
        bufC, bufC_b = k.sbuf("bufC", [128, L], BF16)
        ctmp = [k.sbuf("ctmp%d" % i, [128, 1056], F32) for i in range(2)]
        u_tm, u_tm_b = bufB[:].rearrange("p (a b) -> p a b", a=64), bufB_b
        y_tm, y_tm_b = pbuf[:, 0:L].rearrange("p (a b) -> p a b", a=64), pbuf_b
        k_tm, k_tm_b = k.sbuf("k_tm", [128, 128, 128], BF16)
        zt = [k.sbuf("zt0", [33, 512], F32)] * 2
        zts = [k.dsem()] * 2
        hb0f = hblk[0][0][:].rearrange("p a b -> p (a b)").bitcast(F32)
        hm = [(hb0f[0:64, i * 512:(i + 1) * 512], hblk[0][1]) for i in range(3)]
        rr, rr_b = hb0f[0:64, 1536:2048], hblk[0][1]
        dec = [k.sbuf("dec%d" % i, [128, 128], F32) for i in range(2)]
        kf32 = [k.sbuf("kf32_%d" % i, [128, 128], F32) for i in range(2)]
        kab = [k.sbuf("kab%d" % i, [128, 128], F32) for i in range(2)]
        nrm, nrm_b = k.sbuf("nrm", [128, 128], F32)
        scol, scol_b = k.sbuf("scol", [128, 2], F32)
        kc_t, kc_b = k.sbuf("kc_t", [128, 2 * CTX], F32)
        Ak = [k.sbuf("Ak%d" % i, [128, 512], BF16) for i in range(2)]
        Au = [k.sbuf("Au%d" % i, [64, 512], BF16) for i in range(2)]
        KF = [k.sbuf("KF%d" % i, [128, 3, 256], F32) for i in range(2)]
        T1 = [k.sbuf("T1_%d" % i, [128, 512], F32) for i in range(2)]
        T2 = [k.sbuf("T2_%d" % i, [128, 512], F32) for i in range(2)]
        Yb = [k.sbuf("Yb%d" % i, [128, 512], BF16) for i in range(2)]
        Bs = [k.sbuf("Bs%d" % i, [128, 2, 2, 4, 65], BF16) for i in range(2)]
        h1s, h1s_b = k.sbuf("h1s", [128, 4, 65], F32)
        fin = [k.sbuf("fin0", [128, 1024], F32)] * 2
        fout = [k.sbuf("fout0", [128, 1024], BF16)] * 2
        fos = [k.dsem()] * 2
        pc_, pc_b = ctmp[0][0][:, 0:4 * (CTX + 2)].rearrange("p (a b) -> p a b", a=4), ctmp[0][1]
        xcv, xcv_b = ctmp[1][0][:, 0:4 * CTX].rearrange("p (a b) -> p a b", a=4), ctmp[1][1]
        acc = [k.sbuf("acc%d" % i, [128, CTX], F32) for i in range(2)]
        oc, oc_b = k.sbuf("oc", [128, CTX], BF16)
        ocs = k.dsem()
        w_v = w.rearrange("(k p) n -> p k n", p=128)

        h3_scr, _ = k.dram("h3_scr", [64, 2 * L], F32)
        h3c_scr, _ = k.dram("h3c_scr", [64, 2 * CTX], F32)
        h3_rb = [Buf("h3r") for _ in range(33)]
        h3s = k.dsem()

        def filter_mlp(ztile, ztile_b, ncols, j):
            cur, cur_b = ztile, ztile_b
            kdim = 33
            for layer in range(3):
                ps, psb = P[2 + (layer % 2)]
                wl = w1t[:, :] if layer == 0 else w23t[:, layer - 1, :]
                wl_b = w1t_b if layer == 0 else w23t_b
                k.op("pe", lambda e, ps=ps, wl=wl, cur=cur, kdim=kdim: e.matmul(
                    ps[0:64, :ncols], lhsT=wl, rhs=cur[0:kdim, :ncols], start=True, stop=True), [wl_b, cur_b], [psb])
                o_, o_b = hm[layer]
                k.op("dve", lambda e, o_=o_, ps=ps, layer=layer: e.tensor_scalar(
                    out=o_[:, :ncols], in0=ps[0:64, :ncols], scalar1=fcol[:, 0:1], scalar2=fcol[:, layer + 1:layer + 2],
                    op0=ALU.mult, op1=ALU.add), [psb, fcol_b], [o_b])
                MAGIC = 12582912.0
                k.op("dve", lambda e, o_=o_: e.tensor_scalar(
                    out=rr[:, :ncols], in0=o_[:, :ncols], scalar1=1.0 / TWO_PI, scalar2=MAGIC,
                    op0=ALU.mult, op1=ALU.add), [o_b], [rr_b])
                k.op("dve", lambda e: e.tensor_scalar(
                    out=rr[:, :ncols], in0=rr[:, :ncols], scalar1=MAGIC, scalar2=TWO_PI,
                    op0=ALU.subtract, op1=ALU.mult), [rr_b], [rr_b])
                k.op("dve", lambda e, o_=o_: e.tensor_tensor(out=o_[:, :ncols], in0=o_[:, :ncols], in1=rr[:, :ncols], op=ALU.subtract),
                     [o_b, rr_b], [o_b])
                k.op("act", lambda e, o_=o_: e.activation(out=o_[:, :ncols], in_=o_[:, :ncols], func=AF.Sin, scale=0.999999),
                     [o_b], [o_b])
                cur, cur_b, kdim = o_, o_b, 64
            return cur, cur_b

        def do_sg(sg):
            c0 = sg * 128
            for ty in range(4):
                for half in range(2):
                    i = (ty * 2 + half) % 2
                    t_, t_b = wst[i]
                    k.dma("sp", t_[:].rearrange("p (k n) -> p k n", k=4), w_v[:, half * 4:half * 4 + 4, ty * 512 + c0: ty * 512 + c0 + 128],
                          t_b, b_in, wss[i])
                    k.op("dve", lambda e, t_=t_, ty=ty, half=half: e.tensor_copy(
                        out=wbf[:, half * 4:half * 4 + 4, ty * 128:(ty + 1) * 128], in_=t_[:].rearrange("p (k n) -> p k n", k=4)),
                        [t_b], [wbf_b])
            for blk in range(2 * L // 512):
                if sg == 0:
                    z_, z_b = zt[blk % 2]
                    k.dma("sp", z_[:], zext[:, blk * 512:(blk + 1) * 512], z_b, b_in, zts[blk % 2])
                    h3, h3_b = filter_mlp(z_, z_b, 512, blk)
                    k.dma("act", h3_scr[:, blk * 512:(blk + 1) * 512], h3, h3_rb[blk], h3_b, h3s)
                else:
                    h3, h3_b = hm[2]
                    k.dma("sp", h3, h3_scr[:, blk * 512:(blk + 1) * 512], h3_b, h3_rb[blk], h3s)
                fb = 0 if blk < L // 512 else 1
                pk, pkb = P[4 + blk % 2]
                for tt in range(4):
                    k.op("pe", lambda e, pk=pk, tt=tt, h3=h3, fb=fb: e.matmul(
                        pk[:, tt * 128:(tt + 1) * 128], lhsT=h3[:, tt * 128:(tt + 1) * 128], rhs=w4t[:, fb, c0:c0 + 128],
                        start=(tt == 0), stop=True), [h3_b, w4t_b], [pkb], inc=(tt == 3))
                for tt in range(4):
                    m1 = blk * 4 + tt
                    d_, d_b = dec[m1 % 2]
                    f_, f_b = kf32[m1 % 2]
                    a_, a_b = kab[m1 % 2]
                    k.op("act", lambda e, d_=d_, m1=m1: e.activation(out=d_[:], in_=drow_t[:, c0:c0 + 128], func=AF.Exp,
                                                                       scale=negt_t[:, m1:m1 + 1]), [drow_b, negt_b], [d_b])
                    k.op("dve", lambda e, f_=f_, pk=pk, tt=tt, d_=d_: e.tensor_tensor(
                        out=f_[:], in0=pk[:, tt * 128:(tt + 1) * 128], in1=d_[:], op=ALU.mult), [pkb, d_b], [f_b])
                    k.op("pool", lambda e, f_=f_, m1=m1: e.tensor_copy(out=k_tm[:, m1, :], in_=f_[:]), [f_b], [k_tm_b])
                    k.op("act", lambda e, a_=a_, f_=f_: e.activation(out=a_[:], in_=f_[:], func=AF.Abs), [f_b], [a_b])
                    k.op("pe", lambda e, a_=a_, m1=m1: e.matmul(P[6][0][:, 0:128], lhsT=onesf[:], rhs=a_[:], start=(m1 == 0),
                                                                 stop=(m1 == 127)), [onesf_b, a_b], [P[6][1]])
            k.op("dve", lambda e: e.tensor_copy(out=nrm[:], in_=P[6][0][:, 0:128]), [P[6][1]], [nrm_b])
            row_to_cols(k, P[7][0], P[7][1], 0, nrm, nrm_b, 1, one, one_b)
            k.op("dve", lambda e: e.tensor_scalar(out=scol[:, 0:1], in0=P[7][0][:, 0:1], scalar1=32768.0, scalar2=None, op0=ALU.mult),
                 [P[7][1]], [scol_b])
            k.op("dve", lambda e: e.reciprocal(out=scol[:, 0:1], in_=scol[:, 0:1]), [scol_b], [scol_b])
            if sg == 0:
                z_, z_b = zt[0]
                k.dma("sp", z_[:], zextc[:, :], z_b, b_in, zts[0])
                h3, h3_b = filter_mlp(z_, z_b, 512, 0)
                k.dma("act", h3c_scr[:, :], h3, h3_rb[32], h3_b, h3s)
            else:
                h3, h3_b = hm[2]
                k.dma("sp", h3, h3c_scr[:, :], h3_b, h3_rb[32], h3s)
            pk, pkb = P[4]
            for fb in range(2):
                k.op("pe", lambda e, fb=fb, h3=h3: e.matmul(pk[:, fb * 256:(fb + 1) * 256], lhsT=w4t[:, fb, c0:c0 + 128],
                                                             rhs=h3[:, fb * 256:(fb + 1) * 256], start=(fb == 0), stop=True),
                     [w4t_b, h3_b], [pkb], inc=(fb == 1))
            k.op("act", lambda e: e.activation(out=kc_t[:], in_=textc_t[:], func=AF.Exp, scale=ndcol_t[:, sg:sg + 1]),
                 [textc_b, ndcol_b], [kc_b])
            k.op("dve", lambda e: e.tensor_tensor(out=kc_t[:], in0=pk[:], in1=kc_t[:], op=ALU.mult), [pkb, kc_b], [kc_b])
            k.op("dve", lambda e: e.tensor_reduce(out=scol[:, 1:2], in_=kc_t[:], axis=AX.X, op=ALU.add, apply_absolute_value=True),
                 [kc_b], [scol_b])
            k.op("dve", lambda e: e.reciprocal(out=scol[:, 1:2], in_=scol[:, 1:2]), [scol_b], [scol_b])
            if stop == 2 and sg == 0:
                dump(dbgh[:, 0:16384], k_tm[:].rearrange("p a b -> p (a b)"), k_tm_b)
                dump(dbgf[:, 0:512], kc_t[:], kc_b)
                dump(dbgf[:, 512:514], scol[:], scol_b)
                finish()
                return True
            def proj_stream(ty, consume):
                for blk in range(L // 512):
                    hb, hb_b = hblk[blk % 2]
                    k.dma("sp", hb[:], hT_scr[:, :, blk * 512:(blk + 1) * 512], hb_b, hT_scr_b, hbs[blk % 2])
                    ps, psb = P[blk % 2]
                    for kc in range(8):
                        k.op("pe", lambda e, ps=ps, kc=kc, hb=hb: e.matmul(
                            ps[:], lhsT=wbf[:, kc, ty * 128:(ty + 1) * 128], rhs=hb[:, kc, :], start=(kc == 0), stop=(kc == 7)),
                            [wbf_b, hb_b], [psb], inc=(kc == 7))
                    consume(blk, ps, psb)

            def to_pbuf(blk, ps, psb):
                k.op("act", lambda e, ps=ps, blk=blk: e.activation(out=pbuf[:, 1 + blk * 512:1 + (blk + 1) * 512], in_=ps[:],
                                                                     func=AF.Identity), [psb], [pbuf_b])

            def conv_to(dst, dst_b, vi):
                for ch in range(L // 1024):
                    o = ch * 1024
                    a_, a_b = ctmp[ch % 2]
                    k.op("act", lambda e, a_=a_, o=o: e.activation(out=a_[:, 0:1024], in_=pbuf[:, 1 + o:1 + o + 1024], func=AF.Identity,
                                                                    scale=cv(3 + vi, sg), bias=cv(9 + vi, sg)), [pbuf_b, colv_b], [a_b])
                    k.op("dve", lambda e, a_=a_, o=o: e.scalar_tensor_tensor(out=a_[:, 0:1024], in0=pbuf[:, o:o + 1024], scalar=cv(vi, sg),
                                                                              in1=a_[:, 0:1024], op0=ALU.mult, op1=ALU.add), [pbuf_b, colv_b, a_b], [a_b])
                    k.op("dve", lambda e, a_=a_, o=o: e.scalar_tensor_tensor(out=dst[:, o:o + 1024], in0=pbuf[:, 2 + o:2 + o + 1024],
                                                                               scalar=cv(6 + vi, sg), in1=a_[:, 0:1024], op0=ALU.mult, op1=ALU.add),
                         [pbuf_b, colv_b, a_b], [dst_b])

            k.op("dve", lambda e: e.memset(pbuf[:, 0:1], 0.0), [], [pbuf_b])
            k.op("dve", lambda e: e.memset(pbuf[:, L + 1:L + 2], 0.0), [], [pbuf_b])
            proj_stream(1, to_pbuf)
            conv_to(bufA, bufA_b, 1)
            proj_stream(2, to_pbuf)
            conv_to(bufB, bufB_b, 2)
            for ch in range(4):
                o = ch * 2048
                k.op("dve", lambda e, o=o: e.tensor_tensor(out=bufA[:, o:o + 2048], in0=bufA[:, o:o + 2048], in1=bufB[:, o:o + 2048],
                                                            op=ALU.mult), [bufA_b, bufB_b], [bufA_b])
            proj_stream(0, to_pbuf)
            conv_to(bufC, bufC_b, 0)

            def to_silu(blk, ps, psb):
                k.op("act", lambda e, ps=ps, blk=blk: e.activation(out=bufB[:, blk * 512:(blk + 1) * 512], in_=ps[:], func=AF.Silu),
                     [psb], [bufB_b])
            proj_stream(3, to_silu)
            for ch in range(4):
                o = ch * 2048
                k.op("dve", lambda e, o=o: e.tensor_tensor(out=bufC[:, o:o + 2048], in0=bufC[:, o:o + 2048], in1=bufB[:, o:o + 2048],
                                                            op=ALU.mult), [bufC_b, bufB_b], [bufC_b])
            if stop == 3 and sg == 0:
                dump(dbgh[:, 0:8192], bufA[:], bufA_b)
                dump(dbgh[:, 8192:16384], bufC[:], bufC_b)
                finish()
                return True
            for n8 in range(8):
                pt, pb = P[n8 % 2]
                ptb = pt[:].bitcast(BF16)[:, 0:1024].rearrange("p (a c) -> p a c", a=8)
                for a in range(8):
                    n1 = n8 * 8 + a
                    k.op("pe", lambda e, ptb=ptb, a=a, n1=n1: e.transpose(ptb[:, a, :], bufA[:, n1 * 128:(n1 + 1) * 128], ident),
                         [bufA_b, M_b], [pb], inc=(a == 7))
                k.op("act", lambda e, ptb=ptb, n8=n8: e.activation(out=u_tm[:, n8 * 8:(n8 + 1) * 8, :], in_=ptb, func=AF.Identity),
                     [pb], [u_tm_b])
            hb, hb_b = hblk[0]
            k.dma("sp", hb[:, :, 0:CTX], hT_scr[:, :, L:L + CTX], hb_b, hT_scr_b, hbs[0])
            k.op("dve", lambda e: e.memset(pc_, 0.0), [], [pc_b])
            for ty in range(4):
                ps, psb = P[ty % 2]
                for kc in range(8):
                    k.op("pe", lambda e, ps=ps, kc=kc, ty=ty: e.matmul(ps[:, 0:CTX], lhsT=wbf[:, kc, ty * 128:(ty + 1) * 128],
                                                                        rhs=hb[:, kc, 0:CTX], start=(kc == 0), stop=(kc == 7)),
                         [wbf_b, hb_b], [psb], inc=(kc == 7))
                if ty < 3:
                    k.op("act", lambda e, ps=ps, ty=ty: e.activation(out=pc_[:, ty, 1:CTX + 1], in_=ps[:, 0:CTX], func=AF.Identity),
                         [psb], [pc_b])
                else:
                    k.op("act", lambda e, ps=ps: e.activation(out=xcv[:, 3, :], in_=ps[:, 0:CTX], func=AF.Silu), [psb], [xcv_b])
            for ty in range(3):
                k.op("act", lambda e, ty=ty: e.activation(out=xcv[:, ty, :], in_=pc_[:, ty, 1:CTX + 1], func=AF.Identity,
                                                            scale=cv(3 + ty, sg), bias=cv(9 + ty, sg)), [pc_b, colv_b], [xcv_b])
                k.op("dve", lambda e, ty=ty: e.scalar_tensor_tensor(out=xcv[:, ty, :], in0=pc_[:, ty, 0:CTX], scalar=cv(ty, sg),
                                                                     in1=xcv[:, ty, :], op0=ALU.mult, op1=ALU.add), [pc_b, colv_b, xcv_b], [xcv_b])
                k.op("dve", lambda e, ty=ty: e.scalar_tensor_tensor(out=xcv[:, ty, :], in0=pc_[:, ty, 2:CTX + 2], scalar=cv(6 + ty, sg),
                                                                     in1=xcv[:, ty, :], op0=ALU.mult, op1=ALU.add), [pc_b, colv_b, xcv_b], [xcv_b])
            k.op("dve", lambda e: e.tensor_tensor(out=xcv[:, 1, :], in0=xcv[:, 1, :], in1=xcv[:, 2, :], op=ALU.mult), [xcv_b], [xcv_b])
            k.op("dve", lambda e: e.tensor_tensor(out=xcv[:, 0, :], in0=xcv[:, 0, :], in1=xcv[:, 3, :], op=ALU.mult), [xcv_b], [xcv_b])
            uc = xcv[:, 1, :]
            ctt = [(T1[0][0][:, 0:CTX], T1[0][1]), (T2[0][0][:, 0:CTX], T2[0][1])]
            for a in range(2):
                k.op("pool", lambda e, a=a: e.memset(acc[a][0][:], 0.0), [], [acc[a][1]])
            lag_ops = []
            for lag in range(-(CTX - 1), CTX):
                if lag >= 0:
                    lag_ops.append((slice(lag, CTX), slice(0, CTX - lag), lag))
                else:
                    m = -lag
                    lag_ops.append((slice(0, CTX - m), slice(m, CTX), 2 * CTX - m))
            lag_state = [0]

            def emit_lags(nops):
                for _ in range(nops):
                    n = lag_state[0]
                    if n >= len(lag_ops):
                        return
                    lag_state[0] += 1
                    osl, isl, idx = lag_ops[n]
                    a_, a_b = acc[n % 2]
                    k.op("dve", lambda e, a_=a_, osl=osl, isl=isl, idx=idx: e.scalar_tensor_tensor(
                        out=a_[:, osl], in0=uc[:, isl], scalar=kc_t[:, idx:idx + 1], in1=a_[:, osl], op0=ALU.mult, op1=ALU.add),
                        [xcv_b, kc_b, a_b], [a_b])
            def S0(c):
                pp = c % 2
                k.op("pe", lambda e: e.matmul(P[pp][0][:], lhsT=k_tm[:, :, c], rhs=F256, start=True, stop=True),
                     [k_tm_b, M_b], [P[pp][1]])

            def S1(c):
                pp = c % 2
                k.op("act", lambda e: e.activation(out=Ak[pp][0][:], in_=P[pp][0][:], func=AF.Identity), [P[pp][1]], [Ak[pp][1]])

            def S2(c):
                pp = c % 2
                A_, A_b = Ak[pp]
                ps, psb = P[2 + pp]
                k.op("pe", lambda e: e.matmul(ps[:], lhsT=CS[:, 0, :], rhs=A_[:], start=True, stop=False), [M_b, A_b], [psb], inc=False)
                k.op("pe", lambda e: e.matmul(ps[:, 0:256], lhsT=CS[:, 1, :], rhs=A_[:, 256:512], start=False, stop=False),
                     [M_b, A_b], [psb], inc=False)
                k.op("pe", lambda e: e.matmul(ps[:, 256:512], lhsT=CS[:, 2, :], rhs=A_[:, 0:256], start=False, stop=True),
                     [M_b, A_b], [psb])

            def S3(c):
                pp = c % 2
                ps, psb = P[2 + pp]
                KF_, KF_b = KF[pp]
                k.op("act", lambda e: e.activation(out=KF_[:, 0:2, :], in_=ps[:].rearrange("p (a b) -> p a b", a=2), func=AF.Identity),
                     [psb], [KF_b])
                k.op("act", lambda e: e.activation(out=KF_[:, 2, :], in_=ps[:, 256:512], func=AF.Identity, scale=-1.0),
                     [psb], [KF_b])

            def S4(c):
                pp = c % 2
                k.op("pe", lambda e: e.matmul(P[pp][0][0:64, :], lhsT=u_tm[:, :, c], rhs=F256, start=True, stop=True),
                     [u_tm_b, M_b], [P[pp][1]])

            def S5(c):
                pp = c % 2
                k.op("act", lambda e: e.activation(out=Au[pp][0][:], in_=P[pp][0][0:64, :], func=AF.Identity), [P[pp][1]], [Au[pp][1]])

            def S6(c):
                pp = c % 2
                A_, A_b = Au[pp]
                ps, psb = P[2 + pp]
                k.op("pe", lambda e: e.matmul(ps[:], lhsT=CS[0:64, 0, :], rhs=A_[:], start=True, stop=False), [M_b, A_b], [psb], inc=False)
                k.op("pe", lambda e: e.matmul(ps[:, 0:256], lhsT=CS[0:64, 1, :], rhs=A_[:, 256:512], start=False, stop=False),
                     [M_b, A_b], [psb], inc=False)
                k.op("pe", lambda e: e.matmul(ps[:, 256:512], lhsT=CS[0:64, 2, :], rhs=A_[:, 0:256], start=False, stop=True),
                     [M_b, A_b], [psb])

            def S7(c):
                pp = c % 2
                X, X_b = P[2 + pp]
                KF_, KF_b = KF[pp]
                a1, a1_b = T1[pp]
                a2, a2_b = T2[pp]
                k.op("dve", lambda e: e.tensor_tensor(out=a1[:].rearrange("p (a b) -> p a b", a=2), in0=X[:].rearrange("p (a b) -> p a b", a=2),
                                                       in1=KF_[:, 0:1, :].to_broadcast([128, 2, 256]), op=ALU.mult), [X_b, KF_b], [a1_b])
                k.op("dve", lambda e: e.tensor_tensor(out=a2[:, 0:256], in0=X[:, 256:512], in1=KF_[:, 2, :], op=ALU.mult),
                     [X_b, KF_b], [a2_b])
                k.op("dve", lambda e: e.tensor_tensor(out=a2[:, 256:512], in0=X[:, 0:256], in1=KF_[:, 1, :], op=ALU.mult),
                     [X_b, KF_b], [a2_b])

            def S8(c):
                pp = c % 2
                k.op("pool", lambda e: e.tensor_tensor(out=Yb[pp][0][:], in0=T1[pp][0][:], in1=T2[pp][0][:], op=ALU.add),
                     [T1[pp][1], T2[pp][1]], [Yb[pp][1]])

            def S9(c):
                pp = c % 2
                Y_, Y_b = Yb[pp]
                ps, psb = P[4 + pp]
                for jj in range(2):
                    k.op("pe", lambda e, jj=jj: e.matmul(ps[:, jj * 130:(jj + 1) * 130], lhsT=Y_[:, jj * 128:(jj + 1) * 128],
                                                          rhs=RI[:, 0, :], start=True, stop=False), [Y_b, M_b], [psb], inc=False)
                    k.op("pe", lambda e, jj=jj: e.matmul(ps[:, jj * 130:(jj + 1) * 130], lhsT=Y_[:, 256 + jj * 128:256 + (jj + 1) * 128],
                                                          rhs=RI[:, 1, :], start=False, stop=True), [Y_b, M_b], [psb], inc=(jj == 1))

            def S10(c):
                pp = c % 2
                cq = c % 4
                Bt_, Bt_b = Bs[(c // 4) % 2]
                ps, psb = P[4 + pp]
                k.op("act", lambda e: e.activation(
                    out=Bt_[:, :, :, cq, :], in_=ps[:, 0:260].rearrange("p (j r n) -> p j r n", j=2, r=2), func=AF.Identity),
                    [psb], [Bt_b])

            def S11(c):
                if c % 4 != 3:
                    return
                Bt_, Bt_b = Bs[(c // 4) % 2]
                for hh in range(2):
                    n = 0
                    for jj in range(2):
                        for ri in range(2):
                            k.op("pe", lambda e, hh=hh, jj=jj, ri=ri, n=n: e.matmul(
                                P[6 + hh][0][:, 0:260], lhsT=CSs[:, jj, hh, ri, :],
                                rhs=Bt_[:, jj, ri, :, :].rearrange("p c n -> p (c n)"), start=(n == 0), stop=(n == 3)),
                                [M_b, Bt_b], [P[6 + hh][1]], inc=(n == 3))
                            n += 1
                k.op("act", lambda e: e.activation(out=h1s[:], in_=P[7][0][:, 0:260].rearrange("p (c n) -> p c n", c=4), func=AF.Identity),
                     [P[7][1]], [h1s_b])
                cb = c - 3
                k.op("dve", lambda e: e.tensor_tensor(
                    out=y_tm[:, :, cb:cb + 4].rearrange("p n c -> p c n"),
                    in0=P[6][0][:, 0:260].rearrange("p (c n) -> p c n", c=4)[:, :, 1:65], in1=h1s[:, :, 0:64], op=ALU.add),
                    [P[6][1], h1s_b], [y_tm_b])

            stages = [(S0, 0), (S1, 0), (S2, 1), (S3, 1), (S4, 1), (S5, 1), (S6, 2), (S7, 2), (S8, 2), (S9, 3), (S10, 3), (S11, 3)]
            for tstep in range(128 + 3):
                for fn_, off in stages:
                    cch = tstep - off
                    if 0 <= cch < 128:
                        fn_(cch)
                emit_lags(4)
            emit_lags(len(lag_ops))
            if stop == 4 and sg == 0:
                dump(dbgh[:, 0:8192], y_tm[:].rearrange("p a b -> p (a b)"), y_tm_b)
                dump(dbgf[:, 0:2], scol[:], scol_b)
                finish()
                return True
            for n8 in range(8):
                pt, pb = P[n8 % 2]
                ptb = pt[:].bitcast(BF16)[:, 0:1024].rearrange("p (a c) -> p a c", a=8)
                for a in range(8):
                    n1 = n8 * 8 + a
                    k.op("pe", lambda e, ptb=ptb, a=a, n1=n1: e.transpose(ptb[:, a, :], y_tm[:, n1, :], ident),
                         [y_tm_b, M_b], [pb], inc=(a == 7))
                f_, f_b = fin[n8 % 2]
                o_, o_b = fout[n8 % 2]
                o = n8 * 1024
                k.op("act", lambda e, f_=f_, ptb=ptb: e.activation(out=f_[:], in_=ptb.rearrange("p a c -> p (a c)"), func=AF.Identity,
                                                                     scale=scol[:, 0:1]), [pb, scol_b], [f_b])
                k.op("dve", lambda e, f_=f_, o=o: e.scalar_tensor_tensor(out=f_[:], in0=bufA[:, o:o + 1024], scalar=cv(12, sg), in1=f_[:],
                                                                          op0=ALU.mult, op1=ALU.add), [bufA_b, colv_b, f_b], [f_b])
                k.op("dve", lambda e, f_=f_, o_=o_, o=o: e.tensor_tensor(out=o_[:], in0=f_[:], in1=bufC[:, o:o + 1024], op=ALU.mult),
                     [f_b, bufC_b], [o_b])
                ob = Buf("yo")
                outs.append(ob)
                k.dma("act", ygT.rows_cols(c0, c0 + 128, o, 1024), o_[:], ob, o_b, fos[n8 % 2])
            k.op("dve", lambda e: e.tensor_tensor(out=acc[0][0][:], in0=acc[0][0][:], in1=acc[1][0][:], op=ALU.add),
                 [acc[0][1], acc[1][1]], [acc[0][1]])
            k.op("dve", lambda e: e.tensor_scalar(out=acc[0][0][:], in0=acc[0][0][:], scalar1=scol[:, 1:2], scalar2=None, op0=ALU.mult),
                 [acc[0][1], scol_b], [acc[0][1]])
            k.op("dve", lambda e: e.scalar_tensor_tensor(out=acc[0][0][:], in0=uc, scalar=cv(12, sg), in1=acc[0][0][:],
                                                          op0=ALU.mult, op1=ALU.add), [xcv_b, colv_b, acc[0][1]], [acc[0][1]])
            k.op("dve", lambda e: e.tensor_tensor(out=oc[:], in0=acc[0][0][:], in1=xcv[:, 0, :], op=ALU.mult), [acc[0][1], xcv_b], [oc_b])
            ob = Buf("yco")
            outs.append(ob)
            k.dma("act", ygTc[c0:c0 + 128, :], oc[:], ob, oc_b, ocs)
            return False
        for sg_ in range(4):
            if do_sg(sg_):
                return nc
        finish()
    return nc


def run_hy(x, ctx, c, c_ctx, ada_w0, ada_b0, norm_g0, inp, stop=0):
    nc = build_hy(stop)
    f32 = np.float32
    mats = hy_consts()
    zext, text = hy_ztab(L)
    zextc, textc = hy_ztab(CTX)
    negt = np.ascontiguousarray((-text).reshape(128, 128).T)
    textc_b = np.ascontiguousarray(np.broadcast_to(textc[None, :], (128, 2 * CTX))).astype(f32)
    deltas = np.abs(np.linspace(MIN_DECAY, MAX_DECAY, E, dtype=f32))
    w_in, conv_w, conv_b = inp["hy_w_in"][0], inp["hy_conv_w"][0], inp["hy_conv_b"][0]
    in_maps = []
    for core in range(NCORES):
        b, g = divmod(core, 4)
        ch = np.arange(512 * g, 512 * g + 512)
        cols = np.concatenate([ty * E + ch for ty in range(4)])
        vecs = np.concatenate([conv_w[tap, ty * E + ch] for tap in range(3) for ty in range(3)] +
                              [conv_b[ty * E + ch] for ty in range(3)] + [inp["hy_d"][0][ch]]).astype(f32)
        fvec = np.concatenate([inp["hy_freq"][0], inp["hy_fb1"][0], inp["hy_fb2"][0], inp["hy_fb3"][0]]).astype(f32)
        fw4 = np.stack([inp["hy_fw4"][0][:, ch], inp["hy_fw4"][0][:, E + ch]], axis=1)
        dl = deltas[ch]
        in_maps.append({
            "x": np.ascontiguousarray(x[b]), "xc": np.ascontiguousarray(ctx[b]),
            "cvec": np.ascontiguousarray(np.stack([c[b], c_ctx])),
            "ada_w": np.ascontiguousarray(ada_w0[:, :2048]), "ada_b": np.ascontiguousarray(ada_b0[:2048]), "norm_g": norm_g0,
            "w": np.ascontiguousarray(w_in[:, cols]), "vecs": vecs, "fvec": fvec, "fw1": inp["hy_fw1"][0],
            "fw23": np.ascontiguousarray(np.stack([inp["hy_fw2"][0], inp["hy_fw3"][0]], axis=1)),
            "fw4": np.ascontiguousarray(fw4), "zext": zext, "zextc": zextc, "negt": negt, "textc": textc_b,
            "drow": np.ascontiguousarray(np.broadcast_to(dl[None, :], (128, 512))).astype(f32),
            "ndcol": np.ascontiguousarray(-dl.reshape(4, 128).T), "mats": mats, "identf": np.eye(64, dtype=np.float32)})
    res = run_bass_kernel_spmd(nc, in_maps, core_ids=list(range(NCORES)))
    if stop:
        return res.results
    ygT = [np.concatenate([res.results[b * 4 + g]["ygT"] for g in range(4)], axis=0) for b in range(B)]
    ygTc = [np.concatenate([res.results[b * 4 + g]["ygTc"] for g in range(4)], axis=0) for b in range(B)]
    return ygT, ygTc


RG = [[0, 1, 2, 3], [4, 5, 6, 7]]


def _all_gather(nc, srcs, dsts, name):
    n = len(srcs)
    with ExitStack() as es:
        sem = es.enter_context(nc.semaphore(name))
        block = es.enter_context(nc.Block())

        def pool(g):
            for src, dst in zip(srcs, dsts):
                g.collective_compute("AllGather", ALU.bypass, replica_groups=RG, ins=[src], outs=[dst]).then_inc(sem, 1)
            g.wait_ge(sem, n)

        def other(e):
            e.wait_ge(sem, n)
        block.gpsimd(pool)
        block.tensor(other)
        block.scalar(other)
        block.vector(other)
        block.sync(other)


FUSED_INPUTS = [
    ("x", [L, D], F32), ("xc", [CTX, D], F32), ("cvec", [2, D], F32), ("identf", [64, 64], F32),
    ("hy_ada_w", [D, 2048], F32), ("hy_ada_b", [2048], F32), ("hy_norm_g", [D], F32), ("hy_w", [D, 2048], F32),
    ("vecs", [13 * 512], F32), ("fvec", [256], F32), ("fw1", [33, 64], F32), ("fw23", [64, 2, 64], F32),
    ("fw4", [64, 2, 512], F32), ("zext", [33, 2 * L], F32), ("zextc", [33, 2 * CTX], F32), ("negt", [128, 128], F32),
    ("textc", [128, 2 * CTX], F32), ("drow", [128, 512], F32), ("ndcol", [128, 4], F32), ("mats", [128, 2308], BF16),
    ("at_ada_w", [D, 2048], F32), ("at_ada_b", [2048], F32), ("at_norm_g", [D], F32), ("at_w", [D, 1280], F32),
    ("qkg", [256], F32), ("cosT", [128, L], F32), ("sinT", [128, L], F32), ("consts", [128, 4, 128], BF16),
    ("w_out0", [E, D], F32), ("ada_w_gt0", [D, D], F32), ("ada_b_gt0", [D], F32),
    ("w_out1", [E, D], F32), ("ada_w_gt1", [D, D], F32), ("ada_b_gt1", [D], F32),
]


def build_fused():
    nc = bass.Bass("TRN2", target_bir_lowering=False)
    I = {n: nc.dram_tensor(n, sh, dty, kind="ExternalInput").ap() for n, sh, dty in FUSED_INPUTS}
    out = nc.dram_tensor("out", [L, D], F32, kind="ExternalOutput").ap()
    CW = 1024
    widths = [CW] * (L // CW) + [CTX]
    yg_loc = [nc.dram_tensor("yg_loc%d" % j, [512, wd], BF16, kind="Internal").ap() for j, wd in enumerate(widths)]
    yg_all = [nc.dram_tensor("yg_all%d" % j, [E, wd], BF16, kind="Internal").ap() for j, wd in enumerate(widths)]
    x1_full = nc.dram_tensor("x1_full", [L, D], F32, kind="Internal").ap()
    og_loc = [nc.dram_tensor("og_loc%d" % j, [512, CW], BF16, kind="Internal").ap() for j in range(L // CW)]
    og_all = [nc.dram_tensor("og_all%d" % j, [E, CW], BF16, kind="Internal").ap() for j in range(L // CW)]
    T_hy = dict(I)
    T_hy.update(ada_w=I["hy_ada_w"], ada_b=I["hy_ada_b"], norm_g=I["hy_norm_g"], w=I["hy_w"],
                ygT=Chunked(yg_loc[:L // CW], CW), ygTc=yg_loc[L // CW])
    build_hy(0, nc=nc, T=T_hy, pre="h_")
    _all_gather(nc, yg_loc, yg_all, "cc1")
    T_at = dict(I)
    T_at.update(ada_w=I["at_ada_w"], ada_b=I["at_ada_b"], norm_g=I["at_norm_g"], w=I["at_w"], cvec0=I["cvec"],
                yg_all=Chunked(yg_all, CW), x1_full=x1_full, og_loc=Chunked(og_loc, CW))
    build_att(0, nc=nc, T=T_at, pre="a_")
    _all_gather(nc, og_loc, og_all, "cc2")
    T_op = dict(ygT=Chunked(og_all, CW), xres=x1_full, w_out=I["w_out1"], cvec=I["cvec"], ada_w_gt=I["ada_w_gt1"],
                ada_b_gt=I["ada_b_gt1"], xout=out)
    build_op(L // 128, 0, nc=nc, T=T_op, pre="o_")
    return nc


def fused_inputs(inp):
    f32 = np.float32
    x, c, ctx, c_ctx = inp["x"], inp["c"], inp["ctx"], inp["c_ctx"]
    ada_w, ada_b, norm_g = inp["ada_w"], inp["ada_b"], inp["norm_g"]
    mats = hy_consts()
    zext, text = hy_ztab(L)
    zextc, textc = hy_ztab(CTX)
    negt = np.ascontiguousarray((-text).reshape(128, 128).T)
    textc_b = np.ascontiguousarray(np.broadcast_to(textc[None, :], (128, 2 * CTX))).astype(f32)
    deltas = np.abs(np.linspace(MIN_DECAY, MAX_DECAY, E, dtype=f32))
    w_in, conv_w, conv_b = inp["hy_w_in"][0], inp["hy_conv_w"][0], inp["hy_conv_b"][0]
    cosT, sinT = rope_tables()
    consts = att_consts()
    QD, KVD = 2048, 512
    shared = {
        "identf": np.eye(64, dtype=f32), "hy_ada_w": np.ascontiguousarray(ada_w[0][:, :2048]),
        "hy_ada_b": np.ascontiguousarray(ada_b[0][:2048]), "hy_norm_g": norm_g[0],
        "fvec": np.concatenate([inp["hy_freq"][0], inp["hy_fb1"][0], inp["hy_fb2"][0], inp["hy_fb3"][0]]).astype(f32),
        "fw1": inp["hy_fw1"][0], "fw23": np.ascontiguousarray(np.stack([inp["hy_fw2"][0], inp["hy_fw3"][0]], axis=1)),
        "zext": zext, "zextc": zextc, "negt": negt, "textc": textc_b, "mats": mats,
        "at_ada_w": np.ascontiguousarray(ada_w[1][:, :2048]), "at_ada_b": np.ascontiguousarray(ada_b[1][:2048]),
        "at_norm_g": norm_g[1], "qkg": np.ascontiguousarray(np.concatenate([inp["at_q_g"][0], inp["at_k_g"][0]])),
        "cosT": cosT, "sinT": sinT, "consts": consts,
        "w_out0": inp["hy_w_out"][0], "ada_w_gt0": np.ascontiguousarray(ada_w[0][:, 2048:]),
        "ada_b_gt0": np.ascontiguousarray(ada_b[0][2048:]),
        "w_out1": inp["at_w_out"][0], "ada_w_gt1": np.ascontiguousarray(ada_w[1][:, 2048:]),
        "ada_b_gt1": np.ascontiguousarray(ada_b[1][2048:]),
    }
    in_maps = []
    for core in range(NCORES):
        b, g = divmod(core, 4)
        ch = np.arange(512 * g, 512 * g + 512)
        cols = np.concatenate([ty * E + ch for ty in range(4)])
        vecs = np.concatenate([conv_w[tap, ty * E + ch] for tap in range(3) for ty in range(3)] +
                              [conv_b[ty * E + ch] for ty in range(3)] + [inp["hy_d"][0][ch]]).astype(f32)
        fw4 = np.stack([inp["hy_fw4"][0][:, ch], inp["hy_fw4"][0][:, E + ch]], axis=1)
        dl = deltas[ch]
        acols = np.concatenate([np.arange(512 * g, 512 * g + 512), QD + np.arange(128 * g, 128 * g + 128),
                                QD + KVD + np.arange(128 * g, 128 * g + 128), QD + 2 * KVD + np.arange(512 * g, 512 * g + 512)])
        m = dict(shared)
        m.update({
            "x": np.ascontiguousarray(x[b]), "xc": np.ascontiguousarray(ctx[b]),
            "cvec": np.ascontiguousarray(np.stack([c[b], c_ctx])),
            "hy_w": np.ascontiguousarray(w_in[:, cols]), "vecs": vecs, "fw4": np.ascontiguousarray(fw4),
            "drow": np.ascontiguousarray(np.broadcast_to(dl[None, :], (128, 512))).astype(f32),
            "ndcol": np.ascontiguousarray(-dl.reshape(4, 128).T),
            "at_w": np.ascontiguousarray(inp["at_w_in"][0][:, acols]),
        })
        in_maps.append(m)
    return in_maps


def kernel(**inp):
    inp = {k_: np.asarray(v) for k_, v in inp.items()}
    nc = build_fused()
    in_maps = fused_inputs(inp)
    res = run_bass_kernel_spmd(nc, in_maps, core_ids=list(range(NCORES)))
    out = np.stack([res.results[b * 4]["out"] for b in range(B)], axis=0)
    return out.astype(np.float32)


def kernel_unfused(**inp):
    inp = {k_: np.asarray(v) for k_, v in inp.items()}
    x, c, ctx, c_ctx = inp["x"], inp["c"], inp["ctx"], inp["c_ctx"]
    ada_w, ada_b, norm_g = inp["ada_w"], inp["ada_b"], inp["norm_g"]
    ygT, ygTc = run_hy(x, ctx, c, c_ctx, ada_w[0], ada_b[0], norm_g[0], inp)
    x1, x1c = run_op(ygT, x, inp["hy_w_out"][0], c, c_ctx, np.ascontiguousarray(ada_w[0][:, 2048:]),
                     np.ascontiguousarray(ada_b[0][2048:]), True, ygTc, ctx)
    ogT = run_att(x1, x1c, c, c_ctx, ada_w[1], ada_b[1], norm_g[1], inp["at_w_in"][0], inp["at_q_g"][0], inp["at_k_g"][0])
    out, _ = run_op(ogT, x1, inp["at_w_out"][0], c, c_ctx, np.ascontiguousarray(ada_w[1][:, 2048:]),
                    np.ascontiguousarray(ada_b[1][2048:]), False)
    return out.astype(np.float32)
```

```python
import math
from contextlib import ExitStack

import numpy as np
import ml_dtypes
import concourse.bass as bass
import concourse.mybir as mybir
from concourse.bass_utils import run_bass_kernel_spmd

F32 = mybir.dt.float32
BF16 = mybir.dt.bfloat16
ALU = mybir.AluOpType
AF = mybir.ActivationFunctionType
AX = mybir.AxisListType

D = 1024
B = 2
L = 8192
CTX = 256
E = 2048
EPS = 1e-6
NCORES = 8


class Sem:
    def __init__(self, h):
        self.h = h
        self.cnt = 0


class Buf:
    __slots__ = ("name", "w", "rs")

    def __init__(self, name):
        self.name = name
        self.w = None
        self.rs = []


class KB:
    ENG = ("pe", "act", "dve", "pool", "sp")

    def __init__(self, nc, es, pre=""):
        self.nc = nc
        self.es = es
        self.pre = pre
        self.all_sems = []
        self.esem = {e: Sem(es.enter_context(nc.semaphore(pre + "e_" + e))) for e in self.ENG}
        self.all_sems.extend(self.esem.values())
        self.prog = {e: [] for e in self.ENG}
        self.seen = {e: {} for e in self.ENG}
        self.pending_noinc = {e: False for e in self.ENG}
        self.nsem = 0

    def sbuf(self, name, shape, dt):
        t = self.es.enter_context(self.nc.sbuf_tensor(self.pre + name, list(shape), dt))
        return t, Buf(name)

    def psum(self, name, shape, dt=F32):
        t = self.es.enter_context(self.nc.psum_tensor(self.pre + name, list(shape), dt))
        return t, Buf(name)

    def dsem(self, name=None):
        self.nsem += 1
        sm = Sem(self.es.enter_context(self.nc.semaphore(self.pre + (name or ("d%d" % self.nsem)))))
        self.all_sems.append(sm)
        return sm

    def dram(self, name, shape, dt, kind="Internal"):
        t = self.nc.dram_tensor(self.pre + name, list(shape), dt, kind=kind)
        return t.ap(), Buf(name)

    def barrier_all(self):
        for eng in self.ENG:
            waits = []
            seen = self.seen[eng]
            for sm in self.all_sems:
                if sm.cnt > 0 and seen.get(sm, 0) < sm.cnt:
                    seen[sm] = sm.cnt
                    waits.append((sm, sm.cnt))
            self.prog[eng].append((waits, None, None, 0))

    def _waits(self, eng, reads, writes):
        deps = []
        for b in reads:
            if b.w is not None:
                deps.append(b.w)
        for b in writes:
            if b.w is not None:
                deps.append(b.w)
            deps.extend(b.rs)
        need = {}
        for s, v in deps:
            if s is self.esem[eng] and eng == "pe":
                continue
            if need.get(s, 0) < v:
                need[s] = v
        out = []
        seen = self.seen[eng]
        for s, v in need.items():
            if seen.get(s, 0) >= v:
                continue
            seen[s] = v
            out.append((s, v))
        return out

    def op(self, eng, fn, reads=(), writes=(), inc=True):
        waits = self._waits(eng, reads, writes)
        s = self.esem[eng]
        if inc:
            s.cnt += 1
            val = s.cnt
        else:
            val = s.cnt + 1
        for b in writes:
            b.w = (s, val)
            b.rs = []
        for b in reads:
            if b not in writes:
                b.rs.append((s, val))
                if len(b.rs) > 64:
                    b.rs = _compact(b.rs)
        self.prog[eng].append((waits, fn, s if inc else None, 1))

    def dma(self, q, out, in_, dst, src, sem):
        waits = self._waits(q, [src], [dst])
        sem.cnt += 16
        val = sem.cnt
        dst.w = (sem, val)
        dst.rs = []
        src.rs.append((sem, val))
        if len(src.rs) > 64:
            src.rs = _compact(src.rs)
        self.prog[q].append((waits, lambda e: e.dma_start(out=out, in_=in_), sem, 16))

    def wait_all(self, eng, bufs):
        waits = self._waits(eng, list(bufs), [])
        self.prog[eng].append((waits, None, None, 0))

    def emit(self):
        nc = self.nc
        with nc.Block() as block:
            def run(eng_name):
                def body(e):
                    for waits, fn, sem, n in self.prog[eng_name]:
                        for s, v in waits:
                            e.wait_ge(s.h, v)
                        if fn is None:
                            continue
                        ins = fn(e)
                        if sem is not None:
                            ins.then_inc(sem.h, n)
                return body
            block.tensor(run("pe"))
            block.scalar(run("act"))
            block.vector(run("dve"))
            block.gpsimd(run("pool"))
            block.sync(run("sp"))


class Chunked:
    def __init__(self, aps, cw):
        self.aps = aps
        self.cw = cw

    def cols(self, c0, n):
        j, o = divmod(c0, self.cw)
        assert o + n <= self.cw
        return self.aps[j][:, o:o + n]

    def rows_cols(self, r0, r1, c0, n):
        j, o = divmod(c0, self.cw)
        assert o + n <= self.cw
        return self.aps[j][r0:r1, o:o + n]


def _compact(rs):
    best = {}
    for s, v in rs:
        if best.get(s, 0) < v:
            best[s] = v
    return list(best.items())


def bf16_np(a):
    return np.asarray(a).astype(ml_dtypes.bfloat16)


def emit_wout_bf(k, w_out, b_in, stage=None):
    wb, wb_b = k.sbuf("wb", [128, 16, D], BF16)
    wst = [stage if stage is not None else k.sbuf("wost0", [128, D], F32)] * 2
    wst_s = [k.dsem()] * 2
    w_v = w_out.rearrange("(k p) n -> p k n", p=128)
    for kc in range(16):
        t, tb = wst[kc % 2]
        k.dma("sp", t[:], w_v[:, kc, :], tb, b_in, wst_s[kc % 2])
        k.op("dve", lambda e, t=t, kc=kc: e.tensor_copy(out=wb[:, kc, :], in_=t[:]), [tb], [wb_b])
    return wb, wb_b


def emit_gt_rows(k, cvec, ada_w, ada_b, b_in, pg, scratch=None):
    cT, cT_b = k.sbuf("g_cT", [128, 2, 8], F32)
    cs = k.dsem()
    k.dma("sp", cT[:], cvec.rearrange("r (p k) -> p r k", k=8), cT_b, b_in, cs)
    sT, sT_b = k.sbuf("g_sT", [128, 2, 8], F32)
    k.op("act", lambda e: e.activation(out=sT[:], in_=cT[:], func=AF.Silu), [cT_b], [sT_b])
    if scratch is None:
        srep_t, srep_b = k.sbuf("g_srep", [128, 2, 8, 128], F32)
        srep = srep_t[:]
        aw = [k.sbuf("g_aw%d" % i, [128, 8, 128], F32) for i in range(2)]
        aw = [(t_[:], b_) for t_, b_ in aw]
        brow_t, brow_b = k.sbuf("g_brow", [128, D], F32)
        brow = brow_t[:]
    else:
        (srep, srep_b), aw0, aw1, (brow, brow_b) = scratch
        aw = [aw0, aw1]
    k.op("dve", lambda e: e.tensor_copy(out=srep, in_=sT[:].unsqueeze(3).to_broadcast([128, 2, 8, 128])),
         [sT_b], [srep_b])
    aws = [k.dsem() for _ in range(2)]
    a_v = ada_w.rearrange("(p k) n -> p k n", k=8)
    bs = k.dsem()
    k.dma("sp", brow, ada_b.rearrange("(o n) -> o n", o=1).partition_broadcast(128), brow_b, b_in, bs)
    gt, gt_b = k.sbuf("g_gt", [128, 2, D], F32)
    for cc in range(8):
        at, at_b = aw[cc % 2]
        k.dma("sp", at, a_v[:, :, cc * 128:(cc + 1) * 128], at_b, b_in, aws[cc % 2])
        for r in range(2):
            p, pb = pg[r * 2 + cc // 4]
            o = (cc % 4) * 128
            for kc in range(8):
                k.op("pe", lambda e, p=p, r=r, kc=kc, at=at, o=o: e.matmul(
                    p[:, o:o + 128], lhsT=srep[:, r, kc, :], rhs=at[:, kc, :],
                    start=(kc == 0), stop=(kc == 7)), [srep_b, at_b], [pb], inc=(kc == 7))
    for r in range(2):
        for nb in range(2):
            p, pb = pg[r * 2 + nb]
            k.op("dve", lambda e, p=p, r=r, nb=nb: e.tensor_tensor(
                out=gt[:, r, nb * 512:(nb + 1) * 512], in0=p[:], in1=brow[:, nb * 512:(nb + 1) * 512],
                op=ALU.add), [pb, brow_b], [gt_b])
    return gt, gt_b


def build_op(n_lat_tiles, n_ctx_tok, nc=None, T=None, pre=""):
    fused = nc is not None
    if not fused:
        nc = bass.Bass("TRN2", target_bir_lowering=False)
    ntok = n_lat_tiles * 128 + n_ctx_tok
    dt = nc.dram_tensor

    def inp_(name, shape, dty):
        if fused:
            return T[name]
        return dt(name, shape, dty, kind="ExternalInput").ap()
    ygT = inp_("ygT", [E, ntok], BF16)
    if not fused:
        ygT = Chunked([ygT], ntok)
    xres = inp_("xres", [ntok, D], F32)
    w_out = inp_("w_out", [E, D], F32)
    cvec = inp_("cvec", [2, D], F32)
    ada_w = inp_("ada_w_gt", [D, D], F32)
    ada_b = inp_("ada_b_gt", [D], F32)
    xout = T["xout"] if fused else dt("xout", [ntok, D], F32, kind="ExternalOutput").ap()
    b_in = Buf("in")
    b_out = Buf("xout")
    with ExitStack() as es:
        k = KB(nc, es, pre)
        pg = [k.psum("pg%d" % i, [128, 512]) for i in range(4)]
        wb, wb_b = emit_wout_bf(k, w_out, b_in)
        gt, gt_b = emit_gt_rows(k, cvec, ada_w, ada_b, b_in, pg)
        tiles = [(128, 0)] * n_lat_tiles
        if n_ctx_tok:
            tiles.append((n_ctx_tok, 1))
        NB = 2
        yb = [k.sbuf("yb%d" % i, [128, 16, 128], BF16) for i in range(NB)]
        yb_s = [k.dsem() for _ in range(NB)]
        xb = [k.sbuf("xb%d" % i, [128, D], F32) for i in range(NB)]
        xb_s = [k.dsem() for _ in range(NB)]
        ob = [k.sbuf("ob%d" % i, [128, D], F32) for i in range(NB)]
        ob_s = [k.dsem() for _ in range(NB)]
        po = [k.psum("po%d" % i, [128, 512]) for i in range(4)]
        t0 = 0
        outs = []
        for ti, (m, which) in enumerate(tiles):
            s = ti % NB
            yt, yt_b = yb[s]
            xt, xt_b = xb[s]
            ot, ot_b = ob[s]
            k.dma("sp", yt[:, :, :m], ygT.cols(t0, m).rearrange("(k p) t -> p k t", p=128), yt_b, b_in, yb_s[s])
            k.dma("sp", xt[:m, :], xres[t0:t0 + m, :], xt_b, b_in, xb_s[s])
            for nb in range(2):
                p, pb = po[(ti % 2) * 2 + nb]
                for kc in range(16):
                    k.op("pe", lambda e, p=p, yt=yt, kc=kc, nb=nb, m=m: e.matmul(
                        p[:m, :], lhsT=yt[:, kc, :m], rhs=wb[:, kc, nb * 512:(nb + 1) * 512],
                        start=(kc == 0), stop=(kc == 15)), [yt_b, wb_b], [pb], inc=(kc == 15))
                sl = slice(nb * 512, (nb + 1) * 512)
                k.op("dve", lambda e, p=p, ot=ot, sl=sl, m=m, which=which: e.tensor_tensor(
                    out=ot[:m, sl], in0=p[:m, :], in1=gt[:m, which, sl], op=ALU.mult), [pb, gt_b], [ot_b])
                k.op("dve", lambda e, ot=ot, xt=xt, sl=sl, m=m: e.tensor_tensor(
                    out=ot[:m, sl], in0=ot[:m, sl], in1=xt[:m, sl], op=ALU.add), [ot_b, xt_b], [ot_b])
            ob_ = Buf("xout%d" % ti)
            outs.append(ob_)
            k.dma("act", xout[t0:t0 + m, :], ot[:m, :], ob_, ot_b, ob_s[s])
            t0 += m
        k.wait_all("act", outs)
        k.barrier_all()
        k.emit()
    return nc


def run_op(ygT_full, xres_list, w_out, c, c_ctx, ada_w_gt, ada_b_gt, with_ctx, ygT_ctx=None, xctx=None):
    n_ctx = 64 if with_ctx else 0
    nc = build_op(16, n_ctx)
    in_maps = []
    for core in range(NCORES):
        b, q = divmod(core, 4)
        yg = ygT_full[b][:, q * 2048:(q + 1) * 2048]
        xr = xres_list[b, q * 2048:(q + 1) * 2048]
        if with_ctx:
            yg = np.concatenate([yg, ygT_ctx[b][:, q * 64:(q + 1) * 64]], axis=1)
            xr = np.concatenate([xr, xctx[b, q * 64:(q + 1) * 64]], axis=0)
        in_maps.append({
            "ygT": np.ascontiguousarray(yg), "xres": np.ascontiguousarray(xr),
            "w_out": w_out, "cvec": np.ascontiguousarray(np.stack([c[b], c_ctx])),
            "ada_w_gt": ada_w_gt, "ada_b_gt": ada_b_gt})
    res = run_bass_kernel_spmd(nc, in_maps, core_ids=list(range(NCORES)))
    xo = np.empty((B, L, D), np.float32)
    xc = np.empty((B, CTX, D), np.float32) if with_ctx else None
    for core in range(NCORES):
        b, q = divmod(core, 4)
        r = res.results[core]["xout"]
        xo[b, q * 2048:(q + 1) * 2048] = r[:2048]
        if with_ctx:
            xc[b, q * 64:(q + 1) * 64] = r[2048:]
    return xo, xc


def row_to_cols(k, pst, pb, col0, row_t, row_b, n, one_t, one_b, off=0):
    for j in range(n):
        k.op("pe", lambda e, j=j: e.matmul(pst[:, col0 + j:col0 + j + 1],
                                            lhsT=row_t[0:1, off + j * 128:off + (j + 1) * 128],
                                            rhs=one_t[0:1, 0:1], start=True, stop=True),
             [row_b, one_b], [pb])


def vec_to_cols(k, vec_ap, n, m, identf, identf_b, pst, pb, col0, b_in, name):
    vt, vt_b = k.sbuf(name, [n, m], F32)
    s = k.dsem()
    k.dma("sp", vt[:], vec_ap.rearrange("(j p) -> j p", p=m), vt_b, b_in, s)
    k.op("pe", lambda e: e.matmul(pst[0:m, col0:col0 + n], lhsT=vt[0:n, 0:m], rhs=identf[0:n, 0:n], start=True, stop=True),
         [vt_b, identf_b], [pb])


def emit_mod(k, cvec, ada_w, ada_b, norm_g, identf, identf_b, b_in, pst, pb, pst2, pb2):
    cT, cT_b = k.sbuf("m_cT", [128, 2, 8], F32)
    s0 = k.dsem()
    k.dma("sp", cT[:], cvec.rearrange("r (p k) -> p r k", k=8), cT_b, b_in, s0)
    sT, sT_b = k.sbuf("m_sT", [128, 8, 2], F32)
    k.op("act", lambda e: e.activation(out=sT[:].rearrange("p k r -> p r k"), in_=cT[:], func=AF.Silu), [cT_b], [sT_b])
    awc = [k.sbuf("m_aw0", [128, 8, 128], F32)] * 2
    aws = [k.dsem()] * 2
    a_v = ada_w.rearrange("(p k) n -> p k n", k=8)
    for j in range(16):
        t, tb = awc[j % 2]
        k.dma("sp", t[:], a_v[:, :, j * 128:(j + 1) * 128], tb, b_in, aws[j % 2])
        for kc in range(8):
            k.op("pe", lambda e, t=t, kc=kc, j=j: e.matmul(pst[:, 2 * j:2 * j + 2], lhsT=t[:, kc, :], rhs=sT[:, kc, :],
                                                             start=(kc == 0), stop=(kc == 7)), [tb, sT_b], [pb], inc=(kc == 7))
    vec_to_cols(k, ada_b, 16, 128, identf, identf_b, pst2, pb2, 0, b_in, "m_bv")
    bg, bg_b = k.sbuf("m_bg", [128, 24], F32)
    k.op("dve", lambda e: e.tensor_copy(out=bg[:, 0:16], in_=pst2[:, 0:16]), [pb2], [bg_b])
    vec_to_cols(k, norm_g, 8, 128, identf, identf_b, pst2, pb2, 16, b_in, "m_gv")
    k.op("dve", lambda e: e.tensor_copy(out=bg[:, 16:24], in_=pst2[:, 16:24]), [pb2], [bg_b])
    mod, mod_b = k.sbuf("m_mod", [128, 16, 2], F32)
    k.op("dve", lambda e: e.tensor_tensor(out=mod[:], in0=pst[:, 0:32].rearrange("p (j r) -> p j r", r=2),
                                           in1=bg[:, 0:16].unsqueeze(2).to_broadcast([128, 16, 2]), op=ALU.add), [pb, bg_b], [mod_b])
    gm, gm_b = k.sbuf("m_gm", [128, 8, 2], F32)
    sh, sh_b = k.sbuf("m_sh", [128, 8, 2], F32)
    k.op("dve", lambda e: e.tensor_copy(out=sh[:], in_=mod[:, 0:8, :]), [mod_b], [sh_b])
    k.op("dve", lambda e: e.scalar_tensor_tensor(
        out=gm[:], in0=mod[:, 8:16, :], scalar=1.0, in1=bg[:, 16:24].unsqueeze(2).to_broadcast([128, 8, 2]),
        op0=ALU.add, op1=ALU.mult), [mod_b, bg_b], [gm_b])
    return (gm, gm_b), (sh, sh_b)


class HT:
    def __init__(self, k, ident_t, ident_b, gm, sh, alloc_x=True):
        self.k = k
        self.ident = (ident_t, ident_b)
        self.gm, self.sh = gm, sh
        self.xt = [k.sbuf("h_xt0", [128, D], F32)] * 2 if alloc_x else None
        self.xs = [k.dsem()] * 2 if alloc_x else None
        self.ss = [k.sbuf("h_ss%d" % i, [128, 2], F32) for i in range(2)]
        self.xh = [k.sbuf("h_xh%d" % i, [128, D], BF16) for i in range(2)]
        self.tmp = [k.sbuf("h_tmp0", [128, 8, 128], F32)] * 2
        self.n = 0

    def tile(self, x_ap, b_in, r, pst_bf, pb, dst_ap, dst_b, src=None):
        k = self.k
        i = self.n % 2
        self.n += 1
        xt, xt_b = self.xt[i] if src is None else src
        ss, ss_b = self.ss[i]
        xh, xh_b = self.xh[i]
        tmp, tmp_b = self.tmp[i]
        junk, junk_b = xh, xh_b
        (gm, gm_b), (sh, sh_b) = self.gm, self.sh
        if src is None:
            k.dma("sp", xt[:], x_ap, xt_b, b_in, self.xs[i])
        k.op("act", lambda e: e.activation(out=junk[:], in_=xt[:], func=AF.Square, accum_out=ss[:, 0:1]),
             [xt_b], [junk_b, ss_b])
        k.op("act", lambda e: e.activation(out=ss[:, 1:2], in_=ss[:, 0:1], func=AF.Sqrt, scale=1.0 / D, bias=EPS),
             [ss_b], [ss_b])
        k.op("dve", lambda e: e.reciprocal(out=ss[:, 1:2], in_=ss[:, 1:2]), [ss_b], [ss_b])
        k.op("act", lambda e: e.activation(out=xh[:], in_=xt[:], func=AF.Identity, scale=ss[:, 1:2]), [xt_b, ss_b], [xh_b])
        for kc in range(8):
            k.op("pe", lambda e, kc=kc: e.transpose(pst_bf[:, kc, :], xh[:, kc * 128:(kc + 1) * 128], self.ident[0][:]),
                 [xh_b, self.ident[1]], [pb], inc=(kc == 7))
        k.op("dve", lambda e: e.tensor_tensor(out=tmp[:], in0=pst_bf, in1=gm[:, :, r:r + 1].to_broadcast([128, 8, 128]),
                                               op=ALU.mult), [pb, gm_b], [tmp_b])
        k.op("dve", lambda e: e.tensor_tensor(out=dst_ap, in0=tmp[:], in1=sh[:, :, r:r + 1].to_broadcast([128, 8, 128]),
                                               op=ALU.add), [tmp_b, sh_b], [dst_b])


NKT = (L + CTX) // 128


class _Stop(Exception):
    pass


def build_att(stop=0, nc=None, T=None, pre=""):
    fused = nc is not None
    if not fused:
        nc = bass.Bass("TRN2", target_bir_lowering=False)
    dt = nc.dram_tensor

    def inp_(name, shape, dty):
        if fused:
            return T[name]
        return dt(name, shape, dty, kind="ExternalInput").ap()
    x1 = inp_("x1" if not fused else "x", [L, D], F32)
    xc = inp_("xc", [CTX, D], F32)
    cvec = inp_("cvec", [2, D], F32)
    ada_w = inp_("ada_w", [D, 2048], F32)
    ada_b = inp_("ada_b", [2048], F32)
    norm_g = inp_("norm_g", [D], F32)
    w = inp_("w", [D, 1280], F32)
    qkg = inp_("qkg", [256], F32)
    cosT = inp_("cosT", [128, L], F32)
    sinT = inp_("sinT", [128, L], F32)
    consts = inp_("consts", [128, 4, 128], BF16)
    identf_d = inp_("identf", [64, 64], F32)
    if fused:
        ogT = T["og_loc"]
    else:
        ogT = Chunked([dt("ogT", [512, L], BF16, kind="ExternalOutput").ap()], L)
    dbgf = dt("dbgf", [128, 2048], F32, kind="ExternalOutput").ap() if stop else None
    dbgh = dt("dbgh", [128, 8192], BF16, kind="ExternalOutput").ap() if stop else None
    b_in = Buf("in")
    with ExitStack() as es:
        k = KB(nc, es, pre)
        dbg_outs = []
        dsm = k.dsem()

        def dump(ap_out, ap_in, src_b):
            ob = Buf("dbg")
            dbg_outs.append(ob)
            k.dma("sp", ap_out, ap_in, ob, src_b, dsm)

        def finish():
            k.wait_all("sp", dbg_outs)
            k.barrier_all()
            k.emit()
        q_scr, _ = k.dram("q_scr", [512, L], BF16)
        g_scr, _ = k.dram("g_scr", [512, L], BF16)
        q_rb = {(cc, bl): Buf("qs") for cc in range(4) for bl in range(16)}
        g_rb = {(cc, bl): Buf("gs") for cc in range(4) for bl in range(16)}
        P = [k.psum("P%d" % i, [128, 512]) for i in range(8)]
        cst, cst_b = k.sbuf("cst", [128, 4, 128], BF16)
        s_c = k.dsem()
        k.dma("sp", cst[:], consts, cst_b, b_in, k.dsem())
        ident, permT, onesm, ones1 = (cst[:, i, :] for i in range(4))
        identf, identf_b = k.sbuf("identf_sb", [64, 64], F32)
        k.dma("sp", identf[:], identf_d, identf_b, b_in, k.dsem())
        gm, sh = emit_mod(k, cvec, ada_w, ada_b, norm_g, identf, identf_b, b_in, P[0][0], P[0][1], P[1][0], P[1][1])
        vec_to_cols(k, qkg, 2, 128, identf, identf_b, P[2][0], P[2][1], 0, b_in, "qkv")
        gcol, gcol_b = k.sbuf("gcol", [128, 2], F32)
        k.op("dve", lambda e: e.tensor_copy(out=gcol[:], in_=P[2][0][:, 0:2]), [P[2][1]], [gcol_b])
        if stop == 1:
            dump(dbgf[:, 0:16], gm[0][:].rearrange("p k r -> p (k r)"), gm[1])
            dump(dbgf[:, 16:32], sh[0][:].rearrange("p k r -> p (k r)"), sh[1])
            dump(dbgf[:, 32:34], gcol[:], gcol_b)
            finish()
            return nc
        hTb = [k.sbuf("hTb%d" % i, [128, 8, 512], BF16) for i in range(2)]
        wst = [k.sbuf("wst0", [128, 1280], F32)] * 2
        if fused:
            w0b, w0b_b = emit_wout_bf(k, T["w_out0"], b_in, stage=(wst[0][0][:, 0:D], wst[0][1]))
            h0f = hTb[0][0][:].rearrange("p a b -> p (a b)").bitcast(F32)
            h1f = hTb[1][0][:].rearrange("p a b -> p (a b)").bitcast(F32)
            scratch = ((h0f.rearrange("p (r k m) -> p r k m", r=2, k=8), hTb[0][1]),
                       (h1f[:, 0:1024].rearrange("p (k m) -> p k m", k=8), hTb[1][1]),
                       (h1f[:, 1024:2048].rearrange("p (k m) -> p k m", k=8), hTb[1][1]),
                       (wst[0][0][:, 0:D], wst[0][1]))
            gt0, gt0_b = emit_gt_rows(k, T["cvec0"], T["ada_w_gt0"], T["ada_b_gt0"], b_in, [P[2], P[3], P[4], P[5]], scratch=scratch)
            ygb, ygb_b = k.sbuf("ygb", [128, 16, 512], BF16)
            ygs = k.dsem()
            xin = [k.sbuf("xin0", [128, D], F32)] * 2
            xins = [k.dsem()] * 2
            x1t = [k.sbuf("x1t%d" % i, [128, D], F32) for i in range(2)]
            x1s = [k.dsem() for _ in range(2)]
            yga = T["yg_all"]
            x1_tiles = {}
        wbf, wbf_b = k.sbuf("wbf", [128, 8, 1280], BF16)
        wss = [k.dsem()] * 2
        w_v = w.rearrange("(k p) n -> p k n", p=128)
        for kc in range(8):
            t, tb = wst[kc % 2]
            k.dma("sp", t[:], w_v[:, kc, :], tb, b_in, wss[kc % 2])
            k.op("dve", lambda e, t=t, kc=kc: e.tensor_copy(out=wbf[:, kc, :], in_=t[:]), [tb], [wbf_b])
        KT, KT_b = k.sbuf("KT", [128, L + CTX], BF16)
        V, V_b = k.sbuf("V", [128, NKT, 128], BF16)
        ht = HT(k, ident, cst_b, gm, sh, alloc_x=not fused)
        ntile = 0
        sqb = [k.sbuf("sqb%d" % i, [128, 512], BF16) for i in range(2)]
        qgb = [k.sbuf("qgb%d" % i, [128, 512], BF16) for i in range(2)]
        rst = [k.sbuf("rst%d" % i, [128, 512], F32) for i in range(2)]
        t1 = [k.sbuf("t1_%d" % i, [128, 512], F32) for i in range(2)]
        t2 = [k.sbuf("t2_%d" % i, [128, 512], F32) for i in range(2)]
        qo = [k.sbuf("qo%d" % i, [128, 512], BF16) for i in range(2)]
        qos = [k.dsem() for _ in range(2)]
        cs = [k.sbuf("cs0", [128, 2, 512], F32)] * 2
        css = [k.dsem()] * 2
        gto = [k.sbuf("gto%d" % i, [128, 512], BF16) for i in range(2)]
        gtos = [k.dsem() for _ in range(2)]
        nq = 0
        ngt = 0
        for blk in range(17):
            lat = blk < 16
            ntok = 512 if lat else 256
            tok0 = blk * 512
            hb, hb_b = hTb[blk % 2]
            if fused:
                yg_src = yga.cols(tok0, ntok) if lat else yga.cols(L, CTX)
                k.dma("sp", ygb[:, :, 0:ntok], yg_src.rearrange("(k p) t -> p k t", p=128), ygb_b, b_in, ygs)
            for t in range(ntok // 128):
                xa = x1[tok0 + t * 128: tok0 + (t + 1) * 128, :] if lat else xc[t * 128:(t + 1) * 128, :]
                pt, pb = P[t % 2]
                src = None
                if fused:
                    j = ntile % 2
                    ntile += 1
                    xi, xi_b = xin[j]
                    xo_, xo_b = x1t[j]
                    k.dma("sp", xi[:], xa, xi_b, b_in, xins[j])
                    for nb in range(2):
                        pp, ppb = P[6 + nb]
                        for kc in range(16):
                            k.op("pe", lambda e, pp=pp, kc=kc, nb=nb, t=t: e.matmul(
                                pp[:], lhsT=ygb[:, kc, t * 128:(t + 1) * 128], rhs=w0b[:, kc, nb * 512:(nb + 1) * 512],
                                start=(kc == 0), stop=(kc == 15)), [ygb_b, w0b_b], [ppb], inc=(kc == 15))
                        sl = slice(nb * 512, (nb + 1) * 512)
                        r_ = 0 if lat else 1
                        k.op("dve", lambda e, pp=pp, xo_=xo_, sl=sl, r_=r_: e.tensor_tensor(
                            out=xo_[:, sl], in0=pp[:], in1=gt0[:, r_, sl], op=ALU.mult), [ppb, gt0_b], [xo_b])
                        k.op("dve", lambda e, xo_=xo_, xi=xi, sl=sl: e.tensor_tensor(
                            out=xo_[:, sl], in0=xo_[:, sl], in1=xi[:, sl], op=ALU.add), [xo_b, xi_b], [xo_b])
                    if lat:
                        xb_ = Buf("x1f")
                        x1_tiles[tok0 // 128 + t] = xb_
                        k.dma("act", T["x1_full"][tok0 + t * 128: tok0 + (t + 1) * 128, :], xo_[:], xb_, xo_b, x1s[j])
                    src = (xo_, xo_b)
                ht.tile(xa, b_in, 0 if lat else 1, pt[:].bitcast(BF16)[:, 0:1024].rearrange("p (k t) -> p k t", k=8), pb,
                        hb[:, :, t * 128:(t + 1) * 128], hb_b, src=src)
            if stop == 2:
                dump(dbgh[:, 0:4096], hb[:].rearrange("p k t -> p (k t)"), hb_b)
                finish()
                return nc
            if lat:
                ct, ct_b = cs[blk % 2]
                k.dma("sp", ct[:, 0, :], cosT[:, tok0:tok0 + 512], ct_b, b_in, css[blk % 2])
                k.dma("sp", ct[:, 1, :], sinT[:, tok0:tok0 + 512], ct_b, b_in, css[blk % 2])
            for cc in ([0, 1, 2, 3, 4] if lat else [4]):
                i = nq % 2
                nq += 1
                ps, psb = P[2 + i]
                for kc in range(8):
                    k.op("pe", lambda e, ps=ps, kc=kc, cc=cc, hb=hb, ntok=ntok: e.matmul(
                        ps[:, :ntok], lhsT=wbf[:, kc, cc * 128:(cc + 1) * 128], rhs=hb[:, kc, :ntok],
                        start=(kc == 0), stop=(kc == 7)), [wbf_b, hb_b], [psb], inc=(kc == 7))
                sq, sq_b = sqb[i]
                qg_, qg_b = qgb[i]
                r_, r_b = rst[i]
                gi = 0 if cc < 4 else 1
                k.op("act", lambda e, sq=sq, ps=ps, ntok=ntok: e.activation(out=sq[:, :ntok], in_=ps[:, :ntok], func=AF.Square),
                     [psb], [sq_b])
                k.op("act", lambda e, qg_=qg_, ps=ps, ntok=ntok, gi=gi: e.activation(
                    out=qg_[:, :ntok], in_=ps[:, :ntok], func=AF.Identity, scale=gcol[:, gi:gi + 1]), [psb, gcol_b], [qg_b])
                pm, pmb = P[4 + i]
                k.op("pe", lambda e, pm=pm, sq=sq, ntok=ntok: e.matmul(pm[:, :ntok], lhsT=onesm, rhs=sq[:, :ntok], start=True, stop=True),
                     [cst_b, sq_b], [pmb])
                k.op("act", lambda e, r_=r_, pm=pm, ntok=ntok: e.activation(out=r_[:, :ntok], in_=pm[:, :ntok], func=AF.Sqrt, bias=EPS),
                     [pmb], [r_b])
                k.op("dve", lambda e, r_=r_, ntok=ntok: e.reciprocal(out=r_[:, :ntok], in_=r_[:, :ntok]), [r_b], [r_b])
                if lat:
                    pr, prb = P[6 + i]
                    k.op("pe", lambda e, pr=pr, qg_=qg_: e.matmul(pr[:], lhsT=permT, rhs=qg_[:], start=True, stop=True),
                         [cst_b, qg_b], [prb])
                    a1, a1_b = t1[i]
                    a2, a2_b = t2[i]
                    k.op("dve", lambda e, a1=a1, qg_=qg_, ct=ct: e.tensor_tensor(out=a1[:], in0=qg_[:], in1=ct[:, 0, :], op=ALU.mult),
                         [qg_b, ct_b], [a1_b])
                    k.op("dve", lambda e, a2=a2, pr=pr, ct=ct: e.tensor_tensor(out=a2[:], in0=pr[:], in1=ct[:, 1, :], op=ALU.mult),
                         [prb, ct_b], [a2_b])
                    k.op("dve", lambda e, a1=a1, a2=a2: e.tensor_tensor(out=a1[:], in0=a1[:], in1=a2[:], op=ALU.add),
                         [a1_b, a2_b], [a1_b])
                    if cc < 4:
                        o_, o_b = qo[i]
                        k.op("dve", lambda e, o_=o_, a1=a1, r_=r_: e.tensor_tensor(out=o_[:], in0=a1[:], in1=r_[:], op=ALU.mult),
                             [a1_b, r_b], [o_b])
                        k.dma("act", q_scr[cc * 128:(cc + 1) * 128, tok0:tok0 + 512], o_[:], q_rb[(cc, blk)], o_b, qos[i])
                    else:
                        k.op("dve", lambda e, a1=a1, r_=r_, tok0=tok0: e.tensor_tensor(
                            out=KT[:, tok0:tok0 + 512], in0=a1[:], in1=r_[:], op=ALU.mult), [a1_b, r_b], [KT_b])
                else:
                    k.op("dve", lambda e, qg_=qg_, r_=r_: e.tensor_tensor(
                        out=KT[:, L:L + 256], in0=qg_[:, :256], in1=r_[:, :256], op=ALU.mult), [qg_b, r_b], [KT_b])
            if lat:
                for cc in range(4):
                    i = ngt % 2
                    ngt += 1
                    ps, psb = P[2 + i]
                    for kc in range(8):
                        k.op("pe", lambda e, ps=ps, kc=kc, cc=cc, hb=hb: e.matmul(
                            ps[:], lhsT=wbf[:, kc, 768 + cc * 128:768 + (cc + 1) * 128], rhs=hb[:, kc, :],
                            start=(kc == 0), stop=(kc == 7)), [wbf_b, hb_b], [psb], inc=(kc == 7))
                    g_, g_b = gto[i]
                    k.op("act", lambda e, g_=g_, ps=ps: e.activation(out=g_[:], in_=ps[:], func=AF.Silu), [psb], [g_b])
                    k.dma("act", g_scr[cc * 128:(cc + 1) * 128, tok0:tok0 + 512], g_[:], g_rb[(cc, blk)], g_b, gtos[i])
            for t in range(ntok // 128):
                ps, psb = P[6 + t % 2]
                for kc in range(8):
                    k.op("pe", lambda e, ps=ps, kc=kc, hb=hb, t=t: e.matmul(
                        ps[:, 0:128], lhsT=hb[:, kc, t * 128:(t + 1) * 128], rhs=wbf[:, kc, 640:768],
                        start=(kc == 0), stop=(kc == 7)), [wbf_b, hb_b], [psb], inc=(kc == 7))
                kt = (tok0 // 128 + t) if lat else (L // 128 + t)
                k.op("dve", lambda e, ps=ps, kt=kt: e.tensor_copy(out=V[:, kt, :], in_=ps[:, 0:128]), [psb], [V_b])
            if stop == 3 or (stop == 4 and blk == 16):
                dump(dbgh[:, 0:512], KT[:, 0:512], KT_b)
                dump(dbgh[:, 512:1024], V[:, 0:4, :].rearrange("p a b -> p (a b)"), V_b)
                dump(dbgh[:, 1024:1280], KT[:, L:L + 256], KT_b)
                dump(dbgh[:, 1280:1536], V[:, 64:66, :].rearrange("p a b -> p (a b)"), V_b)
                dump(dbgh[:, 2048:2560], q_scr[0:128, 0:512], q_rb[(0, 0)])
                dump(dbgh[:, 2560:3072], g_scr[0:128, 0:512], g_rb[(0, 0)])
                finish()
                return nc
        qbk = [k.sbuf("qbk%d" % i, [128, 512], BF16) for i in range(2)]
        qbs = [k.dsem() for _ in range(2)]
        gbk = [k.sbuf("gbk%d" % i, [128, 512], BF16) for i in range(2)]
        gbs = [k.dsem() for _ in range(2)]
        pT = [k.sbuf("pT%d" % i, [128, 512], BF16) for i in range(3)] + [sqb[0], sqb[1], qgb[0]]
        SB = [P[0], P[1], P[2], P[7]]
        rd = [k.sbuf("rd%d" % i, [128, 512], F32) for i in range(2)]
        o1 = [k.sbuf("o1_%d" % i, [128, 512], F32) for i in range(2)]
        o2 = [k.sbuf("o2_%d" % i, [128, 512], BF16) for i in range(2)]
        o2s = [k.dsem() for _ in range(2)]
        outs = []
        scale = 1.0 / math.sqrt(128.0)
        iters = [(h, qb) for h in range(4) for qb in range(16)]
        if stop >= 10:
            iters = iters[:stop - 10]

        def load_qg(j):
            h_, qb_ = iters[j]
            q__, q__b = qbk[j % 2]
            gk_, gk__b = gbk[j % 2]
            k.dma("sp", q__[:], q_scr[h_ * 128:(h_ + 1) * 128, qb_ * 512:(qb_ + 1) * 512], q__b, q_rb[(h_, qb_)], qbs[j % 2])
            k.dma("sp", gk_[:], g_scr[h_ * 128:(h_ + 1) * 128, qb_ * 512:(qb_ + 1) * 512], gk__b, g_rb[(h_, qb_)], gbs[j % 2])
        load_qg(0)
        if True:
            for it, (h, qb) in enumerate(iters):
                i = it % 2
                q_, q_b = qbk[i]
                gk, gk_b = gbk[i]
                if it + 1 < len(iters):
                    load_qg(it + 1)
                O, O_b = P[3 + i]
                Dn, Dn_b = P[5 + i]
                ac, ac_b = t1[i]

                def emit_s(kt):
                    S, S_b = SB[kt % 4]
                    k.op("pe", lambda e, S=S, kt=kt, q_=q_: e.matmul(S[:], lhsT=KT[:, kt * 128:(kt + 1) * 128], rhs=q_[:],
                                                                      start=True, stop=True), [KT_b, q_b], [S_b])
                emit_s(0)
                emit_s(1)
                for kt in range(NKT):
                    if kt + 2 < NKT:
                        emit_s(kt + 2)
                    S, S_b = SB[kt % 4]
                    p_, p_b = pT[kt % 6]
                    k.op("act", lambda e, p_=p_, S=S: e.activation(out=p_[:], in_=S[:], func=AF.Exp, scale=scale), [S_b], [p_b])
                    on_pe = (kt % 4 == 0)
                    k.op("pe", lambda e, p_=p_, kt=kt, O=O: e.matmul(O[:], lhsT=V[:, kt, :], rhs=p_[:], start=(kt == 0),
                                                                      stop=(kt == NKT - 1)), [V_b, p_b], [O_b], inc=not on_pe)
                    if on_pe:
                        k.op("pe", lambda e, p_=p_, kt=kt, Dn=Dn: e.matmul(Dn[:], lhsT=ones1, rhs=p_[:], start=(kt == 0),
                                                                            stop=False), [cst_b, p_b], [Dn_b])
                    elif kt == 1:
                        k.op("dve", lambda e, p_=p_, ac=ac: e.tensor_copy(out=ac[:], in_=p_[:]), [p_b], [ac_b])
                    else:
                        k.op("dve", lambda e, p_=p_, ac=ac: e.tensor_tensor(out=ac[:], in0=ac[:], in1=p_[:], op=ALU.add),
                             [ac_b, p_b], [ac_b])
                acb_, acb_b = qo[i]
                k.op("dve", lambda e, ac=ac, acb_=acb_: e.tensor_copy(out=acb_[:], in_=ac[:]), [ac_b], [acb_b])
                k.op("pe", lambda e, acb_=acb_, Dn=Dn: e.matmul(Dn[:], lhsT=ones1, rhs=acb_[:], start=False, stop=True),
                     [cst_b, acb_b], [Dn_b])
                r_, r_b = rd[i]
                a_, a_b = o1[i]
                b_, b_b = o2[i]
                k.op("dve", lambda e, r_=r_, Dn=Dn: e.reciprocal(out=r_[:], in_=Dn[:]), [Dn_b], [r_b])
                k.op("dve", lambda e, a_=a_, r_=r_, O=O: e.tensor_tensor(out=a_[:], in0=O[:], in1=r_[:], op=ALU.mult), [O_b, r_b], [a_b])
                k.op("dve", lambda e, a_=a_, b_=b_, gk=gk: e.tensor_tensor(out=b_[:], in0=a_[:], in1=gk[:], op=ALU.mult),
                     [a_b, gk_b], [b_b])
                ob = Buf("o%d" % it)
                outs.append(ob)
                o_ap = ogT.rows_cols(h * 128, (h + 1) * 128, qb * 512, 512)
                k.dma("sp", o_ap, b_[:], ob, b_b, o2s[i])
        k.wait_all("sp", outs)
        if fused:
            k.wait_all("sp", list(x1_tiles.values()))
        if stop:
            finish()
        else:
            k.barrier_all()
            k.emit()
    return nc


def rope_tables():
    rows = L // 64
    row = np.repeat(np.arange(rows, dtype=np.float32), 64)
    col = np.tile(np.arange(64, dtype=np.float32), rows)
    inv = (1.0 / (10000.0 ** (np.arange(0, 64, 2, dtype=np.float32) / 64))).astype(np.float32)
    row_ang = row[:, None] * inv[None, :]
    col_ang = col[:, None] * inv[None, :]
    cosT = np.empty((128, L), np.float32)
    sinT = np.empty((128, L), np.float32)
    for m in range(128):
        ang = row_ang if m < 64 else col_ang
        j = m % 32
        first = (m % 64) < 32
        cosT[m] = np.cos(ang[:, j])
        sinT[m] = (-1.0 if first else 1.0) * np.sin(ang[:, j])
    return cosT, sinT


def att_consts():
    c = np.zeros((128, 4, 128), np.float32)
    c[:, 0, :] = np.eye(128)
    for m in range(128):
        partner = m + 32 if (m % 64) < 32 else m - 32
        c[partner, 1, m] = 1.0
    c[:, 2, :] = 1.0 / 128.0
    c[:, 3, :] = 1.0
    return bf16_np(c)


def run_att(x1, x1c, c, c_ctx, ada_w1, ada_b1, norm_g1, w_in, q_g, k_g, stop=0):
    nc = build_att(stop)
    cosT, sinT = rope_tables()
    consts = att_consts()
    in_maps = []
    QD, KVD = 2048, 512
    for core in range(NCORES):
        b, g = divmod(core, 4)
        cols = np.concatenate([np.arange(512 * g, 512 * g + 512), QD + np.arange(128 * g, 128 * g + 128),
                               QD + KVD + np.arange(128 * g, 128 * g + 128), QD + 2 * KVD + np.arange(512 * g, 512 * g + 512)])
        in_maps.append({
            "x1": np.ascontiguousarray(x1[b]), "xc": np.ascontiguousarray(x1c[b]),
            "cvec": np.ascontiguousarray(np.stack([c[b], c_ctx])),
            "ada_w": np.ascontiguousarray(ada_w1[:, :2048]), "ada_b": np.ascontiguousarray(ada_b1[:2048]),
            "norm_g": norm_g1, "w": np.ascontiguousarray(w_in[:, cols]),
            "qkg": np.ascontiguousarray(np.concatenate([q_g, k_g])), "cosT": cosT, "sinT": sinT, "consts": consts,
            "identf": np.eye(64, dtype=np.float32)})
    res = run_bass_kernel_spmd(nc, in_maps, core_ids=list(range(NCORES)))
    if stop:
        return res.results
    ogT = [np.concatenate([res.results[b * 4 + g]["ogT"] for g in range(4)], axis=0) for b in range(B)]
    return ogT


MAX_DECAY = math.log(1e-2) / 0.3
MIN_DECAY = math.log(1e-2) / 1.5
RLIST = [127] + list(range(64))
TWO_PI = 2.0 * math.pi


def hy_consts():
    f32 = np.float32
    n2 = np.arange(128)[:, None].astype(np.float64)
    k2 = np.arange(256)[None, :].astype(np.float64)
    ang = TWO_PI * n2 * k2 / 256.0
    F256 = np.concatenate([np.cos(ang), -np.sin(ang)], axis=1)
    n1 = np.arange(128)[:, None].astype(np.float64)
    k1 = np.arange(128)[None, :].astype(np.float64)
    a1 = TWO_PI * n1 * k1 / 128.0
    CS = np.stack([np.cos(a1), np.sin(a1), -np.sin(a1)], axis=1)
    r = np.array(RLIST)[None, :].astype(np.float64)
    ar = TWO_PI * np.arange(128)[:, None] * r / 128.0
    RI = np.stack([np.concatenate([np.cos(ar), np.sin(ar)], 1), np.concatenate([-np.sin(ar), np.cos(ar)], 1)], axis=1)
    CSs = np.zeros((128, 2, 2, 2, 128))
    for jj in range(2):
        for hh in range(2):
            a = TWO_PI * (jj * 128 + np.arange(128)[:, None]) * (hh * 128 + np.arange(128)[None, :]) / 256.0
            CSs[:, jj, hh, 0, :] = np.cos(a)
            CSs[:, jj, hh, 1, :] = -np.sin(a)
    ident = np.eye(128)
    mats = np.concatenate([F256, CS.reshape(128, -1), RI.reshape(128, -1), CSs.reshape(128, -1), ident], axis=1)
    return bf16_np(mats.astype(f32))


def hy_ztab(Lf):
    f32 = np.float32
    t = np.linspace(0.0, 1.0, Lf, dtype=f32)[:, None]
    w = (TWO_PI * np.arange(Lf, dtype=f32)[:, None] / Lf).astype(f32)
    fr = np.linspace(1e-4, 15, 16, dtype=f32)[None, :]
    z = np.concatenate([t, np.cos(fr * w), -np.sin(fr * w)], axis=-1).astype(f32)
    pos = np.concatenate([np.arange(Lf), [0], Lf - np.arange(Lf + 1, 2 * Lf)])
    zext = np.ascontiguousarray(z[pos].T)
    text = t[pos, 0].copy()
    text[Lf] = 1e4
    return zext, text.astype(f32)


def build_hy(stop=0, nc=None, T=None, pre=""):
    fused = nc is not None
    if not fused:
        nc = bass.Bass("TRN2", target_bir_lowering=False)
    dt = nc.dram_tensor

    def inp_(name, shape, dty):
        if fused:
            return T[name]
        return dt(name, shape, dty, kind="ExternalInput").ap()
    x = inp_("x", [L, D], F32)
    xc = inp_("xc", [CTX, D], F32)
    cvec = inp_("cvec", [2, D], F32)
    ada_w = inp_("ada_w", [D, 2048], F32)
    ada_b = inp_("ada_b", [2048], F32)
    norm_g = inp_("norm_g", [D], F32)
    w = inp_("w", [D, 2048], F32)
    vecs = inp_("vecs", [13 * 512], F32)
    fvec = inp_("fvec", [4 * 64], F32)
    fw1 = inp_("fw1", [33, 64], F32)
    fw23 = inp_("fw23", [64, 2, 64], F32)
    fw4 = inp_("fw4", [64, 2, 512], F32)
    zext = inp_("zext", [33, 2 * L], F32)
    zextc = inp_("zextc", [33, 2 * CTX], F32)
    negt = inp_("negt", [128, 128], F32)
    textc = inp_("textc", [128, 2 * CTX], F32)
    drow = inp_("drow", [128, 512], F32)
    ndcol = inp_("ndcol", [128, 4], F32)
    mats = inp_("mats", [128, 2308], BF16)
    identf_d = inp_("identf", [64, 64], F32)
    if fused:
        ygT, ygTc = T["ygT"], T["ygTc"]
    else:
        ygT = Chunked([dt("ygT", [512, L], BF16, kind="ExternalOutput").ap()], L)
        ygTc = dt("ygTc", [512, CTX], BF16, kind="ExternalOutput").ap()
    dbgf = dt("dbgf", [128, 4096], F32, kind="ExternalOutput").ap() if stop else None
    dbgh = dt("dbgh", [128, 16384], BF16, kind="ExternalOutput").ap() if stop else None
    b_in = Buf("in")
    with ExitStack() as es:
        k = KB(nc, es, pre)
        outs = []
        dsm = k.dsem()

        def dump(ap_out, ap_in, src_b):
            ob = Buf("dbg")
            outs.append(ob)
            k.dma("sp", ap_out, ap_in, ob, src_b, dsm)

        def finish():
            k.wait_all("sp", outs)
            k.barrier_all()
            k.emit()
        NT = (L + CTX) // 128
        hT_scr, hT_scr_b = k.dram("hT_scr", [128, 8, L + CTX], BF16)
        P = [k.psum("P%d" % i, [128, 512]) for i in range(8)]
        M, M_b = k.sbuf("mats_sb", [128, 2308], BF16)
        s_c = k.dsem()
        k.dma("sp", M[:], mats, M_b, b_in, k.dsem())
        F256 = M[:, 0:512]
        CS = M[:, 512:896].rearrange("p (a b) -> p a b", a=3)
        RI = M[:, 896:1156].rearrange("p (a b) -> p a b", a=2)
        CSs = M[:, 1156:2180].rearrange("p (j h t s) -> p j h t s", j=2, h=2, t=2)
        ident = M[:, 2180:2308]
        identf, identf_b = k.sbuf("identf_sb", [64, 64], F32)
        k.dma("sp", identf[:], identf_d, identf_b, b_in, k.dsem())
        gm, sh = emit_mod(k, cvec, ada_w, ada_b, norm_g, identf, identf_b, b_in, P[0][0], P[0][1], P[1][0], P[1][1])
        one, one_b = k.sbuf("one1", [1, 4], F32)
        k.op("dve", lambda e: e.memset(one[:], 1.0), [], [one_b])
        vec_to_cols(k, vecs, 52, 128, identf, identf_b, P[2][0], P[2][1], 0, b_in, "vv")
        colv, colv_b = k.sbuf("colv", [128, 52], F32)
        k.op("dve", lambda e: e.tensor_copy(out=colv[:], in_=P[2][0][:, 0:52]), [P[2][1]], [colv_b])

        def cv(v, sg):
            return colv[:, v * 4 + sg:v * 4 + sg + 1]
        vec_to_cols(k, fvec, 4, 64, identf, identf_b, P[3][0], P[3][1], 0, b_in, "fv")
        fcol, fcol_b = k.sbuf("fcol", [64, 4], F32)
        k.op("dve", lambda e: e.tensor_copy(out=fcol[:], in_=P[3][0][0:64, 0:4]), [P[3][1]], [fcol_b])
        k.op("dve", lambda e: e.tensor_tensor(out=fcol[:, 1:4], in0=fcol[:, 1:4], in1=fcol[:, 0:1].to_broadcast([64, 3]),
                                               op=ALU.mult), [fcol_b], [fcol_b])
        w1t, w1t_b = k.sbuf("w1t", [33, 64], F32)
        k.dma("sp", w1t[:], fw1, w1t_b, b_in, k.dsem())
        w23t, w23t_b = k.sbuf("w23t", [64, 2, 64], F32)
        k.dma("sp", w23t[:], fw23, w23t_b, b_in, k.dsem())
        w4t, w4t_b = k.sbuf("w4t", [64, 2, 512], F32)
        k.dma("sp", w4t[:], fw4, w4t_b, b_in, k.dsem())
        negt_t, negt_b = k.sbuf("negt_sb", [128, 128], F32)
        k.dma("sp", negt_t[:], negt, negt_b, b_in, k.dsem())
        textc_t, textc_b = k.sbuf("textc_sb", [128, 2 * CTX], F32)
        k.dma("sp", textc_t[:], textc, textc_b, b_in, k.dsem())
        drow_t, drow_b = k.sbuf("drow_sb", [128, 512], F32)
        k.dma("sp", drow_t[:], drow, drow_b, b_in, k.dsem())
        ndcol_t, ndcol_b = k.sbuf("ndcol_sb", [128, 4], F32)
        k.dma("sp", ndcol_t[:], ndcol, ndcol_b, b_in, k.dsem())
        onesf, onesf_b = k.sbuf("onesf", [128, 128], F32)
        k.op("pool", lambda e: e.memset(onesf[:], 1.0), [], [onesf_b])
        if stop == 1:
            dump(dbgf[:, 0:52], colv[:], colv_b)
            dump(dbgf[0:64, 64:68], fcol[:], fcol_b)
            finish()
            return nc
        ht = HT(k, ident, M_b, gm, sh)
        hst = [k.sbuf("hst0", [128, 8, 128], BF16)] * 2
        hss = [k.dsem()] * 2
        for t in range(NT):
            lat = t < L // 128
            xa = x[t * 128:(t + 1) * 128, :] if lat else xc[(t - L // 128) * 128:(t - L // 128 + 1) * 128, :]
            pt, pb = P[t % 2]
            hs_, hs_b = hst[t % 2]
            ht.tile(xa, b_in, 0 if lat else 1, pt[:].bitcast(BF16)[:, 0:1024].rearrange("p (k t) -> p k t", k=8), pb, hs_[:], hs_b)
            k.dma("act", hT_scr[:, :, t * 128:(t + 1) * 128], hs_[:], hT_scr_b, hs_b, hss[t % 2])
        hT_all = Buf("hT_all")
        k.wait_all("sp", [hT_scr_b])
        wst = [k.sbuf("wst0", [128, 512], F32)] * 2
        wss = [k.dsem()] * 2
        wbf, wbf_b = k.sbuf("wbf", [128, 8, 512], BF16)
        hblk = [k.sbuf("hblk%d" % i, [128, 8, 512], BF16) for i in range(2)]
        hbs = [k.dsem() for _ in range(2)]
        pbuf, pbuf_b = k.sbuf("pbuf", [128, L + 2], BF16)
        bufA, bufA_b = k.sbuf("bufA", [128, L], BF16)
        bufB, bufB_b = k.sbuf("bufB", [128, L], BF16)
        bufC, bufC_b = k.sbuf("bufC", [128, L], BF16)
        ctmp = [k.sbuf("ctmp%d" % i, [128, 1056], F32) for i in range(2)]
        u_tm, u_tm_b = bufB[:].rearrange("p (a b) -> p a b", a=64), bufB_b
        y_tm, y_tm_b = pbuf[:, 0:L].rearrange("p (a b) -> p a b", a=64), pbuf_b
        k_tm, k_tm_b = k.sbuf("k_tm", [128, 128, 128], BF16)
        zt = [k.sbuf("zt0", [33, 512], F32)] * 2
        zts = [k.dsem()] * 2
        hb0f = hblk[0][0][:].rearrange("p a b -> p (a b)").bitcast(F32)
        hm = [(hb0f[0:64, i * 512:(i + 1) * 512], hblk[0][1]) for i in range(3)]
        rr, rr_b = hb0f[0:64, 1536:2048], hblk[0][1]
        dec = [k.sbuf("dec%d" % i, [128, 128], F32) for i in range(2)]
        kf32 = [k.sbuf("kf32_%d" % i, [128, 128], F32) for i in range(2)]
        kab = [k.sbuf("kab%d" % i, [128, 128], F32) for i in range(2)]
        nrm, nrm_b = k.sbuf("nrm", [128, 128], F32)
        scol, scol_b = k.sbuf("scol", [128, 2], F32)
        kc_t, kc_b = k.sbuf("kc_t", [128, 2 * CTX], F32)
        Ak = [k.sbuf("Ak%d" % i, [128, 512], BF16) for i in range(2)]
        Au = [k.sbuf("Au%d" % i, [64, 512], BF16) for i in range(2)]
        KF = [k.sbuf("KF%d" % i, [128, 3, 256], F32) for i in range(2)]
        T1 = [k.sbuf("T1_%d" % i, [128, 512], F32) for i in range(2)]
        T2 = [k.sbuf("T2_%d" % i, [128, 512], F32) for i in range(2)]
        Yb = [k.sbuf("Yb%d" % i, [128, 512], BF16) for i in range(2)]
        Bs = [k.sbuf("Bs%d" % i, [128, 2, 2, 4, 65], BF16) for i in range(2)]
        h1s, h1s_b = k.sbuf("h1s", [128, 4, 65], F32)
        fin = [k.sbuf("fin0", [128, 1024], F32)] * 2
        fout = [k.sbuf("fout0", [128, 1024], BF16)] * 2
        fos = [k.dsem()] * 2
        pc_, pc_b = ctmp[0][0][:, 0:4 * (CTX + 2)].rearrange("p (a b) -> p a b", a=4), ctmp[0][1]
        xcv, xcv_b = ctmp[1][0][:, 0:4 * CTX].rearrange("p (a b) -> p a b", a=4), ctmp[1][1]
        acc = [k.sbuf("acc%d" % i, [128, CTX], F32) for i in range(2)]
        oc, oc_b = k.sbuf("oc", [128, CTX], BF16)
        ocs = k.dsem()
        w_v = w.rearrange("(k p) n -> p k n", p=128)

        h3_scr, _ = k.dram("h3_scr", [64, 2 * L], F32)
        h3c_scr, _ = k.dram("h3c_scr", [64, 2 * CTX], F32)
        h3_rb = [Buf("h3r") for _ in range(33)]
        h3s = k.dsem()

        def filter_mlp(ztile, ztile_b, ncols, j):
            cur, cur_b = ztile, ztile_b
            kdim = 33
            for layer in range(3):
                ps, psb = P[2 + (layer % 2)]
                wl = w1t[:, :] if layer == 0 else w23t[:, layer - 1, :]
                wl_b = w1t_b if layer == 0 else w23t_b
                k.op("pe", lambda e, ps=ps, wl=wl, cur=cur, kdim=kdim: e.matmul(
                    ps[0:64, :ncols], lhsT=wl, rhs=cur[0:kdim, :ncols], start=True, stop=True), [wl_b, cur_b], [psb])
                o_, o_b = hm[layer]
                k.op("dve", lambda e, o_=o_, ps=ps, layer=layer: e.tensor_scalar(
                    out=o_[:, :ncols], in0=ps[0:64, :ncols], scalar1=fcol[:, 0:1], scalar2=fcol[:, layer + 1:layer + 2],
                    op0=ALU.mult, op1=ALU.add), [psb, fcol_b], [o_b])
                MAGIC = 12582912.0
                k.op("dve", lambda e, o_=o_: e.tensor_scalar(
                    out=rr[:, :ncols], in0=o_[:, :ncols], scalar1=1.0 / TWO_PI, scalar2=MAGIC,
                    op0=ALU.mult, op1=ALU.add), [o_b], [rr_b])
                k.op("dve", lambda e: e.tensor_scalar(
                    out=rr[:, :ncols], in0=rr[:, :ncols], scalar1=MAGIC, scalar2=TWO_PI,
                    op0=ALU.subtract, op1=ALU.mult), [rr_b], [rr_b])
                k.op("dve", lambda e, o_=o_: e.tensor_tensor(out=o_[:, :ncols], in0=o_[:, :ncols], in1=rr[:, :ncols], op=ALU.subtract),
                     [o_b, rr_b], [o_b])
                k.op("act", lambda e, o_=o_: e.activation(out=o_[:, :ncols], in_=o_[:, :ncols], func=AF.Sin, scale=0.999999),
                     [o_b], [o_b])
                cur, cur_b, kdim = o_, o_b, 64
            return cur, cur_b

        def do_sg(sg):
            c0 = sg * 128
            for ty in range(4):
                for half in range(2):
                    i = (ty * 2 + half) % 2
                    t_, t_b = wst[i]
                    k.dma("sp", t_[:].rearrange("p (k n) -> p k n", k=4), w_v[:, half * 4:half * 4 + 4, ty * 512 + c0: ty * 512 + c0 + 128],
                          t_b, b_in, wss[i])
                    k.op("dve", lambda e, t_=t_, ty=ty, half=half: e.tensor_copy(
                        out=wbf[:, half * 4:half * 4 + 4, ty * 128:(ty + 1) * 128], in_=t_[:].rearrange("p (k n) -> p k n", k=4)),
                        [t_b], [wbf_b])
            for blk in range(2 * L // 512):
                if sg == 0:
                    z_, z_b = zt[blk % 2]
                    k.dma("sp", z_[:], zext[:, blk * 512:(blk + 1) * 512], z_b, b_in, zts[blk % 2])
                    h3, h3_b = filter_mlp(z_, z_b, 512, blk)
                    k.dma("act", h3_scr[:, blk * 512:(blk + 1) * 512], h3, h3_rb[blk], h3_b, h3s)
                else:
                    h3, h3_b = hm[2]
                    k.dma("sp", h3, h3_scr[:, blk * 512:(blk + 1) * 512], h3_b, h3_rb[blk], h3s)
                fb = 0 if blk < L // 512 else 1
                pk, pkb = P[4 + blk % 2]
                for tt in range(4):
                    k.op("pe", lambda e, pk=pk, tt=tt, h3=h3, fb=fb: e.matmul(
                        pk[:, tt * 128:(tt + 1) * 128], lhsT=h3[:, tt * 128:(tt + 1) * 128], rhs=w4t[:, fb, c0:c0 + 128],
                        start=(tt == 0), stop=True), [h3_b, w4t_b], [pkb], inc=(tt == 3))
                for tt in range(4):
                    m1 = blk * 4 + tt
                    d_, d_b = dec[m1 % 2]
                    f_, f_b = kf32[m1 % 2]
                    a_, a_b = kab[m1 % 2]
                    k.op("act", lambda e, d_=d_, m1=m1: e.activation(out=d_[:], in_=drow_t[:, c0:c0 + 128], func=AF.Exp,
                                                                       scale=negt_t[:, m1:m1 + 1]), [drow_b, negt_b], [d_b])
                    k.op("dve", lambda e, f_=f_, pk=pk, tt=tt, d_=d_: e.tensor_tensor(
                        out=f_[:], in0=pk[:, tt * 128:(tt + 1) * 128], in1=d_[:], op=ALU.mult), [pkb, d_b], [f_b])
                    k.op("pool", lambda e, f_=f_, m1=m1: e.tensor_copy(out=k_tm[:, m1, :], in_=f_[:]), [f_b], [k_tm_b])
                    k.op("act", lambda e, a_=a_, f_=f_: e.activation(out=a_[:], in_=f_[:], func=AF.Abs), [f_b], [a_b])
                    k.op("pe", lambda e, a_=a_, m1=m1: e.matmul(P[6][0][:, 0:128], lhsT=onesf[:], rhs=a_[:], start=(m1 == 0),
                                                                 stop=(m1 == 127)), [onesf_b, a_b], [P[6][1]])
            k.op("dve", lambda e: e.tensor_copy(out=nrm[:], in_=P[6][0][:, 0:128]), [P[6][1]], [nrm_b])
            row_to_cols(k, P[7][0], P[7][1], 0, nrm, nrm_b, 1, one, one_b)
            k.op("dve", lambda e: e.tensor_scalar(out=scol[:, 0:1], in0=P[7][0][:, 0:1], scalar1=32768.0, scalar2=None, op0=ALU.mult),
                 [P[7][1]], [scol_b])
            k.op("dve", lambda e: e.reciprocal(out=scol[:, 0:1], in_=scol[:, 0:1]), [scol_b], [scol_b])
            if sg == 0:
                z_, z_b = zt[0]
                k.dma("sp", z_[:], zextc[:, :], z_b, b_in, zts[0])
                h3, h3_b = filter_mlp(z_, z_b, 512, 0)
                k.dma("act", h3c_scr[:, :], h3, h3_rb[32], h3_b, h3s)
                for rb_ in h3_rb:
                    rb_.w = (h3s, h3s.cnt)
            else:
                h3, h3_b = hm[2]
                k.dma("sp", h3, h3c_scr[:, :], h3_b, h3_rb[32], h3s)
            pk, pkb = P[4]
            for fb in range(2):
                k.op("pe", lambda e, fb=fb, h3=h3: e.matmul(pk[:, fb * 256:(fb + 1) * 256], lhsT=w4t[:, fb, c0:c0 + 128],
                                                             rhs=h3[:, fb * 256:(fb + 1) * 256], start=(fb == 0), stop=True),
                     [w4t_b, h3_b], [pkb], inc=(fb == 1))
            k.op("act", lambda e: e.activation(out=kc_t[:], in_=textc_t[:], func=AF.Exp, scale=ndcol_t[:, sg:sg + 1]),
                 [textc_b, ndcol_b], [kc_b])
            k.op("dve", lambda e: e.tensor_tensor(out=kc_t[:], in0=pk[:], in1=kc_t[:], op=ALU.mult), [pkb, kc_b], [kc_b])
            k.op("dve", lambda e: e.tensor_reduce(out=scol[:, 1:2], in_=kc_t[:], axis=AX.X, op=ALU.add, apply_absolute_value=True),
                 [kc_b], [scol_b])
            k.op("dve", lambda e: e.reciprocal(out=scol[:, 1:2], in_=scol[:, 1:2]), [scol_b], [scol_b])
            if stop == 2 and sg == 0:
                dump(dbgh[:, 0:16384], k_tm[:].rearrange("p a b -> p (a b)"), k_tm_b)
                dump(dbgf[:, 0:512], kc_t[:], kc_b)
                dump(dbgf[:, 512:514], scol[:], scol_b)
                finish()
                return True
            def proj_stream(ty, consume):
                for blk in range(L // 512):
                    hb, hb_b = hblk[blk % 2]
                    k.dma("sp", hb[:], hT_scr[:, :, blk * 512:(blk + 1) * 512], hb_b, hT_scr_b, hbs[blk % 2])
                    ps, psb = P[blk % 2]
                    for kc in range(8):
                        k.op("pe", lambda e, ps=ps, kc=kc, hb=hb: e.matmul(
                            ps[:], lhsT=wbf[:, kc, ty * 128:(ty + 1) * 128], rhs=hb[:, kc, :], start=(kc == 0), stop=(kc == 7)),
                            [wbf_b, hb_b], [psb], inc=(kc == 7))
                    consume(blk, ps, psb)

            def to_pbuf(blk, ps, psb):
                k.op("act", lambda e, ps=ps, blk=blk: e.activation(out=pbuf[:, 1 + blk * 512:1 + (blk + 1) * 512], in_=ps[:],
                                                                     func=AF.Identity), [psb], [pbuf_b])

            def conv_to(dst, dst_b, vi):
                for ch in range(L // 1024):
                    o = ch * 1024
                    a_, a_b = ctmp[ch % 2]
                    k.op("act", lambda e, a_=a_, o=o: e.activation(out=a_[:, 0:1024], in_=pbuf[:, 1 + o:1 + o + 1024], func=AF.Identity,
                                                                    scale=cv(3 + vi, sg), bias=cv(9 + vi, sg)), [pbuf_b, colv_b], [a_b])
                    k.op("dve", lambda e, a_=a_, o=o: e.scalar_tensor_tensor(out=a_[:, 0:1024], in0=pbuf[:, o:o + 1024], scalar=cv(vi, sg),
                                                                              in1=a_[:, 0:1024], op0=ALU.mult, op1=ALU.add), [pbuf_b, colv_b, a_b], [a_b])
                    k.op("dve", lambda e, a_=a_, o=o: e.scalar_tensor_tensor(out=dst[:, o:o + 1024], in0=pbuf[:, 2 + o:2 + o + 1024],
                                                                               scalar=cv(6 + vi, sg), in1=a_[:, 0:1024], op0=ALU.mult, op1=ALU.add),
                         [pbuf_b, colv_b, a_b], [dst_b])

            k.op("dve", lambda e: e.memset(pbuf[:, 0:1], 0.0), [], [pbuf_b])
            k.op("dve", lambda e: e.memset(pbuf[:, L + 1:L + 2], 0.0), [], [pbuf_b])
            proj_stream(1, to_pbuf)
            conv_to(bufA, bufA_b, 1)
            proj_stream(2, to_pbuf)
            conv_to(bufB, bufB_b, 2)
            for ch in range(4):
                o = ch * 2048
                k.op("dve", lambda e, o=o: e.tensor_tensor(out=bufA[:, o:o + 2048], in0=bufA[:, o:o + 2048], in1=bufB[:, o:o + 2048],
                                                            op=ALU.mult), [bufA_b, bufB_b], [bufA_b])
            proj_stream(0, to_pbuf)
            conv_to(bufC, bufC_b, 0)

            def to_silu(blk, ps, psb):
                k.op("act", lambda e, ps=ps, blk=blk: e.activation(out=bufB[:, blk * 512:(blk + 1) * 512], in_=ps[:], func=AF.Silu),
                     [psb], [bufB_b])
            proj_stream(3, to_silu)
            for ch in range(4):
                o = ch * 2048
                k.op("dve", lambda e, o=o: e.tensor_tensor(out=bufC[:, o:o + 2048], in0=bufC[:, o:o + 2048], in1=bufB[:, o:o + 2048],
                                                            op=ALU.mult), [bufC_b, bufB_b], [bufC_b])
            if stop == 3 and sg == 0:
                dump(dbgh[:, 0:8192], bufA[:], bufA_b)
                dump(dbgh[:, 8192:16384], bufC[:], bufC_b)
                finish()
                return True
            for n8 in range(8):
                pt, pb = P[n8 % 2]
                ptb = pt[:].bitcast(BF16)[:, 0:1024].rearrange("p (a c) -> p a c", a=8)
                for a in range(8):
                    n1 = n8 * 8 + a
                    k.op("pe", lambda e, ptb=ptb, a=a, n1=n1: e.transpose(ptb[:, a, :], bufA[:, n1 * 128:(n1 + 1) * 128], ident),
                         [bufA_b, M_b], [pb], inc=(a == 7))
                k.op("act", lambda e, ptb=ptb, n8=n8: e.activation(out=u_tm[:, n8 * 8:(n8 + 1) * 8, :], in_=ptb, func=AF.Identity),
                     [pb], [u_tm_b])
            hb, hb_b = hblk[0]
            k.dma("sp", hb[:, :, 0:CTX], hT_scr[:, :, L:L + CTX], hb_b, hT_scr_b, hbs[0])
            k.op("dve", lambda e: e.memset(pc_, 0.0), [], [pc_b])
            for ty in range(4):
                ps, psb = P[ty % 2]
                for kc in range(8):
                    k.op("pe", lambda e, ps=ps, kc=kc, ty=ty: e.matmul(ps[:, 0:CTX], lhsT=wbf[:, kc, ty * 128:(ty + 1) * 128],
                                                                        rhs=hb[:, kc, 0:CTX], start=(kc == 0), stop=(kc == 7)),
                         [wbf_b, hb_b], [psb], inc=(kc == 7))
                if ty < 3:
                    k.op("act", lambda e, ps=ps, ty=ty: e.activation(out=pc_[:, ty, 1:CTX + 1], in_=ps[:, 0:CTX], func=AF.Identity),
                         [psb], [pc_b])
                else:
                    k.op("act", lambda e, ps=ps: e.activation(out=xcv[:, 3, :], in_=ps[:, 0:CTX], func=AF.Silu), [psb], [xcv_b])
            for ty in range(3):
                k.op("act", lambda e, ty=ty: e.activation(out=xcv[:, ty, :], in_=pc_[:, ty, 1:CTX + 1], func=AF.Identity,
                                                            scale=cv(3 + ty, sg), bias=cv(9 + ty, sg)), [pc_b, colv_b], [xcv_b])
                k.op("dve", lambda e, ty=ty: e.scalar_tensor_tensor(out=xcv[:, ty, :], in0=pc_[:, ty, 0:CTX], scalar=cv(ty, sg),
                                                                     in1=xcv[:, ty, :], op0=ALU.mult, op1=ALU.add), [pc_b, colv_b, xcv_b], [xcv_b])
                k.op("dve", lambda e, ty=ty: e.scalar_tensor_tensor(out=xcv[:, ty, :], in0=pc_[:, ty, 2:CTX + 2], scalar=cv(6 + ty, sg),
                                                                     in1=xcv[:, ty, :], op0=ALU.mult, op1=ALU.add), [pc_b, colv_b, xcv_b], [xcv_b])
            k.op("dve", lambda e: e.tensor_tensor(out=xcv[:, 1, :], in0=xcv[:, 1, :], in1=xcv[:, 2, :], op=ALU.mult), [xcv_b], [xcv_b])
            k.op("dve", lambda e: e.tensor_tensor(out=xcv[:, 0, :], in0=xcv[:, 0, :], in1=xcv[:, 3, :], op=ALU.mult), [xcv_b], [xcv_b])
            uc = xcv[:, 1, :]
            ctt = [(T1[0][0][:, 0:CTX], T1[0][1]), (T2[0][0][:, 0:CTX], T2[0][1])]
            for a in range(2):
                k.op("pool", lambda e, a=a: e.memset(acc[a][0][:], 0.0), [], [acc[a][1]])
            lag_ops = []
            for lag in range(-(CTX - 1), CTX):
                if lag >= 0:
                    lag_ops.append((slice(lag, CTX), slice(0, CTX - lag), lag))
                else:
                    m = -lag
                    lag_ops.append((slice(0, CTX - m), slice(m, CTX), 2 * CTX - m))
            lag_state = [0]

            def emit_lags(nops):
                for _ in range(nops):
                    n = lag_state[0]
                    if n >= len(lag_ops):
                        return
                    lag_state[0] += 1
                    osl, isl, idx = lag_ops[n]
                    a_, a_b = acc[n % 2]
                    k.op("dve", lambda e, a_=a_, osl=osl, isl=isl, idx=idx: e.scalar_tensor_tensor(
                        out=a_[:, osl], in0=uc[:, isl], scalar=kc_t[:, idx:idx + 1], in1=a_[:, osl], op0=ALU.mult, op1=ALU.add),
                        [xcv_b, kc_b, a_b], [a_b])
            def S0(c):
                pp = c % 2
                k.op("pe", lambda e: e.matmul(P[pp][0][:], lhsT=k_tm[:, :, c], rhs=F256, start=True, stop=True),
                     [k_tm_b, M_b], [P[pp][1]])

            def S1(c):
                pp = c % 2
                k.op("act", lambda e: e.activation(out=Ak[pp][0][:], in_=P[pp][0][:], func=AF.Identity), [P[pp][1]], [Ak[pp][1]])

            def S2(c):
                pp = c % 2
                A_, A_b = Ak[pp]
                ps, psb = P[2 + pp]
                k.op("pe", lambda e: e.matmul(ps[:], lhsT=CS[:, 0, :], rhs=A_[:], start=True, stop=False), [M_b, A_b], [psb], inc=False)
                k.op("pe", lambda e: e.matmul(ps[:, 0:256], lhsT=CS[:, 1, :], rhs=A_[:, 256:512], start=False, stop=False),
                     [M_b, A_b], [psb], inc=False)
                k.op("pe", lambda e: e.matmul(ps[:, 256:512], lhsT=CS[:, 2, :], rhs=A_[:, 0:256], start=False, stop=True),
                     [M_b, A_b], [psb])

            def S3(c):
                pp = c % 2
                ps, psb = P[2 + pp]
                KF_, KF_b = KF[pp]
                k.op("act", lambda e: e.activation(out=KF_[:, 0:2, :], in_=ps[:].rearrange("p (a b) -> p a b", a=2), func=AF.Identity),
                     [psb], [KF_b])
                k.op("act", lambda e: e.activation(out=KF_[:, 2, :], in_=ps[:, 256:512], func=AF.Identity, scale=-1.0),
                     [psb], [KF_b])

            def S4(c):
                pp = c % 2
                k.op("pe", lambda e: e.matmul(P[pp][0][0:64, :], lhsT=u_tm[:, :, c], rhs=F256, start=True, stop=True),
                     [u_tm_b, M_b], [P[pp][1]])

            def S5(c):
                pp = c % 2
                k.op("act", lambda e: e.activation(out=Au[pp][0][:], in_=P[pp][0][0:64, :], func=AF.Identity), [P[pp][1]], [Au[pp][1]])

            def S6(c):
                pp = c % 2
                A_, A_b = Au[pp]
                ps, psb = P[2 + pp]
                k.op("pe", lambda e: e.matmul(ps[:], lhsT=CS[0:64, 0, :], rhs=A_[:], start=True, stop=False), [M_b, A_b], [psb], inc=False)
                k.op("pe", lambda e: e.matmul(ps[:, 0:256], lhsT=CS[0:64, 1, :], rhs=A_[:, 256:512], start=False, stop=False),
                     [M_b, A_b], [psb], inc=False)
                k.op("pe", lambda e: e.matmul(ps[:, 256:512], lhsT=CS[0:64, 2, :], rhs=A_[:, 0:256], start=False, stop=True),
                     [M_b, A_b], [psb])

            def S7(c):
                pp = c % 2
                X, X_b = P[2 + pp]
                KF_, KF_b = KF[pp]
                a1, a1_b = T1[pp]
                a2, a2_b = T2[pp]
                k.op("dve", lambda e: e.tensor_tensor(out=a1[:].rearrange("p (a b) -> p a b", a=2), in0=X[:].rearrange("p (a b) -> p a b", a=2),
                                                       in1=KF_[:, 0:1, :].to_broadcast([128, 2, 256]), op=ALU.mult), [X_b, KF_b], [a1_b])
                k.op("dve", lambda e: e.tensor_tensor(out=a2[:, 0:256], in0=X[:, 256:512], in1=KF_[:, 2, :], op=ALU.mult),
                     [X_b, KF_b], [a2_b])
                k.op("dve", lambda e: e.tensor_tensor(out=a2[:, 256:512], in0=X[:, 0:256], in1=KF_[:, 1, :], op=ALU.mult),
                     [X_b, KF_b], [a2_b])

            def S8(c):
                pp = c % 2
                k.op("pool", lambda e: e.tensor_tensor(out=Yb[pp][0][:], in0=T1[pp][0][:], in1=T2[pp][0][:], op=ALU.add),
                     [T1[pp][1], T2[pp][1]], [Yb[pp][1]])

            def S9(c):
                pp = c % 2
                Y_, Y_b = Yb[pp]
                ps, psb = P[4 + pp]
                for jj in range(2):
                    k.op("pe", lambda e, jj=jj: e.matmul(ps[:, jj * 130:(jj + 1) * 130], lhsT=Y_[:, jj * 128:(jj + 1) * 128],
                                                          rhs=RI[:, 0, :], start=True, stop=False), [Y_b, M_b], [psb], inc=False)
                    k.op("pe", lambda e, jj=jj: e.matmul(ps[:, jj * 130:(jj + 1) * 130], lhsT=Y_[:, 256 + jj * 128:256 + (jj + 1) * 128],
                                                          rhs=RI[:, 1, :], start=False, stop=True), [Y_b, M_b], [psb], inc=(jj == 1))

            def S10(c):
                pp = c % 2
                cq = c % 4
                Bt_, Bt_b = Bs[(c // 4) % 2]
                ps, psb = P[4 + pp]
                k.op("act", lambda e: e.activation(
                    out=Bt_[:, :, :, cq, :], in_=ps[:, 0:260].rearrange("p (j r n) -> p j r n", j=2, r=2), func=AF.Identity),
                    [psb], [Bt_b])

            def S11(c):
                if c % 4 != 3:
                    return
                Bt_, Bt_b = Bs[(c // 4) % 2]
                for hh in range(2):
                    n = 0
                    for jj in range(2):
                        for ri in range(2):
                            k.op("pe", lambda e, hh=hh, jj=jj, ri=ri, n=n: e.matmul(
                                P[6 + hh][0][:, 0:260], lhsT=CSs[:, jj, hh, ri, :],
                                rhs=Bt_[:, jj, ri, :, :].rearrange("p c n -> p (c n)"), start=(n == 0), stop=(n == 3)),
                                [M_b, Bt_b], [P[6 + hh][1]], inc=(n == 3))
                            n += 1
                k.op("act", lambda e: e.activation(out=h1s[:], in_=P[7][0][:, 0:260].rearrange("p (c n) -> p c n", c=4), func=AF.Identity),
                     [P[7][1]], [h1s_b])
                cb = c - 3
                k.op("dve", lambda e: e.tensor_tensor(
                    out=y_tm[:, :, cb:cb + 4].rearrange("p n c -> p c n"),
                    in0=P[6][0][:, 0:260].rearrange("p (c n) -> p c n", c=4)[:, :, 1:65], in1=h1s[:, :, 0:64], op=ALU.add),
                    [P[6][1], h1s_b], [y_tm_b])

            stages = [(S0, 0), (S1, 0), (S2, 1), (S3, 1), (S4, 1), (S5, 1), (S6, 2), (S7, 2), (S8, 2), (S9, 3), (S10, 3), (S11, 3)]
            for tstep in range(128 + 3):
                for fn_, off in stages:
                    cch = tstep - off
                    if 0 <= cch < 128:
                        fn_(cch)
                emit_lags(4)
            emit_lags(len(lag_ops))
            if stop == 4 and sg == 0:
                dump(dbgh[:, 0:8192], y_tm[:].rearrange("p a b -> p (a b)"), y_tm_b)
                dump(dbgf[:, 0:2], scol[:], scol_b)
                finish()
                return True
            for n8 in range(8):
                pt, pb = P[n8 % 2]
                ptb = pt[:].bitcast(BF16)[:, 0:1024].rearrange("p (a c) -> p a c", a=8)
                for a in range(8):
                    n1 = n8 * 8 + a
                    k.op("pe", lambda e, ptb=ptb, a=a, n1=n1: e.transpose(ptb[:, a, :], y_tm[:, n1, :], ident),
                         [y_tm_b, M_b], [pb], inc=(a == 7))
                f_, f_b = fin[n8 % 2]
                o_, o_b = fout[n8 % 2]
                o = n8 * 1024
                k.op("act", lambda e, f_=f_, ptb=ptb: e.activation(out=f_[:], in_=ptb.rearrange("p a c -> p (a c)"), func=AF.Identity,
                                                                     scale=scol[:, 0:1]), [pb, scol_b], [f_b])
                k.op("dve", lambda e, f_=f_, o=o: e.scalar_tensor_tensor(out=f_[:], in0=bufA[:, o:o + 1024], scalar=cv(12, sg), in1=f_[:],
                                                                          op0=ALU.mult, op1=ALU.add), [bufA_b, colv_b, f_b], [f_b])
                k.op("dve", lambda e, f_=f_, o_=o_, o=o: e.tensor_tensor(out=o_[:], in0=f_[:], in1=bufC[:, o:o + 1024], op=ALU.mult),
                     [f_b, bufC_b], [o_b])
                ob = Buf("yo")
                outs.append(ob)
                k.dma("act", ygT.rows_cols(c0, c0 + 128, o, 1024), o_[:], ob, o_b, fos[n8 % 2])
            k.op("dve", lambda e: e.tensor_tensor(out=acc[0][0][:], in0=acc[0][0][:], in1=acc[1][0][:], op=ALU.add),
                 [acc[0][1], acc[1][1]], [acc[0][1]])
            k.op("dve", lambda e: e.tensor_scalar(out=acc[0][0][:], in0=acc[0][0][:], scalar1=scol[:, 1:2], scalar2=None, op0=ALU.mult),
                 [acc[0][1], scol_b], [acc[0][1]])
            k.op("dve", lambda e: e.scalar_tensor_tensor(out=acc[0][0][:], in0=uc, scalar=cv(12, sg), in1=acc[0][0][:],
                                                          op0=ALU.mult, op1=ALU.add), [xcv_b, colv_b, acc[0][1]], [acc[0][1]])
            k.op("dve", lambda e: e.tensor_tensor(out=oc[:], in0=acc[0][0][:], in1=xcv[:, 0, :], op=ALU.mult), [acc[0][1], xcv_b], [oc_b])
            ob = Buf("yco")
            outs.append(ob)
            k.dma("act", ygTc[c0:c0 + 128, :], oc[:], ob, oc_b, ocs)
            return False
        for sg_ in range(4):
            if do_sg(sg_):
                return nc
        finish()
    return nc


def run_hy(x, ctx, c, c_ctx, ada_w0, ada_b0, norm_g0, inp, stop=0):
    nc = build_hy(stop)
    f32 = np.float32
    mats = hy_consts()
    zext, text = hy_ztab(L)
    zextc, textc = hy_ztab(CTX)
    negt = np.ascontiguousarray((-text).reshape(128, 128).T)
    textc_b = np.ascontiguousarray(np.broadcast_to(textc[None, :], (128, 2 * CTX))).astype(f32)
    deltas = np.abs(np.linspace(MIN_DECAY, MAX_DECAY, E, dtype=f32))
    w_in, conv_w, conv_b = inp["hy_w_in"][0], inp["hy_conv_w"][0], inp["hy_conv_b"][0]
    in_maps = []
    for core in range(NCORES):
        b, g = divmod(core, 4)
        ch = np.arange(512 * g, 512 * g + 512)
        cols = np.concatenate([ty * E + ch for ty in range(4)])
        vecs = np.concatenate([conv_w[tap, ty * E + ch] for tap in range(3) for ty in range(3)] +
                              [conv_b[ty * E + ch] for ty in range(3)] + [inp["hy_d"][0][ch]]).astype(f32)
        fvec = np.concatenate([inp["hy_freq"][0], inp["hy_fb1"][0], inp["hy_fb2"][0], inp["hy_fb3"][0]]).astype(f32)
        fw4 = np.stack([inp["hy_fw4"][0][:, ch], inp["hy_fw4"][0][:, E + ch]], axis=1)
        dl = deltas[ch]
        in_maps.append({
            "x": np.ascontiguousarray(x[b]), "xc": np.ascontiguousarray(ctx[b]),
            "cvec": np.ascontiguousarray(np.stack([c[b], c_ctx])),
            "ada_w": np.ascontiguousarray(ada_w0[:, :2048]), "ada_b": np.ascontiguousarray(ada_b0[:2048]), "norm_g": norm_g0,
            "w": np.ascontiguousarray(w_in[:, cols]), "vecs": vecs, "fvec": fvec, "fw1": inp["hy_fw1"][0],
            "fw23": np.ascontiguousarray(np.stack([inp["hy_fw2"][0], inp["hy_fw3"][0]], axis=1)),
            "fw4": np.ascontiguousarray(fw4), "zext": zext, "zextc": zextc, "negt": negt, "textc": textc_b,
            "drow": np.ascontiguousarray(np.broadcast_to(dl[None, :], (128, 512))).astype(f32),
            "ndcol": np.ascontiguousarray(-dl.reshape(4, 128).T), "mats": mats, "identf": np.eye(64, dtype=np.float32)})
    res = run_bass_kernel_spmd(nc, in_maps, core_ids=list(range(NCORES)))
    if stop:
        return res.results
    ygT = [np.concatenate([res.results[b * 4 + g]["ygT"] for g in range(4)], axis=0) for b in range(B)]
    ygTc = [np.concatenate([res.results[b * 4 + g]["ygTc"] for g in range(4)], axis=0) for b in range(B)]
    return ygT, ygTc


RG = [[0, 1, 2, 3], [4, 5, 6, 7]]


def _all_gather(nc, srcs, dsts, name):
    n = len(srcs)
    with ExitStack() as es:
        sem = es.enter_context(nc.semaphore(name))
        block = es.enter_context(nc.Block())

        def pool(g):
            for src, dst in zip(srcs, dsts):
                g.collective_compute("AllGather", ALU.bypass, replica_groups=RG, ins=[src], outs=[dst]).then_inc(sem, 1)
            g.wait_ge(sem, n)

        def other(e):
            e.wait_ge(sem, n)
        block.gpsimd(pool)
        block.tensor(other)
        block.scalar(other)
        block.vector(other)
        block.sync(other)


FUSED_INPUTS = [
    ("x", [L, D], F32), ("xc", [CTX, D], F32), ("cvec", [2, D], F32), ("identf", [64, 64], F32),
    ("hy_ada_w", [D, 2048], F32), ("hy_ada_b", [2048], F32), ("hy_norm_g", [D], F32), ("hy_w", [D, 2048], F32),
    ("vecs", [13 * 512], F32), ("fvec", [256], F32), ("fw1", [33, 64], F32), ("fw23", [64, 2, 64], F32),
    ("fw4", [64, 2, 512], F32), ("zext", [33, 2 * L], F32), ("zextc", [33, 2 * CTX], F32), ("negt", [128, 128], F32),
    ("textc", [128, 2 * CTX], F32), ("drow", [128, 512], F32), ("ndcol", [128, 4], F32), ("mats", [128, 2308], BF16),
    ("at_ada_w", [D, 2048], F32), ("at_ada_b", [2048], F32), ("at_norm_g", [D], F32), ("at_w", [D, 1280], F32),
    ("qkg", [256], F32), ("cosT", [128, L], F32), ("sinT", [128, L], F32), ("consts", [128, 4, 128], BF16),
    ("w_out0", [E, D], F32), ("ada_w_gt0", [D, D], F32), ("ada_b_gt0", [D], F32),
    ("w_out1", [E, D], F32), ("ada_w_gt1", [D, D], F32), ("ada_b_gt1", [D], F32),
]


def build_fused():
    nc = bass.Bass("TRN2", target_bir_lowering=False)
    I = {n: nc.dram_tensor(n, sh, dty, kind="ExternalInput").ap() for n, sh, dty in FUSED_INPUTS}
    out = nc.dram_tensor("out", [L, D], F32, kind="ExternalOutput").ap()
    CW = 1024
    widths = [CW] * (L // CW) + [CTX]
    yg_loc = [nc.dram_tensor("yg_loc%d" % j, [512, wd], BF16, kind="Internal").ap() for j, wd in enumerate(widths)]
    yg_all = [nc.dram_tensor("yg_all%d" % j, [E, wd], BF16, kind="Internal").ap() for j, wd in enumerate(widths)]
    x1_full = nc.dram_tensor("x1_full", [L, D], F32, kind="Internal").ap()
    og_loc = [nc.dram_tensor("og_loc%d" % j, [512, CW], BF16, kind="Internal").ap() for j in range(L // CW)]
    og_all = [nc.dram_tensor("og_all%d" % j, [E, CW], BF16, kind="Internal").ap() for j in range(L // CW)]
    T_hy = dict(I)
    T_hy.update(ada_w=I["hy_ada_w"], ada_b=I["hy_ada_b"], norm_g=I["hy_norm_g"], w=I["hy_w"],
                ygT=Chunked(yg_loc[:L // CW], CW), ygTc=yg_loc[L // CW])
    build_hy(0, nc=nc, T=T_hy, pre="h_")
    _all_gather(nc, yg_loc, yg_all, "cc1")
    T_at = dict(I)
    T_at.update(ada_w=I["at_ada_w"], ada_b=I["at_ada_b"], norm_g=I["at_norm_g"], w=I["at_w"], cvec0=I["cvec"],
                yg_all=Chunked(yg_all, CW), x1_full=x1_full, og_loc=Chunked(og_loc, CW))
    build_att(0, nc=nc, T=T_at, pre="a_")
    _all_gather(nc, og_loc, og_all, "cc2")
    T_op = dict(ygT=Chunked(og_all, CW), xres=x1_full, w_out=I["w_out1"], cvec=I["cvec"], ada_w_gt=I["ada_w_gt1"],
                ada_b_gt=I["ada_b_gt1"], xout=out)
    build_op(L // 128, 0, nc=nc, T=T_op, pre="o_")
    return nc


def fused_inputs(inp):
    f32 = np.float32
    x, c, ctx, c_ctx = inp["x"], inp["c"], inp["ctx"], inp["c_ctx"]
    ada_w, ada_b, norm_g = inp["ada_w"], inp["ada_b"], inp["norm_g"]
    mats = hy_consts()
    zext, text = hy_ztab(L)
    zextc, textc = hy_ztab(CTX)
    negt = np.ascontiguousarray((-text).reshape(128, 128).T)
    textc_b = np.ascontiguousarray(np.broadcast_to(textc[None, :], (128, 2 * CTX))).astype(f32)
    deltas = np.abs(np.linspace(MIN_DECAY, MAX_DECAY, E, dtype=f32))
    w_in, conv_w, conv_b = inp["hy_w_in"][0], inp["hy_conv_w"][0], inp["hy_conv_b"][0]
    cosT, sinT = rope_tables()
    consts = att_consts()
    QD, KVD = 2048, 512
    shared = {
        "identf": np.eye(64, dtype=f32), "hy_ada_w": np.ascontiguousarray(ada_w[0][:, :2048]),
        "hy_ada_b": np.ascontiguousarray(ada_b[0][:2048]), "hy_norm_g": norm_g[0],
        "fvec": np.concatenate([inp["hy_freq"][0], inp["hy_fb1"][0], inp["hy_fb2"][0], inp["hy_fb3"][0]]).astype(f32),
        "fw1": inp["hy_fw1"][0], "fw23": np.ascontiguousarray(np.stack([inp["hy_fw2"][0], inp["hy_fw3"][0]], axis=1)),
        "zext": zext, "zextc": zextc, "negt": negt, "textc": textc_b, "mats": mats,
        "at_ada_w": np.ascontiguousarray(ada_w[1][:, :2048]), "at_ada_b": np.ascontiguousarray(ada_b[1][:2048]),
        "at_norm_g": norm_g[1], "qkg": np.ascontiguousarray(np.concatenate([inp["at_q_g"][0], inp["at_k_g"][0]])),
        "cosT": cosT, "sinT": sinT, "consts": consts,
        "w_out0": inp["hy_w_out"][0], "ada_w_gt0": np.ascontiguousarray(ada_w[0][:, 2048:]),
        "ada_b_gt0": np.ascontiguousarray(ada_b[0][2048:]),
        "w_out1": inp["at_w_out"][0], "ada_w_gt1": np.ascontiguousarray(ada_w[1][:, 2048:]),
        "ada_b_gt1": np.ascontiguousarray(ada_b[1][2048:]),
    }
    in_maps = []
    for core in range(NCORES):
        b, g = divmod(core, 4)
        ch = np.arange(512 * g, 512 * g + 512)
        cols = np.concatenate([ty * E + ch for ty in range(4)])
        vecs = np.concatenate([conv_w[tap, ty * E + ch] for tap in range(3) for ty in range(3)] +
                              [conv_b[ty * E + ch] for ty in range(3)] + [inp["hy_d"][0][ch]]).astype(f32)
        fw4 = np.stack([inp["hy_fw4"][0][:, ch], inp["hy_fw4"][0][:, E + ch]], axis=1)
        dl = deltas[ch]
        acols = np.concatenate([np.arange(512 * g, 512 * g + 512), QD + np.arange(128 * g, 128 * g + 128),
                                QD + KVD + np.arange(128 * g, 128 * g + 128), QD + 2 * KVD + np.arange(512 * g, 512 * g + 512)])
        m = dict(shared)
        m.update({
            "x": np.ascontiguousarray(x[b]), "xc": np.ascontiguousarray(ctx[b]),
            "cvec": np.ascontiguousarray(np.stack([c[b], c_ctx])),
            "hy_w": np.ascontiguousarray(w_in[:, cols]), "vecs": vecs, "fw4": np.ascontiguousarray(fw4),
            "drow": np.ascontiguousarray(np.broadcast_to(dl[None, :], (128, 512))).astype(f32),
            "ndcol": np.ascontiguousarray(-dl.reshape(4, 128).T),
            "at_w": np.ascontiguousarray(inp["at_w_in"][0][:, acols]),
        })
        in_maps.append(m)
    return in_maps


def kernel(**inp):
    inp = {k_: np.asarray(v) for k_, v in inp.items()}
    nc = build_fused()
    in_maps = fused_inputs(inp)
    res = run_bass_kernel_spmd(nc, in_maps, core_ids=list(range(NCORES)))
    out = np.stack([res.results[b * 4]["out"] for b in range(B)], axis=0)
    return out.astype(np.float32)


def kernel_unfused(**inp):
    inp = {k_: np.asarray(v) for k_, v in inp.items()}
    x, c, ctx, c_ctx = inp["x"], inp["c"], inp["ctx"], inp["c_ctx"]
    ada_w, ada_b, norm_g = inp["ada_w"], inp["ada_b"], inp["norm_g"]
    ygT, ygTc = run_hy(x, ctx, c, c_ctx, ada_w[0], ada_b[0], norm_g[0], inp)
    x1, x1c = run_op(ygT, x, inp["hy_w_out"][0], c, c_ctx, np.ascontiguousarray(ada_w[0][:, 2048:]),
                     np.ascontiguousarray(ada_b[0][2048:]), True, ygTc, ctx)
    ogT = run_att(x1, x1c, c, c_ctx, ada_w[1], ada_b[1], norm_g[1], inp["at_w_in"][0], inp["at_q_g"][0], inp["at_k_g"][0])
    out, _ = run_op(ogT, x1, inp["at_w_out"][0], c, c_ctx, np.ascontiguousarray(ada_w[1][:, 2048:]),
                    np.ascontiguousarray(ada_b[1][2048:]), False)
    return out.astype(np.float32)
```
